# Optimizing a Trainium2 kernel written in Bass

```python
import math
import jax, jax.numpy as jnp
from jax import lax
import numpy as np

D_MODEL = 1024
BATCH = 4
SEQ = 4096
DEPTH = 1

D_MIX = D_MODEL
ATTN_WIDTH = D_MIX // 2
CONV_WIDTH = D_MIX - ATTN_WIDTH
N_ATTN_HEADS = 4
ATTN_HEAD_DIM = ATTN_WIDTH // (2 * N_ATTN_HEADS)
CONV_GROUPS = 8
CONV_WIDTH_K = 3
PLE_DIM = 256
Q_BLOCK = 128
EPS = 1e-6
N_IN_COLS = 4 * ATTN_WIDTH + 4 * CONV_WIDTH

kernel_name = "hybrid_diffattn_shortconv_parallel_block"


def rmsnorm(x, g, eps=EPS):
    xf = x.astype(jnp.float32)
    y = xf * lax.rsqrt(jnp.mean(xf * xf, axis=-1, keepdims=True) + eps)
    return (y * g.astype(jnp.float32)).astype(x.dtype)


def alibi_slopes(n_heads):
    h = jnp.arange(1, n_heads + 1, dtype=jnp.float32)
    return jnp.exp2(-8.0 * h / n_heads)


def diff_attention(q1, q2, k1, k2, v, lam):
    b, n_h, s, d = q1.shape
    dv = v.shape[-1]
    n_blk = s // Q_BLOCK
    scale = d ** -0.5
    slopes = alibi_slopes(n_h)
    kpos = jnp.arange(s, dtype=jnp.int32)
    q1b = q1.reshape(b, n_h, n_blk, Q_BLOCK, d).transpose(2, 0, 1, 3, 4)
    q2b = q2.reshape(b, n_h, n_blk, Q_BLOCK, d).transpose(2, 0, 1, 3, 4)
    starts = jnp.arange(n_blk, dtype=jnp.int32) * Q_BLOCK

    def one_block(args):
        q1_blk, q2_blk, start = args
        qpos = start + jnp.arange(Q_BLOCK, dtype=jnp.int32)
        dist = jnp.abs(qpos[:, None] - kpos[None, :]).astype(jnp.float32)
        bias = -slopes[:, None, None] * dist[None]
        s1 = jnp.einsum('bhqd,bhkd->bhqk', q1_blk, k1).astype(jnp.float32) * scale + bias
        s2 = jnp.einsum('bhqd,bhkd->bhqk', q2_blk, k2).astype(jnp.float32) * scale + bias
        probs = jax.nn.softmax(s1, axis=-1) - lam * jax.nn.softmax(s2, axis=-1)
        return jnp.einsum('bhqk,bhkv->bhqv', probs.astype(v.dtype), v)

    o = lax.map(one_block, (q1b, q2b, starts))
    return o.transpose(1, 0, 3, 2, 4).reshape(b, s, n_h, dv)


def depthwise_conv3(u, w):
    c = u.shape[-1]
    rhs = w.astype(u.dtype)[:, None, :]
    pad = (CONV_WIDTH_K - 1) // 2
    return lax.conv_general_dilated(
        u, rhs, window_strides=(1,), padding=((pad, pad),),
        dimension_numbers=('NWC', 'WIO', 'NWC'), feature_group_count=c)


def setup_inputs(seed: int = 0) -> dict:
    key = jax.random.key(seed)
    ks = jax.random.split(key, 16)
    f32 = jnp.float32
    d = ATTN_HEAD_DIM
    x = jax.random.normal(ks[0], (BATCH, SEQ, D_MODEL), f32)
    p = jax.random.normal(ks[1], (DEPTH, BATCH, SEQ, PLE_DIM), f32)
    mix_norm_g = 1.0 + 0.02 * jax.random.normal(ks[2], (DEPTH, D_MODEL), f32)
    w_in = jax.random.normal(ks[3], (DEPTH, D_MODEL, N_IN_COLS), f32) * D_MODEL ** -0.5
    lambda_q1 = 0.1 * jax.random.normal(ks[4], (DEPTH, d), f32)
    lambda_k1 = 0.1 * jax.random.normal(ks[5], (DEPTH, d), f32)
    lambda_q2 = 0.1 * jax.random.normal(ks[6], (DEPTH, d), f32)
    lambda_k2 = 0.1 * jax.random.normal(ks[7], (DEPTH, d), f32)
    subln_g = 1.0 + 0.02 * jax.random.normal(ks[8], (DEPTH, 2 * d), f32)
    conv_w = jax.random.normal(ks[9], (DEPTH, CONV_WIDTH_K, CONV_WIDTH), f32) * CONV_WIDTH_K ** -0.5
    w_out = jax.random.normal(ks[10], (DEPTH, D_MIX, D_MODEL), f32) * D_MIX ** -0.5
    ple_norm_g = 1.0 + 0.02 * jax.random.normal(ks[11], (DEPTH, D_MODEL), f32)
    w_ple_gate = jax.random.normal(ks[12], (DEPTH, D_MODEL, D_MODEL), f32) * D_MODEL ** -0.5
    w_ple_proj = jax.random.normal(ks[13], (DEPTH, PLE_DIM, D_MODEL), f32) * PLE_DIM ** -0.5
    final_norm_g = 1.0 + 0.02 * jax.random.normal(ks[14], (D_MODEL,), f32)
    return {"x": x, "p": p, "mix_norm_g": mix_norm_g, "w_in": w_in,
            "lambda_q1": lambda_q1, "lambda_k1": lambda_k1,
            "lambda_q2": lambda_q2, "lambda_k2": lambda_k2,
            "subln_g": subln_g, "conv_w": conv_w, "w_out": w_out,
            "ple_norm_g": ple_norm_g, "w_ple_gate": w_ple_gate,
            "w_ple_proj": w_ple_proj, "final_norm_g": final_norm_g}


def reference(x, p, mix_norm_g, w_in, lambda_q1, lambda_k1, lambda_q2, lambda_k2,
              subln_g, conv_w, w_out, ple_norm_g, w_ple_gate, w_ple_proj, final_norm_g):
    b, s, _ = x.shape
    n_h, d = N_ATTN_HEADS, ATTN_HEAD_DIM
    split_at = [ATTN_WIDTH, 2 * ATTN_WIDTH, 3 * ATTN_WIDTH, 4 * ATTN_WIDTH,
                4 * ATTN_WIDTH + CONV_WIDTH, 4 * ATTN_WIDTH + 2 * CONV_WIDTH,
                4 * ATTN_WIDTH + 3 * CONV_WIDTH]
    for i in range(DEPTH):
        lambda_init = 0.8 - 0.6 * math.exp(-0.3 * i)
        h = rmsnorm(x, mix_norm_g[i])
        z = h @ w_in[i]
        q, k, v, g_attn, cb, cc, ch, g_conv = jnp.split(z, split_at, axis=-1)

        q = q.reshape(b, s, n_h, 2, d).transpose(0, 2, 3, 1, 4)
        k = k.reshape(b, s, n_h, 2, d).transpose(0, 2, 3, 1, 4)
        vh = v.reshape(b, s, n_h, 2 * d).transpose(0, 2, 1, 3)
        lam = (jnp.exp(jnp.sum(lambda_q1[i].astype(jnp.float32) * lambda_k1[i].astype(jnp.float32)))
               - jnp.exp(jnp.sum(lambda_q2[i].astype(jnp.float32) * lambda_k2[i].astype(jnp.float32)))
               + lambda_init)
        o_attn = diff_attention(q[:, :, 0], q[:, :, 1], k[:, :, 0], k[:, :, 1], vh, lam)
        o_attn = rmsnorm(o_attn, subln_g[i]) * (1.0 - lambda_init)
        o_attn = o_attn.reshape(b, s, ATTN_WIDTH) * jax.nn.silu(g_attn)

        o_conv = cb * depthwise_conv3(cc * ch, conv_w[i]) * jax.nn.silu(g_conv)

        x = x + jnp.concatenate([o_attn, o_conv], axis=-1) @ w_out[i]

        gate = jax.nn.sigmoid(rmsnorm(x, ple_norm_g[i]) @ w_ple_gate[i])
        x = x + gate * (p[i] @ w_ple_proj[i])
    return rmsnorm(x, final_norm_g)
```

```python
import contextlib
import numpy as np
import concourse.bass as bass
import concourse.mybir as mybir
from concourse.bass_utils import run_bass_kernel_spmd

F32 = mybir.dt.float32
BF16 = mybir.dt.bfloat16
AF = mybir.ActivationFunctionType
ALU = mybir.AluOpType
AX = mybir.AxisListType

NCORES = 8
S = 4096
SO = 2048
D = 1024
NH = 4
EPS = 1e-6
SLOPES = [2.0 ** (-8.0 * (h + 1) / NH) for h in range(NH)]
LAMBDA_INIT = 0.8 - 0.6 * 1.0


class Eng:
    def __init__(self, nc, es, name, eng):
        self.nc, self.name, self.eng = nc, name, eng
        self.sem = es.enter_context(nc.semaphore("sem_" + name))
        self.cnt = 0
        self.waited = {}

    def mark(self, instr):
        self.cnt += 1
        instr.then_inc(self.sem, 1)
        return (self.sem, self.cnt, self.name)

    def last(self):
        return (self.sem, self.cnt, self.name) if self.cnt else None

    def need(self, *tks):
        for tk in tks:
            if tk is None:
                continue
            if isinstance(tk, list):
                self.need(*tk)
                continue
            sem, val, name = tk
            if self.waited.get(name, 0) >= val:
                continue
            self.eng.wait_ge(sem, val)
            self.waited[name] = val


class Slot:
    def __init__(self, nc, es, name):
        self.sem = es.enter_context(nc.semaphore("dq_" + name))
        self.cnt = 0
        self.name = "dq_" + name

    def dma(self, q, out, in_, **kw):
        ins = q.eng.dma_start(out=out, in_=in_, **kw)
        self.cnt += 16
        ins.then_inc(self.sem, 16)
        return (self.sem, self.cnt, self.name)


def build_program(debug=None):
    nc = bass.Bass("TRN2", target_bir_lowering=False)

    def din(name, shape):
        return nc.dram_tensor(name, shape, F32, kind="ExternalInput").ap()

    x_d = din("x", [S, D])
    p_d = din("p", [SO, 256])
    win_d = din("w_in", [D, 4096])
    wout_d = din("w_out", [D, D])
    wg_d = din("w_g", [D, D])
    wp_d = din("w_p", [256, D])
    gmix_d = din("gmix", [128, 8])
    gple_d = din("gple", [128, 8])
    gfin_d = din("gfin", [128, D])
    gpleb_d = din("gpleb", [128, D])
    subln_d = din("subln", [128, 1])
    convw_d = din("convw", [128, 12])
    lam_d = din("lamv", [128, 256])
    y_d = nc.dram_tensor("y", [SO, D], F32, kind="ExternalOutput").ap()
    dbg = {}
    if debug:
        for name, shape in debug.items():
            dbg[name] = nc.dram_tensor("dbg_" + name, shape, F32, kind="ExternalOutput").ap()

    es = contextlib.ExitStack()
    with es:
        PE = Eng(nc, es, "pe", nc.tensor)
        ACT = Eng(nc, es, "act", nc.scalar)
        DVE = Eng(nc, es, "dve", nc.vector)
        POOL = Eng(nc, es, "pool", nc.gpsimd)
        SP = Eng(nc, es, "sp", nc.sync)
        ENGS = [PE, ACT, DVE, POOL]

        def sb(name, shape, dt):
            return es.enter_context(nc.sbuf_tensor(name, shape, dt))

        def barrier(engs=None, extra=()):
            tks = [e.last() for e in ENGS] + list(extra)
            for e in (engs or (ENGS + [SP])):
                e.need(*tks)

        QTF = sb("QT", [128, NH * SO], BF16)
        KTF = sb("KT", [128, NH * S], BF16)
        VAF = sb("VA", [128, 32 * NH * 129], BF16)
        QT = QTF[:].rearrange("p (h s) -> p h s", h=NH)
        KT = KTF[:].rearrange("p (h s) -> p h s", h=NH)
        VA = VAF[:].rearrange("p (a b c) -> p a b c", a=32, b=NH)
        SGAF = sb("SGA", [128, NH * SO], BF16)
        SGA = SGAF[:].rearrange("p (h s) -> p h s", h=NH)
        MIXC = sb("MIXC", [128, 4, SO], BF16)
        ident_f = sb("ident_f", [128, 128], F32)
        ident_b = sb("ident_b", [128, 128], BF16)
        ones_b = sb("ones_b", [128, 128], BF16)
        gmix = sb("gmix_s", [128, 8], F32)
        gple = sb("gple_s", [128, 8], F32)
        subln = sb("subln_s", [128, 1], F32)
        sublns = sb("sublns", [128, 1], F32)
        convw = sb("convw_s", [128, 12], F32)
        lamv = sb("lamv_s", [128, 256], F32)
        lamt = sb("lamt", [128, 8], F32)
        neglam = sb("neglam", [128, 1], F32)
        mhalf = sb("mhalf", [128, 1], F32)
        ss_all = sb("ss_all", [128, 48], F32)
        v_all = sb("v_all", [128, 48], F32)
        rstd_all = sb("rstd_all", [128, 48], F32)
        mx_all = sb("mx_all", [128, 48], F32)
        negM = sb("negM", [128, 1], F32)
        uh = sb("uh", [128, 8], F32)
        uhalo = sb("uhalo", [128, 4], F32)

        T_unit = sb("T_unit", [128, 896], F32)
        iotaA = sb("iotaA", [128, 13], F32)
        iotaB = sb("iotaB", [128, 32], F32)
        iotaF = sb("iotaF", [128, 8], F32)
        biasA = sb("biasA", [128, NH, 13], F32)
        biasB = sb("biasB", [128, NH, 32], F32)
        fAB = sb("fAB", [128, NH, 8], F32)
        rs = [sb("rs%d" % i, [128, 8], F32) for i in range(2)]
        c2 = [sb("c2_%d" % i, [128, 4], F32) for i in range(2)]
        ssq = [sb("ssq%d" % i, [128, 4], F32) for i in range(2)]
        vsq = [sb("vsq%d" % i, [128, 4], F32) for i in range(2)]
        rsd = [sb("rsd%d" % i, [128, 4], F32) for i in range(2)]
        mq = sb("mq", [128, 2], F32)
        mk2 = sb("mk2", [128, 2], F32)
        arena_bytes = (nc.sbuf_bytes_remaining - 1024) // 64 * 64
        ARENA = sb("ARENA", [128, arena_bytes // 2], BF16)
        print("arena bytes", arena_bytes)

        class Carver:
            def __init__(self, off=0):
                self.off = off

            def take(self, shape, dt):
                esz = 2 if dt == BF16 else 4
                n = int(np.prod(shape[1:]))
                nbytes = n * esz
                self.off = (self.off + 63) // 64 * 64
                assert self.off + nbytes <= arena_bytes, (self.off, nbytes, arena_bytes)
                a = ARENA[:, self.off // 2:(self.off + nbytes) // 2]
                self.off += nbytes
                if dt != BF16:
                    a = a.bitcast(dt)
                if len(shape) == 3:
                    a = a.rearrange("p (a b) -> p a b", a=shape[1])
                elif len(shape) == 4:
                    a = a.rearrange("p (a b c) -> p a b c", a=shape[1], b=shape[2])
                return a

        cslot = Slot(nc, es, "const")
        tkc = None
        for dst, src in [(gmix, gmix_d), (gple, gple_d), (subln, subln_d), (convw, convw_d), (lamv, lam_d)]:
            tkc = cslot.dma(SP, dst[:], src)
        POOL.need(POOL.mark(POOL.eng.memset(ident_f[:], 0.0)))
        POOL.eng.affine_select(out=ident_f[:], in_=ident_f[:], pattern=[[1, 128]], base=0,
                               channel_multiplier=-1, compare_op=ALU.not_equal, fill=1.0)
        POOL.eng.memset(ones_b[:], 1.0)
        POOL.eng.memset(mhalf[:], -0.5)
        POOL.eng.memset(mx_all[:], 0.0)
        tk_pc = POOL.mark(POOL.eng.memset(VA[:, :, :, 128:129], 1.0))
        DVE.need(tk_pc)
        tk_idb = DVE.mark(DVE.eng.tensor_copy(out=ident_b[:], in_=ident_f[:]))
        lprod = sb("lprod", [128, 128], F32)
        LC = {}

        def late_consts():
            DVE.need(tkc)
            DVE.eng.tensor_tensor(out=lprod[:, 0:64], in0=lamv[:, 0:64], in1=lamv[:, 64:128], op=ALU.mult)
            t0 = DVE.mark(DVE.eng.tensor_tensor(out=lprod[:, 64:128], in0=lamv[:, 128:192], in1=lamv[:, 192:256], op=ALU.mult))
            DVE.need(t0)
            t1 = DVE.mark(DVE.eng.tensor_reduce(out=lamt[:, 0:2], in_=lprod[:].rearrange("p (a b) -> p a b", a=2),
                                                axis=AX.X, op=ALU.add))
            ACT.need(t1)
            t2 = ACT.mark(ACT.eng.activation(out=lamt[:, 2:4], in_=lamt[:, 0:2], func=AF.Exp))
            DVE.need(t2)
            t3 = DVE.mark(DVE.eng.tensor_tensor(out=lamt[:, 4:5], in0=lamt[:, 3:4], in1=lamt[:, 2:3], op=ALU.subtract))
            DVE.need(t3)
            DVE.mark(DVE.eng.tensor_scalar(out=neglam[:], in0=lamt[:, 4:5], scalar1=-LAMBDA_INIT, scalar2=None, op0=ALU.add))
            tk_const = DVE.mark(DVE.eng.tensor_scalar(out=sublns[:], in0=subln[:], scalar1=(1.0 - LAMBDA_INIT) * 0.5,
                                                      scalar2=None, op0=ALU.mult))

            POOL.eng.iota(T_unit[:], pattern=[[1, 896]], base=-384, channel_multiplier=-1, allow_small_or_imprecise_dtypes=True)
            POOL.eng.iota(iotaA[:], pattern=[[128, 13]], base=0, channel_multiplier=-1, allow_small_or_imprecise_dtypes=True)
            POOL.eng.iota(iotaB[:], pattern=[[128, 32]], base=-511, channel_multiplier=1, allow_small_or_imprecise_dtypes=True)
            POOL.eng.iota(iotaF[:, 0:4], pattern=[[128, 4]], base=0, channel_multiplier=1, allow_small_or_imprecise_dtypes=True)
            t_io = POOL.mark(POOL.eng.iota(iotaF[:, 4:8], pattern=[[-128, 4]], base=511, channel_multiplier=-1,
                                           allow_small_or_imprecise_dtypes=True))
            ACT.need(t_io)
            ACT.mark(ACT.eng.activation(out=T_unit[:], in_=T_unit[:], func=AF.Abs))
            for h in range(NH):
                ACT.mark(ACT.eng.activation(out=fAB[:, h, :], in_=iotaF[:], func=AF.Exp, scale=-SLOPES[h]))

            LC["tk_const"] = tk_const
            LC["t_io"] = t_io

        cv = Carver()
        HT = cv.take([128, 8, SO], BF16)
        WB = [cv.take([128, 8, 512], BF16) for _ in range(3)]
        SQB = [cv.take([128, 512], BF16) for _ in range(2)]
        alias0 = cv.off
        NXT = 5
        XT = [cv.take([128, D], F32) for _ in range(NXT)]
        XN = [cv.take([128, D], BF16) for _ in range(2)]
        JUNK = cv.take([128, D], BF16)
        norm_end = cv.off
        cv2 = Carver(alias0)
        UF = cv2.take([128, 2050], F32)
        HS = cv2.take([128, 512], F32)
        TH = [cv2.take([128, 512], F32) for _ in range(2)]
        CA = [cv2.take([128, 512], F32) for _ in range(2)]
        print("phase1 arena used", max(cv.off, cv2.off))
        HTH = sb("HTH", [128, 8, 2], BF16)
        halo_t = sb("halo_t", [128, 4], F32)

        pes = contextlib.ExitStack()
        with pes:
            psT = [pes.enter_context(nc.psum_tensor("psT%d" % i, [128, D], BF16)) for i in range(2)]
            psP = [pes.enter_context(nc.psum_tensor("psP%d" % i, [128, 512], F32)) for i in range(4)]
            psM = pes.enter_context(nc.psum_tensor("psM", [128, 512], F32))
            psP_free = [None] * 4
            psP_i = [-1]

            def psp_next():
                psP_i[0] = (psP_i[0] + 1) % 4
                return psP_i[0]

            xslot = [Slot(nc, es, "x%d" % i) for i in range(NXT)]
            wslot = [Slot(nc, es, "w%d" % i) for i in range(3)]
            w_free = [None, None, None]
            w_ready = [None, None, None]
            psT_free = [None, None]
            psM_free = [None]
            sqb_free = [None, None]
            mx_idx = [0]
            state = {"xt_free": [None] * NXT, "xn_free": [None, None], "nt": 0, "xt_gen": [0] * NXT,
                     "last_sq": None, "last_xn": None, "last_tr": None}
            deferred = []

            def flush_deferred():
                while deferred:
                    deferred.pop(0)()

            def load_w(buf, blocks):
                POOL.need(w_free[buf])
                tk = None
                for (c0, ncol, d0) in blocks:
                    src_ = win_d.rearrange("(c p) n -> p c n", p=128)[:, :, c0:c0 + ncol]
                    tk = wslot[buf].dma(POOL, WB[buf][:, :, d0:d0 + ncol], src_)
                w_ready[buf] = tk

            class NormPipe:
                def __init__(self, rows, tok0_of, ht_free_of):
                    self.rows, self.tok0_of, self.ht_free_of, self.n = rows, tok0_of, ht_free_of, len(rows)
                    self.base = state["nt"]
                    state["nt"] += self.n
                    self.step = 0
                    self.tk = [dict() for _ in range(7)]
                    self.ev = {}

                def _emit_step(self, step):
                    tk_ld, tk_sq, tk_v, tk_r, tk_xn, tk_tr, _ = self.tk
                    rows, base = self.rows, self.base
                    i = step
                    if 0 <= i < self.n:
                        k = base + i
                        s = k % NXT
                        assert state["xt_gen"][s] == k // NXT, (state["xt_gen"], k)
                        SP.need(state["xt_free"][s])
                        tk_ld[i] = xslot[s].dma(SP, XT[s][:], x_d[rows[i] * 128:(rows[i] + 1) * 128, :])
                    i = step - 1
                    if 0 <= i < self.n:
                        k = base + i
                        s = k % NXT
                        ACT.need(tk_ld[i], state["last_sq"])
                        tk_sq[i] = ACT.mark(ACT.eng.activation(out=JUNK[:], in_=XT[s][:], func=AF.Square,
                                                               accum_out=ss_all[:, k:k + 1]))
                        state["last_sq"] = tk_sq[i]
                        DVE.need(tk_sq[i])
                        tk_v[i] = DVE.mark(DVE.eng.tensor_scalar(out=v_all[:, k:k + 1], in0=ss_all[:, k:k + 1],
                                                                 scalar1=1.0 / D, scalar2=EPS, op0=ALU.mult, op1=ALU.add))
                        POOL.need(tk_v[i])
                        tk_r[i] = POOL.mark(POOL.eng.tensor_tensor(out=rstd_all[:, k:k + 1], in0=v_all[:, k:k + 1],
                                                                   in1=mhalf[:], op=ALU.pow))
                    i = step - 2
                    if 0 <= i < self.n:
                        k = base + i
                        s = k % NXT
                        s2 = k % 2
                        DVE.need(tk_r[i], tk_ld[i], state["xn_free"][s2])
                        tk_xn[i] = DVE.mark(DVE.eng.tensor_scalar(out=XN[s2][:], in0=XT[s][:], scalar1=rstd_all[:, k:k + 1],
                                                                  scalar2=None, op0=ALU.mult))
                        state["last_xn"] = tk_xn[i]
                        state["xt_free"][s] = [tk_xn[i], tk_sq[i]]
                        state["xt_gen"][s] += 1
                    i = step - 3
                    if 0 <= i < self.n:
                        k = base + i
                        s2 = k % 2
                        PE.need(tk_xn[i], psT_free[s2], tk_idb)
                        ins = None
                        for c in range(8):
                            ins = PE.eng.transpose(out=psT[s2][:, c * 128:(c + 1) * 128], in_=XN[s2][:, c * 128:(c + 1) * 128],
                                                   identity=ident_b[:])
                        tk_tr[i] = PE.mark(ins)
                        state["last_tr"] = tk_tr[i]
                        state["xn_free"][s2] = tk_tr[i]
                    i = step - 4
                    if 0 <= i < self.n:
                        k = base + i
                        s2 = k % 2
                        t0_ = self.tok0_of(i)
                        DVE.need(tk_tr[i], self.ht_free_of(i), tkc)
                        self.ev[i] = DVE.mark(DVE.eng.tensor_tensor(
                            out=HT[:, :, t0_:t0_ + 128], in0=psT[s2][:].rearrange("p (c t) -> p c t", c=8),
                            in1=gmix[:].unsqueeze(2).to_broadcast([128, 8, 128]), op=ALU.mult))
                        psT_free[s2] = self.ev[i]

                def advance(self, upto):
                    upto = min(upto, self.n - 1)
                    while self.step <= upto + 4:
                        self._emit_step(self.step)
                        self.step += 1
                    return self.ev[upto]

            def proj_fm(buf, wcol0, tokc, ready_tk):
                s = psp_next()
                PE.need(psP_free[s], w_ready[buf], ready_tk)
                ins = None
                for c in range(8):
                    ins = PE.eng.matmul(psP[s][:], lhsT=WB[buf][:, c, wcol0:wcol0 + 128],
                                        rhs=HT[:, c, tokc * 512:(tokc + 1) * 512], start=(c == 0), stop=(c == 7))
                tk = PE.mark(ins)
                flush_deferred()
                return s, tk

            def norm_bound(s, tk_mm):
                idx = mx_idx[0]
                mx_idx[0] += 1
                b = idx % 2
                ACT.need(tk_mm, sqb_free[b])
                t_sq = ACT.mark(ACT.eng.activation(out=SQB[b][:], in_=psP[s][:], func=AF.Square))

                def part_b():
                    PE.need(t_sq, psM_free[0])
                    t_m = PE.mark(PE.eng.matmul(psM[:], lhsT=ones_b[:], rhs=SQB[b][:], start=True, stop=True))
                    sqb_free[b] = t_m
                    DVE.need(t_m)
                    t_r = DVE.mark(DVE.eng.tensor_reduce(out=mx_all[:, idx:idx + 1], in_=psM[:], axis=AX.X, op=ALU.max))
                    psM_free[0] = t_r

                deferred.append(part_b)
                return t_sq

            def k_chunk(buf, h, tokc, tokbase, ready_tk):
                s, tk = proj_fm(buf, h * 128, tokc, ready_tk)
                ACT.need(tk)
                t_cp = ACT.mark(ACT.eng.activation(out=KT[:, h, tokbase + tokc * 512: tokbase + (tokc + 1) * 512], in_=psP[s][:],
                                                   func=AF.Copy))
                t_sq = norm_bound(s, tk)
                psP_free[s] = [t_cp, t_sq]

            def q_chunk(buf, h, tokc, ready_tk):
                s, tk = proj_fm(buf, h * 128, tokc, ready_tk)
                ACT.need(tk)
                t_cp = ACT.mark(ACT.eng.activation(out=QT[:, h, tokc * 512:(tokc + 1) * 512], in_=psP[s][:],
                                                   func=AF.Copy, scale=0.125))
                t_sq = norm_bound(s, tk)
                psP_free[s] = [t_cp, t_sq]

            def v_tile(buf, tl, rglob, ready_tk):
                s = psp_next()
                PE.need(psP_free[s], w_ready[buf], ready_tk)
                ins = None
                for c in range(8):
                    ins = PE.eng.matmul(psP[s][:], lhsT=HT[:, c, tl * 128:(tl + 1) * 128], rhs=WB[buf][:, c, :],
                                        start=(c == 0), stop=(c == 7))
                tk = PE.mark(ins)
                flush_deferred()
                DVE.need(tk, tk_pc)
                psP_free[s] = DVE.mark(DVE.eng.tensor_copy(out=VA[:, rglob, :, 0:128],
                                                           in_=psP[s][:].rearrange("p (h d) -> p h d", h=NH)))

            load_w(0, [(512, 512, 0)])
            load_w(1, [(1024, 512, 0)])
            ht_chunk_free = [None] * 4
            np_a = NormPipe(list(range(16, 32)), lambda i: i * 128, lambda i: None)
            np_b = NormPipe(list(range(0, 16)), lambda i: i * 128, lambda i: ht_chunk_free[i // 4])
            ev0 = np_a.advance(0)
            DVE.need(ev0)
            tk_hth = DVE.mark(DVE.eng.tensor_copy(out=HTH[:], in_=HT[:, :, 0:2]))
            LAG = 4
            for s in range(32 + LAG):
                if s < 16:
                    np_a.advance(s)
                elif s < 32:
                    np_b.advance(s - 16)
                if s == 9:
                    late_consts()
                if s == 8:
                    load_w(2, [(0, 512, 0)])
                m = s - LAG
                if 0 <= m < 16:
                    j, r = m // 4, m % 4
                    rdy = np_a.advance(4 * j + 3)
                    k_chunk(0, r, j, SO, rdy)
                    v_tile(1, 4 * j + r, 16 + 4 * j + r, rdy)
                    if r == 3:
                        ht_chunk_free[j] = PE.last()
                elif 16 <= m < 32:
                    j, r = (m - 16) // 4, (m - 16) % 4
                    rdy = np_b.advance(4 * j + 3)
                    q_chunk(2, r, j, rdy)
            flush_deferred()
            rdy_all = np_b.advance(15)
            norm_done = [state["last_sq"], state["last_xn"], state["last_tr"]]
            w_free[2] = PE.last()
            load_w(2, [(1536, 512, 0)])
            for tokc in range(4):
                for h in range(NH):
                    k_chunk(0, h, tokc, 0, rdy_all)
            flush_deferred()
            w_free[0] = PE.last()
            DVE.need(DVE.last())
            nq = 16
            nk = 32
            t_a = DVE.mark(DVE.eng.tensor_reduce(out=mq[:, 0:1], in_=mx_all[:, 16:32], axis=AX.X, op=ALU.max))
            DVE.need(t_a)
            DVE.eng.tensor_reduce(out=mk2[:, 0:1], in_=mx_all[:, 0:16], axis=AX.X, op=ALU.max)
            t_b = DVE.mark(DVE.eng.tensor_reduce(out=mk2[:, 1:2], in_=mx_all[:, 32:48], axis=AX.X, op=ALU.max))
            DVE.need(t_b)
            t_c = DVE.mark(DVE.eng.tensor_tensor(out=mq[:, 1:2], in0=mk2[:, 0:1], in1=mk2[:, 1:2], op=ALU.max))
            DVE.need(t_c)
            t_d = DVE.mark(DVE.eng.tensor_tensor(out=mk2[:, 0:1], in0=mq[:, 0:1], in1=mq[:, 1:2], op=ALU.add))
            DVE.need(t_d)
            tk_negM = DVE.mark(DVE.eng.tensor_scalar(out=negM[:], in0=mk2[:, 0:1], scalar1=-1.0 / 16.0, scalar2=None, op0=ALU.mult))

            DVE.need(tk_negM, LC["t_io"])
            for h in range(NH):
                DVE.eng.tensor_scalar(out=biasA[:, h, :], in0=iotaA[:], scalar1=-SLOPES[h], scalar2=negM[:], op0=ALU.mult, op1=ALU.add)
                DVE.mark(DVE.eng.tensor_scalar(out=biasB[:, h, :], in0=iotaB[:], scalar1=-SLOPES[h], scalar2=negM[:], op0=ALU.mult, op1=ALU.add))


            def conv_blocks(cc):
                return [(2048 + cc * 128, 128, 0), (2560 + cc * 128, 128, 128), (3072 + cc * 128, 128, 256),
                        (3584 + cc * 128, 128, 384)]

            load_w(0, conv_blocks(0))
            for tl in range(16):
                v_tile(1, tl, tl, rdy_all)
            w_free[1] = PE.last()
            load_w(1, conv_blocks(1))
            th_free = [norm_done, norm_done]
            thi = [0]
            for tokc in range(4):
                for h in range(NH):
                    s, tk = proj_fm(2, h * 128, tokc, rdy_all)
                    b = thi[0] % 2
                    thi[0] += 1
                    ACT.need(tk, th_free[b])
                    t_th = ACT.mark(ACT.eng.activation(out=TH[b][:], in_=psP[s][:], func=AF.Tanh, scale=0.5))
                    DVE.need(t_th)
                    t_sg = DVE.mark(DVE.eng.scalar_tensor_tensor(out=SGA[:, h, tokc * 512:(tokc + 1) * 512], in0=TH[b][:],
                                                                 scalar=1.0, in1=psP[s][:], op0=ALU.add, op1=ALU.mult))
                    th_free[b] = t_sg
                    psP_free[s] = t_sg
            w_free[2] = PE.last()
            load_w(2, conv_blocks(2))
            POOL.need(norm_done)
            tk_pad = POOL.mark(POOL.eng.memset(UF[:, 0:1], 0.0))
            uf_free = norm_done
            ca_free = [norm_done, norm_done]
            hs_free = norm_done
            conv_buf = [0, 1, 2, 0]
            for cc in range(4):
                buf = conv_buf[cc]
                PE.need(psM_free[0], w_ready[buf], tk_hth)
                ins = None
                for g_ in range(2):
                    for c in range(8):
                        ins = PE.eng.matmul(psM[:, g_ * 2:g_ * 2 + 2], lhsT=WB[buf][:, c, 128 + g_ * 128:256 + g_ * 128],
                                            rhs=HTH[:, c, :], start=(c == 0), stop=(c == 7))
                tk_h = PE.mark(ins)
                DVE.need(tk_h)
                t_ht = DVE.mark(DVE.eng.tensor_copy(out=halo_t[:], in_=psM[:, 0:4]))
                psM_free[0] = t_ht
                DVE.need(t_ht, uf_free)
                tk_hl = DVE.mark(DVE.eng.tensor_tensor(out=UF[:, 2049:2050], in0=halo_t[:, 0:1], in1=halo_t[:, 2:3], op=ALU.mult))
                tk_u = []
                for tokc in range(4):
                    sC, tkC = proj_fm(buf, 128, tokc, rdy_all)
                    sH, tkH = proj_fm(buf, 256, tokc, rdy_all)
                    ACT.need(tkH, hs_free)
                    t_hs = ACT.mark(ACT.eng.activation(out=HS[:], in_=psP[sH][:], func=AF.Copy))
                    psP_free[sH] = t_hs
                    DVE.need(t_hs, tkC, uf_free)
                    t_u = DVE.mark(DVE.eng.tensor_tensor(out=UF[:, 1 + tokc * 512: 1 + (tokc + 1) * 512], in0=psP[sC][:],
                                                         in1=HS[:], op=ALU.mult))
                    hs_free = t_u
                    psP_free[sC] = t_u
                    tk_u.append(t_u)
                last_readers = []
                for tokc in range(4):
                    sB, tkB = proj_fm(buf, 0, tokc, rdy_all)
                    sG, tkG = proj_fm(buf, 384, tokc, rdy_all)
                    b = thi[0] % 2
                    thi[0] += 1
                    ACT.need(tkG, th_free[b])
                    t_th = ACT.mark(ACT.eng.activation(out=TH[b][:], in_=psP[sG][:], func=AF.Tanh, scale=0.5))
                    DVE.need(t_th)
                    t_sg = DVE.mark(DVE.eng.scalar_tensor_tensor(out=TH[b][:], in0=TH[b][:], scalar=1.0, in1=psP[sG][:],
                                                                 op0=ALU.add, op1=ALU.mult))
                    psP_free[sG] = t_sg
                    DVE.need(t_sg, tkB)
                    t_sgb = DVE.mark(DVE.eng.scalar_tensor_tensor(out=TH[b][:], in0=TH[b][:], scalar=0.5, in1=psP[sB][:],
                                                                  op0=ALU.mult, op1=ALU.mult))
                    psP_free[sB] = t_sgb
                    a = thi[0] % 2
                    c0 = 1 + tokc * 512
                    DVE.need(tk_u[min(tokc + 1, 3)], tk_pad, tk_hl, ca_free[a], tkc)
                    t_a = DVE.mark(DVE.eng.tensor_scalar(out=CA[a][:], in0=UF[:, c0 - 1:c0 + 511],
                                                         scalar1=convw[:, cc * 3:cc * 3 + 1], scalar2=None, op0=ALU.mult))
                    DVE.need(t_a)
                    t_a = DVE.mark(DVE.eng.scalar_tensor_tensor(out=CA[a][:], in0=UF[:, c0:c0 + 512],
                                                                scalar=convw[:, cc * 3 + 1:cc * 3 + 2], in1=CA[a][:],
                                                                op0=ALU.mult, op1=ALU.add))
                    DVE.need(t_a)
                    t_a = DVE.mark(DVE.eng.scalar_tensor_tensor(out=CA[a][:], in0=UF[:, c0 + 1:c0 + 513],
                                                                scalar=convw[:, cc * 3 + 2:cc * 3 + 3], in1=CA[a][:],
                                                                op0=ALU.mult, op1=ALU.add))
                    DVE.need(t_a, t_sgb)
                    t_o = DVE.mark(DVE.eng.tensor_tensor(out=MIXC[:, cc, tokc * 512:(tokc + 1) * 512], in0=CA[a][:],
                                                         in1=TH[b][:], op=ALU.mult))
                    ca_free[a] = t_o
                    th_free[b] = t_o
                    last_readers = [t_o]
                uf_free = last_readers
                w_free[buf] = PE.last()
                if cc == 0:
                    load_w(0, conv_blocks(3))

            P1END = {"psP": list(psP_free), "psM": psM_free[0], "pe": PE.last(), "act": ACT.last(), "dve": DVE.last(),
                     "pool": POOL.last()}

        cv = Carver()
        MIXA = cv.take([128, NH, SO], BF16)
        ET = [cv.take([128, 1024], BF16) for _ in range(3)]
        DG = [cv.take([128, 1024], F32) for _ in range(2)]
        OACC = [cv.take([128, 8, 129], F32) for _ in range(2)]
        OT = [cv.take([128, 4, 128], F32) for _ in range(2)]
        ON = [cv.take([128, 4, 128], F32) for _ in range(2)]
        STG = [cv.take([128, 8, 129], F32) for _ in range(1)]
        WOUT = cv.take([128, 8, D], BF16)
        WG = cv.take([128, 8, D], BF16)
        WP = cv.take([128, 2, D], BF16)
        print("phase2 arena used", cv.off)

        class RawCarver:
            def __init__(self, flat, nbytes):
                self.flat, self.nbytes, self.off = flat, nbytes, 0

            def take(self, shape, dt):
                esz = 2 if dt == BF16 else 4
                n = int(np.prod(shape[1:])) * esz
                self.off = (self.off + 63) // 64 * 64
                assert self.off + n <= self.nbytes, (self.off, n, self.nbytes)
                a = self.flat[:, self.off // 2:(self.off + n) // 2]
                self.off += n
                if dt != BF16:
                    a = a.bitcast(dt)
                if len(shape) == 3:
                    a = a.rearrange("p (a b) -> p a b", a=shape[1])
                return a

        rc1 = RawCarver(KTF[:], NH * S * 2)
        rc2 = RawCarver(VAF[:], 32 * NH * 129 * 2)
        rc3 = RawCarver(QTF[:], NH * SO * 2)
        rc4 = RawCarver(SGAF[:], NH * SO * 2)
        PF32 = rc1.take([128, 16, 256], F32)
        XT3 = [rc1.take([128, D], F32) for _ in range(2)]
        X1 = [rc1.take([128, D], F32) for _ in range(2)]
        TH3 = [rc2.take([128, D], F32) for _ in range(2)]
        X2 = [rc2.take([128, D], F32) for _ in range(2)]
        YO = [rc2.take([128, D], F32) for _ in range(2)]
        PTALL = rc3.take([128, 2, SO], BF16)
        X1N = [rc3.take([128, D], BF16) for _ in range(2)]
        X1NT = [rc3.take([128, 8, 128], BF16) for _ in range(2)]
        GFIN = rc4.take([128, D], F32)
        GPLEB = rc4.take([128, D], F32)
        JUNK3 = rc4.take([128, D], BF16)
        ss3 = sb("ss3", [128, 16], F32)
        v3 = sb("v3", [128, 16], F32)
        r3 = sb("r3", [128, 16], F32)
        ss4 = sb("ss4", [128, 16], F32)
        v4 = sb("v4", [128, 16], F32)
        r4 = sb("r4", [128, 16], F32)

        NTILE = SO // 128
        x3slot = [Slot(nc, es, "x3_%d" % i) for i in range(2)]
        oslot = [Slot(nc, es, "o3_%d" % i) for i in range(2)]
        gslot = Slot(nc, es, "gfin")
        ppslot = [Slot(nc, es, "pld%d" % g) for g in range(4)]
        tk_pld = []
        P3 = {}

        def prefetch_phase3():
            SP.need(PE.last(), DVE.last())
            gslot.dma(SP, GFIN[:], gfin_d)
            P3["gfin"] = gslot.dma(SP, GPLEB[:], gpleb_d)
            for t_ in range(2):
                P3["ldx", t_] = x3slot[t_].dma(SP, XT3[t_][:], x_d[t_ * 128:(t_ + 1) * 128, :])
            for g in range(4):
                tk_pld.append(ppslot[g].dma(SP, PF32[:, 4 * g:4 * g + 4, :],
                                         p_d.rearrange("(t p) c -> p t c", p=128)[:, 4 * g:4 * g + 4, :]))


        wt_slot = Slot(nc, es, "wtail")
        POOL.need(P1END["pe"], P1END["act"], P1END["dve"])
        for half_ in range(2):
            wt_slot.dma(POOL, WOUT[:, :, half_ * 512:(half_ + 1) * 512],
                        wout_d.rearrange("(c p) n -> p c n", p=128)[:, :, half_ * 512:(half_ + 1) * 512])
        for half_ in range(2):
            wt_slot.dma(POOL, WG[:, :, half_ * 512:(half_ + 1) * 512],
                        wg_d.rearrange("(c p) n -> p c n", p=128)[:, :, half_ * 512:(half_ + 1) * 512])
        for half_ in range(2):
            tk_wtail = wt_slot.dma(POOL, WP[:, :, half_ * 512:(half_ + 1) * 512],
                                   wp_d.rearrange("(c p) n -> p c n", p=128)[:, :, half_ * 512:(half_ + 1) * 512])

        pes = contextlib.ExitStack()
        with pes:
            psS = [es.enter_context(nc.psum_tensor("psS%d" % i, [128, 1024], F32)) for i in range(2)]
            psBig = es.enter_context(nc.psum_tensor("psBig", [128, 2048], F32))
            psA = [psBig[:, b_ * 512:(b_ + 1) * 512] for b_ in range(3)]
            psO = psBig[:, 1536:2048]

            def acc_region(r):
                return psA[r // 3][:, (r % 3) * 129:(r % 3) * 129 + 129]

            tiles = []
            UORDER = [(1, 0), (0, 0), (1, 1), (0, 1), (1, 2), (0, 2), (1, 3), (0, 3)] + [(h_, q_) for h_ in (2, 3) for q_ in range(4)]
            for unit, (h, qc) in enumerate(UORDER):
                if True:
                    groups = []
                    groups.append(("B", list(range(4 * qc + 4, 32))))
                    if qc > 0:
                        groups.append(("A", list(range(0, 4 * qc))))
                    groups.append(("C", list(range(4 * qc, 4 * qc + 4))))
                    def dead(g, kt, h=h, qc=qc):
                        dmin = 128 * (4 * qc - kt) - 127 if g == "A" else 128 * (kt - 4 * qc) - 511
                        return g != "C" and SLOPES[h] * dmin >= 110.0
                    groups = [(g, [kt for kt in kts if not dead(g, kt)]) for g, kts in groups]
                    groups = [(g, kts) for g, kts in groups if kts]
                    for gi, (g, kts) in enumerate(groups):
                        for ki, kt in enumerate(kts):
                            tiles.append(dict(h=h, qc=qc, unit=unit, g=g, kt=kt, first=(ki == 0), last=(ki == len(kts) - 1),
                                              gfirst=(gi == 0), unit_last=(gi == len(groups) - 1 and ki == len(kts) - 1)))
            NT = len(tiles)
            psS_free = [None, [P1END["psP"][0], P1END["psP"][1]]]
            et_free = [None, None, None]
            dg_free = [None, None]
            dgi = [0]
            tk_E = {}
            acc_free = [None] * 8
            for r_ in range(8):
                acc_free[r_] = [P1END["psP"][2], P1END["psP"][3], P1END["psM"]][r_ // 3]
            oacc_free = [None, None]
            ot_free = [None, None]
            on_free = [None, None]
            pso_free = [None]
            pending = []
            stg_free = [None, None]
            stg_i = [0]
            unit_evac = {}

            def emit_S(i):
                t = tiles[i]
                h, qc, kt = t["h"], t["qc"], t["kt"]
                b = i % 2
                PE.need(psS_free[b])
                PE.eng.matmul(psS[b][:, 0:512], lhsT=KT[0:64, h, kt * 128:(kt + 1) * 128], rhs=QT[0:64, h, qc * 512:(qc + 1) * 512],
                              start=True, stop=True)
                tk_s = PE.mark(PE.eng.matmul(psS[b][:, 512:1024], lhsT=KT[64:128, h, kt * 128:(kt + 1) * 128],
                                             rhs=QT[64:128, h, qc * 512:(qc + 1) * 512], start=True, stop=True))
                e = i % 3
                if t["g"] == "C":
                    g_ = dgi[0] % 2
                    dgi[0] += 1
                    off = 384 - 128 * (kt - 4 * qc)
                    DVE.need(tk_s, dg_free[g_])
                    tk_d = DVE.mark(DVE.eng.scalar_tensor_tensor(
                        out=DG[g_][:].rearrange("p (j q) -> p j q", j=2),
                        in0=T_unit[:, off:off + 512].unsqueeze(1).to_broadcast([128, 2, 512]), scalar=-SLOPES[h],
                        in1=psS[b][:].rearrange("p (j q) -> p j q", j=2), op0=ALU.mult, op1=ALU.add))
                    psS_free[b] = tk_d
                    ACT.need(tk_d, et_free[e])
                    tk_E[i] = ACT.mark(ACT.eng.activation(out=ET[e][:], in_=DG[g_][:], func=AF.Exp, bias=negM[:], scale=1.0))
                    dg_free[g_] = tk_E[i]
                else:
                    if t["g"] == "A":
                        bias = biasA[:, h, 4 * qc - kt:4 * qc - kt + 1]
                    else:
                        bias = biasB[:, h, kt - 4 * qc:kt - 4 * qc + 1]
                    ACT.need(tk_s, et_free[e])
                    tk_E[i] = ACT.mark(ACT.eng.activation(out=ET[e][:], in_=psS[b][:], func=AF.Exp, bias=bias, scale=1.0))
                    psS_free[b] = tk_E[i]

            def finalize_part1(unit, h, qc, tks):
                u = unit % 2
                DVE.need(tks, ot_free[u])
                t_ = DVE.mark(DVE.eng.reciprocal(out=rs[u][:], in_=OACC[u][:, :, 128]))
                DVE.need(t_, LC["tk_const"])
                t_c2 = DVE.mark(DVE.eng.tensor_scalar(out=c2[u][:], in0=rs[u][:, 4:8], scalar1=neglam[:], scalar2=None, op0=ALU.mult))
                DVE.need(t_c2)
                for sb_ in range(4):
                    t_ = DVE.mark(DVE.eng.tensor_scalar(out=OT[u][:, sb_, :], in0=OACC[u][:, 4 + sb_, 0:128],
                                                        scalar1=c2[u][:, sb_:sb_ + 1], scalar2=None, op0=ALU.mult))
                DVE.need(t_)
                for sb_ in range(4):
                    t_ = DVE.mark(DVE.eng.scalar_tensor_tensor(out=OT[u][:, sb_, :], in0=OACC[u][:, sb_, 0:128],
                                                               scalar=rs[u][:, sb_:sb_ + 1], in1=OT[u][:, sb_, :],
                                                               op0=ALU.mult, op1=ALU.add))
                oacc_free[u] = t_
                DVE.need(t_)
                DVE.need(on_free[u])
                t_ = DVE.mark(DVE.eng.tensor_tensor(out=ON[u][:], in0=OT[u][:], in1=OT[u][:], op=ALU.mult))
                DVE.need(t_)
                t_ = DVE.mark(DVE.eng.tensor_reduce(out=ssq[u][:], in_=ON[u][:], axis=AX.X, op=ALU.add))
                DVE.need(t_)
                POOL.need(t_)
                t_v = POOL.mark(POOL.eng.tensor_scalar(out=vsq[u][:], in0=ssq[u][:], scalar1=1.0 / 128.0, scalar2=EPS,
                                                       op0=ALU.mult, op1=ALU.add))
                POOL.need(t_v)
                t_p = POOL.mark(POOL.eng.tensor_tensor(out=rsd[u][:], in0=vsq[u][:], in1=mhalf[:].to_broadcast([128, 4]), op=ALU.pow))
                POOL.need(t_p, on_free[u])
                t_on = POOL.mark(POOL.eng.tensor_tensor(out=ON[u][:], in0=OT[u][:],
                                                        in1=rsd[u][:].unsqueeze(2).to_broadcast([128, 4, 128]), op=ALU.mult))
                ot_free[u] = t_on

                def part2():
                    PE.need(t_on, pso_free[0])
                    ins = None
                    for sb_ in range(4):
                        ins = PE.eng.transpose(out=psO[:, sb_ * 128:(sb_ + 1) * 128], in_=ON[u][:, sb_, :], identity=ident_f[:])
                    t_tr = PE.mark(ins)
                    on_free[u] = t_tr
                    DVE.need(t_tr)
                    t_m = DVE.mark(DVE.eng.scalar_tensor_tensor(out=MIXA[:, h, qc * 512:(qc + 1) * 512], in0=psO[:], scalar=sublns[:],
                                                                in1=SGA[:, h, qc * 512:(qc + 1) * 512], op0=ALU.mult, op1=ALU.mult))
                    pso_free[0] = t_m

                pending.append([12, part2])

            def emit_PV(i):
                t = tiles[i]
                h, qc, kt, unit = t["h"], t["qc"], t["kt"], t["unit"]
                e = i % 3
                u = unit % 2
                PE.need(tk_E[i])
                ins = None
                tk_bank = []
                for r in range(8):
                    j, sb_ = r // 4, r % 4
                    if t["first"] and r % 3 == 0:
                        PE.need([acc_free[rr] for rr in range(r, min(r + 3, 8))])
                    ins = PE.eng.matmul(acc_region(r), lhsT=ET[e][:, j * 512 + sb_ * 128: j * 512 + (sb_ + 1) * 128],
                                        rhs=VA[:, kt, h, :], start=(t["first"] and r % 3 == 0), stop=t["last"],
                                        skip_group_check=True)
                    if t["last"] and r in (2, 5):
                        tk_bank.append(PE.mark(ins))
                tk_pv = PE.mark(ins)
                tk_bank.append(tk_pv)
                et_free[e] = tk_pv
                if t["last"]:
                    tks = [None] * 8
                    if t["g"] == "C":
                        DVE.need(unit_evac[unit])
                        for bk in range(3):
                            nreg = 3 if bk < 2 else 2
                            DVE.need(tk_bank[bk])
                            tk_ = DVE.mark(DVE.eng.tensor_tensor(
                                out=OACC[u][:, 3 * bk:3 * bk + nreg, :],
                                in0=psA[bk][:, 0:nreg * 129].rearrange("p (r d) -> p r d", r=nreg),
                                in1=OACC[u][:, 3 * bk:3 * bk + nreg, :], op=ALU.add))
                            for r in range(3 * bk, 3 * bk + nreg):
                                tks[r] = tk_
                    else:
                        fo = 0 if t["g"] == "A" else 4
                        sg_ = 0
                        stg_i[0] += 1
                        DVE.need(stg_free[sg_])
                        cps = []
                        for bk in range(3):
                            nreg = 3 if bk < 2 else 2
                            DVE.need(tk_bank[bk])
                            tk_ = DVE.mark(DVE.eng.tensor_copy(
                                out=STG[sg_][:, 3 * bk:3 * bk + nreg, :],
                                in_=psA[bk][:, 0:nreg * 129].rearrange("p (r d) -> p r d", r=nreg)))
                            cps.append(tk_)
                            for r in range(3 * bk, 3 * bk + nreg):
                                tks[r] = tk_
                        gfirst_ = t["gfirst"]
                        prev_ev = None if gfirst_ else unit_evac[unit]
                        ev_ = []
                        unit_evac[unit] = ev_
                        stg_free[sg_] = ev_

                        def scaled_acc(gfirst_=gfirst_, prev_ev=prev_ev, ev_=ev_, cps=cps, u=u, h=h, fo=fo, sg_=sg_):
                            if gfirst_:
                                DVE.need(cps, oacc_free[u])
                            else:
                                DVE.need(cps, prev_ev)
                            for r in range(8):
                                j, sb_ = r // 4, r % 4
                                if gfirst_:
                                    ev_.append(DVE.mark(DVE.eng.tensor_scalar(
                                        out=OACC[u][:, r, :], in0=STG[sg_][:, r, :], scalar1=fAB[:, h, fo + sb_:fo + sb_ + 1],
                                        scalar2=None, op0=ALU.mult)))
                                else:
                                    ev_.append(DVE.mark(DVE.eng.scalar_tensor_tensor(
                                        out=OACC[u][:, r, :], in0=STG[sg_][:, r, :], scalar=fAB[:, h, fo + sb_:fo + sb_ + 1],
                                        in1=OACC[u][:, r, :], op0=ALU.mult, op1=ALU.add)))

                        pending.append([3, scaled_acc])
                    for r in range(8):
                        acc_free[r] = tks[r]
                    if t["g"] == "C":
                        unit_evac[unit] = tks
                    if t["unit_last"]:
                        finalize_part1(unit, h, qc, unit_evac[unit])

            first_h3 = min(k for k, t_ in enumerate(tiles) if t_["h"] == NH - 1)
            for i in range(NT + 2):
                if i < NT:
                    emit_S(i)
                if i == first_h3 + 16:
                    prefetch_phase3()
                if i >= 2:
                    emit_PV(i - 2)
                for pnd in list(pending):
                    pnd[0] -= 1
                    if pnd[0] <= 0:
                        pending.remove(pnd)
                        pnd[1]()
            EY = {}
            tk_lastS = PE.last()
            PE.need(psS_free[0], tk_wtail)
            ins = None
            for half_ in range(2):
                for c in range(8):
                    lhs = MIXA[:, c, 0:128] if c < 4 else MIXC[:, c - 4, 0:128]
                    ins = PE.eng.matmul(psS[0][:, half_ * 512:(half_ + 1) * 512], lhsT=lhs, rhs=WOUT[:, c, half_ * 512:(half_ + 1) * 512],
                                        start=(c == 0), stop=(c == 7))
            EY["y0"] = PE.mark(ins)
            EY["ptall"] = []
            tk_prev = psS_free[1]
            ACT.need(tk_lastS)
            for g in range(4):
                PE.need(tk_pld[g], tk_prev)
                for c2_ in range(2):
                    for tt in range(4):
                        ins = PE.eng.transpose(out=psS[1][:, (c2_ * 4 + tt) * 128:(c2_ * 4 + tt + 1) * 128],
                                               in_=PF32[:, 4 * g + tt, c2_ * 128:(c2_ + 1) * 128], identity=ident_f[:])
                t_tr = PE.mark(ins)
                ACT.need(t_tr)
                tk_prev = ACT.mark(ACT.eng.activation(out=PTALL[:, :, 4 * g * 128:(4 * g + 4) * 128],
                                                      in_=psS[1][:].rearrange("p (c t) -> p c t", c=2), func=AF.Copy))
                EY["ptall"].append(tk_prev)
            for pnd in list(pending):
                pnd[1]()
            pending.clear()
            barrier()

        pes = contextlib.ExitStack()
        with pes:
            psY, psG = psS[0], psS[1]
            psPP = psBig[:, 0:1024]
            psT3 = psBig[:, 1024:1536].bitcast(BF16)
            T = {}
            fr = {"xt3": [None, None], "x1": [None, None], "x1n": [None, None], "x1nt": [None, None],
                  "th3": [None, None], "x2": [None, None], "yo": [None, None],
                  "psY": None, "psG": None, "psPP": None, "psT3": None}
            stores = []

            def st_load(t):
                s = t % 2
                SP.need(fr["xt3"][s])
                T["ldx", t] = x3slot[s].dma(SP, XT3[s][:], x_d[t * 128:(t + 1) * 128, :])

            def st_y(t):
                s = t % 2
                if t == 0:
                    T["y", t] = EY["y0"]
                else:
                    PE.need(fr["psY"], tk_wtail)
                    ins = None
                    for half_ in range(2):
                        for c in range(8):
                            lhs = MIXA[:, c, t * 128:(t + 1) * 128] if c < 4 else MIXC[:, c - 4, t * 128:(t + 1) * 128]
                            ins = PE.eng.matmul(psY[:, half_ * 512:(half_ + 1) * 512], lhsT=lhs, rhs=WOUT[:, c, half_ * 512:(half_ + 1) * 512],
                                                start=(c == 0), stop=(c == 7))
                    T["y", t] = PE.mark(ins)
                DVE.need(T["y", t], T["ldx", t], fr["x1"][s])
                T["x1", t] = DVE.mark(DVE.eng.tensor_tensor(out=X1[s][:], in0=psY[:], in1=XT3[s][:], op=ALU.add))
                fr["psY"] = T["x1", t]
                fr["xt3"][s] = T["x1", t]

            def st_sq3(t):
                s = t % 2
                ACT.need(T["x1", t], fr.get("junk"))
                T["sq3", t] = fr["junk"] = ACT.mark(ACT.eng.activation(out=JUNK3[:], in_=X1[s][:], func=AF.Square, accum_out=ss3[:, t:t + 1]))
                DVE.need(T["sq3", t])
                T["v3", t] = DVE.mark(DVE.eng.tensor_scalar(out=v3[:, t:t + 1], in0=ss3[:, t:t + 1], scalar1=1.0 / D, scalar2=EPS,
                                                            op0=ALU.mult, op1=ALU.add))
                POOL.need(T["v3", t])
                T["r3", t] = POOL.mark(POOL.eng.tensor_tensor(out=r3[:, t:t + 1], in0=v3[:, t:t + 1], in1=mhalf[:], op=ALU.pow))

            def st_x1n(t):
                s = t % 2
                DVE.need(T["r3", t], fr["x1n"][s], P3["gfin"])
                T["x1n", t] = DVE.mark(DVE.eng.scalar_tensor_tensor(out=X1N[s][:], in0=X1[s][:], scalar=r3[:, t:t + 1], in1=GPLEB[:],
                                                                    op0=ALU.mult, op1=ALU.mult))

            def st_c1(t):
                s = t % 2
                PE.need(T["x1n", t], fr["psT3"])
                ins = None
                for c in range(8):
                    ins = PE.eng.transpose(out=psT3[:, c * 128:(c + 1) * 128], in_=X1N[s][:, c * 128:(c + 1) * 128], identity=ident_b[:])
                T["tr3", t] = PE.mark(ins)
                fr["x1n"][s] = T["tr3", t]
                ACT.need(T["tr3", t], fr["x1nt"][s])
                T["x1nt_a", t] = ACT.mark(ACT.eng.activation(out=X1NT[s][:, 0:4, :].rearrange("p c t -> p (c t)"),
                                                             in_=psT3[:, 0:512], func=AF.Copy))
                T["x1nt", t] = ACT.mark(ACT.eng.activation(out=X1NT[s][:, 4:8, :].rearrange("p c t -> p (c t)"),
                                                           in_=psT3[:, 512:1024], func=AF.Copy))
                fr["psT3"] = T["x1nt", t]
                PE.need(T["ptall"], fr["psPP"])
                for half_ in range(2):
                    for c2_ in range(2):
                        ins = PE.eng.matmul(psPP[:, half_ * 512:(half_ + 1) * 512], lhsT=PTALL[:, c2_, t * 128:(t + 1) * 128],
                                            rhs=WP[:, c2_, half_ * 512:(half_ + 1) * 512], start=(c2_ == 0), stop=(c2_ == 1))
                T["pp", t] = PE.mark(ins)
                PE.need(T["x1nt_a", t], fr["psG"])
                for cg in range(2):
                    if cg == 1:
                        PE.need(T["x1nt", t])
                    for half_ in range(2):
                        for c in range(4 * cg, 4 * cg + 4):
                            ins = PE.eng.matmul(psG[:, half_ * 512:(half_ + 1) * 512], lhsT=X1NT[s][:, c, :],
                                                rhs=WG[:, c, half_ * 512:(half_ + 1) * 512], start=(c == 0), stop=(c == 7))
                T["g", t] = PE.mark(ins)
                fr["x1nt"][s] = T["g", t]

            def st_c2a(t):
                s = t % 2
                ACT.need(T["g", t], fr["th3"][s])
                T["th", t] = ACT.mark(ACT.eng.activation(out=TH3[s][:], in_=psG[:], func=AF.Tanh, scale=0.5))
                fr["psG"] = T["th", t]
                DVE.need(T["th", t], T["pp", t])
                T["gp", t] = DVE.mark(DVE.eng.scalar_tensor_tensor(out=TH3[s][:], in0=TH3[s][:], scalar=1.0, in1=psPP[:],
                                                                   op0=ALU.add, op1=ALU.mult))
                fr["psPP"] = T["gp", t]
                DVE.need(T["gp", t], fr["x2"][s])
                T["x2", t] = DVE.mark(DVE.eng.scalar_tensor_tensor(out=X2[s][:], in0=TH3[s][:], scalar=0.5, in1=X1[s][:],
                                                                   op0=ALU.mult, op1=ALU.add))
                fr["th3"][s] = T["x2", t]
                fr["x1"][s] = T["x2", t]

            def st_c2b(t):
                s = t % 2
                ACT.need(T["x2", t], fr.get("junk"))
                T["sq4", t] = fr["junk"] = ACT.mark(ACT.eng.activation(out=JUNK3[:], in_=X2[s][:], func=AF.Square, accum_out=ss4[:, t:t + 1]))
                DVE.need(T["sq4", t])
                T["v4", t] = DVE.mark(DVE.eng.tensor_scalar(out=v4[:, t:t + 1], in0=ss4[:, t:t + 1], scalar1=1.0 / D, scalar2=EPS,
                                                            op0=ALU.mult, op1=ALU.add))
                POOL.need(T["v4", t])
                T["r4", t] = POOL.mark(POOL.eng.tensor_tensor(out=r4[:, t:t + 1], in0=v4[:, t:t + 1], in1=mhalf[:], op=ALU.pow))

            def st_d(t):
                s = t % 2
                DVE.need(T["r4", t], fr["yo"][s], P3["gfin"])
                T["yo", t] = DVE.mark(DVE.eng.scalar_tensor_tensor(out=YO[s][:], in0=X2[s][:], scalar=r4[:, t:t + 1], in1=GFIN[:],
                                                                   op0=ALU.mult, op1=ALU.mult))
                fr["x2"][s] = T["yo", t]
                SP.need(T["yo", t])
                tk = oslot[s].dma(SP, y_d[t * 128:(t + 1) * 128, :], YO[s][:])
                fr["yo"][s] = tk
                stores.append(tk)

            T["ldx", 0], T["ldx", 1] = P3["ldx", 0], P3["ldx", 1]
            T["ptall"] = EY["ptall"]
            fr["psG"] = EY["ptall"][-1]
            for it in range(NTILE + 4):
                if 0 <= it - 1 < NTILE:
                    st_x1n(it - 1)
                if 0 <= it - 2 < NTILE:
                    st_c2a(it - 2)
                if it < NTILE:
                    if 2 <= it + 1 < NTILE:
                        st_load(it + 1)
                    st_y(it)
                if 0 <= it - 1 < NTILE:
                    st_c1(it - 1)
                if 0 <= it - 2 < NTILE:
                    st_c2b(it - 2)
                if it < NTILE:
                    st_sq3(it)
                if 0 <= it - 3 < NTILE:
                    st_d(it - 3)
            SP.need(*stores)
            DVE.need(*stores)
            barrier()

        ost = Slot(nc, es, "out")
        tk_out = []
        if debug:
            dslot = Slot(nc, es, "dbgs")
            barrier()
            stage = Carver(40960).take([128, 4096], F32)
            tkd = None

            def dump(name, src_ap, n):
                nonlocal tkd
                DVE.need(tkd)
                t = DVE.mark(DVE.eng.tensor_copy(out=stage[:, 0:n], in_=src_ap))
                SP.need(t)
                tkd = dslot.dma(SP, dbg[name], stage[:, 0:n])

            if "qt" in dbg:
                dump("qt", QT[:, 0, :], SO)
            if "kt" in dbg:
                dump("kt", KT[:, 1, :], S)
            if "va" in dbg:
                dump("va", VA[:, 17, :, :].rearrange("p h d -> p (h d)"), 516)
            if "sga" in dbg:
                dump("sga", SGA[:, 2, :], SO)
            if "mixc" in dbg:
                dump("mixc", MIXC[:, 3, :], SO)
            if "mixc0" in dbg:
                dump("mixc0", MIXC[:, 0, :], SO)
            if "negm" in dbg:
                dump("negm", negM[:], 1)
            if "ht" in dbg:
                dump("ht", HT[:, 5, :], SO)
            for hh in range(NH):
                if "mixa%d" % hh in dbg:
                    dump("mixa%d" % hh, MIXA[:, hh, :], SO)
            SP.need(tkd)
            DVE.need(tkd)
    return nc


def _core_inputs(inputs, c):
    b, half = c // 2, c % 2
    x = np.asarray(inputs["x"][b], dtype=np.float32)
    p = np.asarray(inputs["p"][0, b], dtype=np.float32)
    cw = np.asarray(inputs["conv_w"][0], dtype=np.float32)
    if half == 1:
        x = x[::-1]
        p = p[::-1]
        cw = cw[::-1]
    p = p[:SO]
    lamv = np.concatenate([np.asarray(inputs[k][0], dtype=np.float32) for k in
                           ("lambda_q1", "lambda_k1", "lambda_q2", "lambda_k2")])
    return {
        "x": np.ascontiguousarray(x),
        "p": np.ascontiguousarray(p),
        "w_in": np.ascontiguousarray(inputs["w_in"][0], dtype=np.float32),
        "w_out": np.ascontiguousarray(inputs["w_out"][0], dtype=np.float32),
        "w_g": np.ascontiguousarray(inputs["w_ple_gate"][0], dtype=np.float32),
        "w_p": np.ascontiguousarray(inputs["w_ple_proj"][0], dtype=np.float32),
        "gmix": np.ascontiguousarray(np.asarray(inputs["mix_norm_g"][0], dtype=np.float32).reshape(8, 128).T),
        "gple": np.ascontiguousarray(np.asarray(inputs["ple_norm_g"][0], dtype=np.float32).reshape(8, 128).T),
        "gfin": np.ascontiguousarray(np.broadcast_to(np.asarray(inputs["final_norm_g"], dtype=np.float32)[None, :], (128, D))),
        "gpleb": np.ascontiguousarray(np.broadcast_to(np.asarray(inputs["ple_norm_g"][0], dtype=np.float32)[None, :], (128, D))),
        "subln": np.ascontiguousarray(np.asarray(inputs["subln_g"][0], dtype=np.float32).reshape(128, 1)),
        "convw": np.ascontiguousarray(cw.reshape(3, 4, 128).transpose(2, 1, 0).reshape(128, 12)),
        "lamv": np.ascontiguousarray(np.broadcast_to(lamv[None, :], (128, 256))),
    }


def kernel(**inputs):
    nc = build_program()
    in_maps = [_core_inputs(inputs, c) for c in range(NCORES)]
    res = run_bass_kernel_spmd(nc, in_maps, core_ids=list(range(NCORES)))
    out = np.empty((4, S, D), dtype=np.float32)
    for c in range(NCORES):
        b, half = c // 2, c % 2
        yc = res.results[c]["y"]
        if half == 0:
            out[b, :SO] = yc
        else:
            out[b, SO:] = yc[::-1]
    return out
```

```python
import contextlib
import numpy as np
import concourse.bass as bass
import concourse.mybir as mybir
from concourse.bass_utils import run_bass_kernel_spmd

F32 = mybir.dt.float32
BF16 = mybir.dt.bfloat16
AF = mybir.ActivationFunctionType
ALU = mybir.AluOpType
AX = mybir.AxisListType

NCORES = 8
S = 4096
SO = 2048
D = 1024
NH = 4
EPS = 1e-6
SLOPES = [2.0 ** (-8.0 * (h + 1) / NH) for h in range(NH)]
LAMBDA_INIT = 0.8 - 0.6 * 1.0


class Eng:
    def __init__(self, nc, es, name, eng):
        self.nc, self.name, self.eng = nc, name, eng
        self.sem = es.enter_context(nc.semaphore("sem_" + name))
        self.cnt = 0
        self.waited = {}

    def mark(self, instr):
        self.cnt += 1
        instr.then_inc(self.sem, 1)
        return (self.sem, self.cnt, self.name)

    def last(self):
        return (self.sem, self.cnt, self.name) if self.cnt else None

    def need(self, *tks):
        for tk in tks:
            if tk is None:
                continue
            if isinstance(tk, list):
                self.need(*tk)
                continue
            sem, val, name = tk
            if self.waited.get(name, 0) >= val:
                continue
            self.eng.wait_ge(sem, val)
            self.waited[name] = val


class Slot:
    def __init__(self, nc, es, name):
        self.sem = es.enter_context(nc.semaphore("dq_" + name))
        self.cnt = 0
        self.name = "dq_" + name

    def dma(self, q, out, in_, **kw):
        ins = q.eng.dma_start(out=out, in_=in_, **kw)
        self.cnt += 16
        ins.then_inc(self.sem, 16)
        return (self.sem, self.cnt, self.name)


def build_program(debug=None):
    nc = bass.Bass("TRN2", target_bir_lowering=False)

    def din(name, shape):
        return nc.dram_tensor(name, shape, F32, kind="ExternalInput").ap()

    x_d = din("x", [S, D])
    p_d = din("p", [SO, 256])
    win_d = din("w_in", [D, 4096])
    wout_d = din("w_out", [D, D])
    wg_d = din("w_g", [D, D])
    wp_d = din("w_p", [256, D])
    gmix_d = din("gmix", [128, 8])
    gple_d = din("gple", [128, 8])
    gfin_d = din("gfin", [128, D])
    gpleb_d = din("gpleb", [128, D])
    subln_d = din("subln", [128, 1])
    convw_d = din("convw", [128, 12])
    lam_d = din("lamv", [128, 256])
    y_d = nc.dram_tensor("y", [SO, D], F32, kind="ExternalOutput").ap()
    dbg = {}
    if debug:
        for name, shape in debug.items():
            dbg[name] = nc.dram_tensor("dbg_" + name, shape, F32, kind="ExternalOutput").ap()

    es = contextlib.ExitStack()
    with es:
        PE = Eng(nc, es, "pe", nc.tensor)
        ACT = Eng(nc, es, "act", nc.scalar)
        DVE = Eng(nc, es, "dve", nc.vector)
        POOL = Eng(nc, es, "pool", nc.gpsimd)
        SP = Eng(nc, es, "sp", nc.sync)
        ENGS = [PE, ACT, DVE, POOL]

        def sb(name, shape, dt):
            return es.enter_context(nc.sbuf_tensor(name, shape, dt))

        def barrier(engs=None, extra=()):
            tks = [e.last() for e in ENGS] + list(extra)
            for e in (engs or (ENGS + [SP])):
                e.need(*tks)

        QTF = sb("QT", [128, NH * SO], BF16)
        KTF = sb("KT", [128, NH * S], BF16)
        VAF = sb("VA", [128, 32 * NH * 129], BF16)
        QT = QTF[:].rearrange("p (h s) -> p h s", h=NH)
        KT = KTF[:].rearrange("p (h s) -> p h s", h=NH)
        VA = VAF[:].rearrange("p (a b c) -> p a b c", a=32, b=NH)
        SGAF = sb("SGA", [128, NH * SO], BF16)
        SGA = SGAF[:].rearrange("p (h s) -> p h s", h=NH)
        MIXC = sb("MIXC", [128, 4, SO], BF16)
        ident_f = sb("ident_f", [128, 128], F32)
        ident_b = sb("ident_b", [128, 128], BF16)
        ones_b = sb("ones_b", [128, 128], BF16)
        gmix = sb("gmix_s", [128, 8], F32)
        gple = sb("gple_s", [128, 8], F32)
        subln = sb("subln_s", [128, 1], F32)
        sublns = sb("sublns", [128, 1], F32)
        convw = sb("convw_s", [128, 12], F32)
        lamv = sb("lamv_s", [128, 256], F32)
        lamt = sb("lamt", [128, 8], F32)
        neglam = sb("neglam", [128, 1], F32)
        mhalf = sb("mhalf", [128, 1], F32)
        ss_all = sb("ss_all", [128, 48], F32)
        v_all = sb("v_all", [128, 48], F32)
        rstd_all = sb("rstd_all", [128, 48], F32)
        mx_all = sb("mx_all", [128, 48], F32)
        negM = sb("negM", [128, 1], F32)
        uh = sb("uh", [128, 8], F32)
        uhalo = sb("uhalo", [128, 4], F32)

        T_unit = sb("T_unit", [128, 896], F32)
        iotaA = sb("iotaA", [128, 13], F32)
        iotaB = sb("iotaB", [128, 32], F32)
        iotaF = sb("iotaF", [128, 8], F32)
        biasA = sb("biasA", [128, NH, 13], F32)
        biasB = sb("biasB", [128, NH, 32], F32)
        fAB = sb("fAB", [128, NH, 8], F32)
        rs = [sb("rs%d" % i, [128, 8], F32) for i in range(2)]
        c2 = [sb("c2_%d" % i, [128, 4], F32) for i in range(2)]
        ssq = [sb("ssq%d" % i, [128, 4], F32) for i in range(2)]
        vsq = [sb("vsq%d" % i, [128, 4], F32) for i in range(2)]
        rsd = [sb("rsd%d" % i, [128, 4], F32) for i in range(2)]
        mq = sb("mq", [128, 2], F32)
        mk2 = sb("mk2", [128, 2], F32)
        arena_bytes = (nc.sbuf_bytes_remaining - 1024) // 64 * 64
        ARENA = sb("ARENA", [128, arena_bytes // 2], BF16)
        print("arena bytes", arena_bytes)

        class Carver:
            def __init__(self, off=0):
                self.off = off

            def take(self, shape, dt):
                esz = 2 if dt == BF16 else 4
                n = int(np.prod(shape[1:]))
                nbytes = n * esz
                self.off = (self.off + 63) // 64 * 64
                assert self.off + nbytes <= arena_bytes, (self.off, nbytes, arena_bytes)
                a = ARENA[:, self.off // 2:(self.off + nbytes) // 2]
                self.off += nbytes
                if dt != BF16:
                    a = a.bitcast(dt)
                if len(shape) == 3:
                    a = a.rearrange("p (a b) -> p a b", a=shape[1])
                elif len(shape) == 4:
                    a = a.rearrange("p (a b c) -> p a b c", a=shape[1], b=shape[2])
                return a

        cslot = Slot(nc, es, "const")
        tkc = None
        for dst, src in [(gmix, gmix_d), (gple, gple_d), (subln, subln_d), (convw, convw_d), (lamv, lam_d)]:
            tkc = cslot.dma(SP, dst[:], src)
        POOL.need(POOL.mark(POOL.eng.memset(ident_f[:], 0.0)))
        POOL.eng.affine_select(out=ident_f[:], in_=ident_f[:], pattern=[[1, 128]], base=0,
                               channel_multiplier=-1, compare_op=ALU.not_equal, fill=1.0)
        POOL.eng.memset(ones_b[:], 1.0)
        POOL.eng.memset(mhalf[:], -0.5)
        POOL.eng.memset(mx_all[:], 0.0)
        tk_pc = POOL.mark(POOL.eng.memset(VA[:, :, :, 128:129], 1.0))
        DVE.need(tk_pc)
        tk_idb = DVE.mark(DVE.eng.tensor_copy(out=ident_b[:], in_=ident_f[:]))
        lprod = sb("lprod", [128, 128], F32)
        LC = {}

        def late_consts():
            DVE.need(tkc)
            DVE.eng.tensor_tensor(out=lprod[:, 0:64], in0=lamv[:, 0:64], in1=lamv[:, 64:128], op=ALU.mult)
            t0 = DVE.mark(DVE.eng.tensor_tensor(out=lprod[:, 64:128], in0=lamv[:, 128:192], in1=lamv[:, 192:256], op=ALU.mult))
            DVE.need(t0)
            t1 = DVE.mark(DVE.eng.tensor_reduce(out=lamt[:, 0:2], in_=lprod[:].rearrange("p (a b) -> p a b", a=2),
                                                axis=AX.X, op=ALU.add))
            ACT.need(t1)
            t2 = ACT.mark(ACT.eng.activation(out=lamt[:, 2:4], in_=lamt[:, 0:2], func=AF.Exp))
            DVE.need(t2)
            t3 = DVE.mark(DVE.eng.tensor_tensor(out=lamt[:, 4:5], in0=lamt[:, 3:4], in1=lamt[:, 2:3], op=ALU.subtract))
            DVE.need(t3)
            DVE.mark(DVE.eng.tensor_scalar(out=neglam[:], in0=lamt[:, 4:5], scalar1=-LAMBDA_INIT, scalar2=None, op0=ALU.add))
            tk_const = DVE.mark(DVE.eng.tensor_scalar(out=sublns[:], in0=subln[:], scalar1=(1.0 - LAMBDA_INIT) * 0.5,
                                                      scalar2=None, op0=ALU.mult))

            POOL.eng.iota(T_unit[:], pattern=[[1, 896]], base=-384, channel_multiplier=-1, allow_small_or_imprecise_dtypes=True)
            POOL.eng.iota(iotaA[:], pattern=[[128, 13]], base=0, channel_multiplier=-1, allow_small_or_imprecise_dtypes=True)
            POOL.eng.iota(iotaB[:], pattern=[[128, 32]], base=-511, channel_multiplier=1, allow_small_or_imprecise_dtypes=True)
            POOL.eng.iota(iotaF[:, 0:4], pattern=[[128, 4]], base=0, channel_multiplier=1, allow_small_or_imprecise_dtypes=True)
            t_io = POOL.mark(POOL.eng.iota(iotaF[:, 4:8], pattern=[[-128, 4]], base=511, channel_multiplier=-1,
                                           allow_small_or_imprecise_dtypes=True))
            ACT.need(t_io)
            ACT.mark(ACT.eng.activation(out=T_unit[:], in_=T_unit[:], func=AF.Abs))
            for h in range(NH):
                ACT.mark(ACT.eng.activation(out=fAB[:, h, :], in_=iotaF[:], func=AF.Exp, scale=-SLOPES[h]))

            LC["tk_const"] = tk_const
            LC["t_io"] = t_io

        cv = Carver()
        HT = cv.take([128, 8, SO], BF16)
        WB = [cv.take([128, 8, 512], BF16) for _ in range(3)]
        SQB = [cv.take([128, 512], BF16) for _ in range(2)]
        alias0 = cv.off
        NXT = 5
        XT = [cv.take([128, D], F32) for _ in range(NXT)]
        XN = [cv.take([128, D], BF16) for _ in range(2)]
        JUNK = cv.take([128, D], BF16)
        norm_end = cv.off
        cv2 = Carver(alias0)
        UF = cv2.take([128, 2050], F32)
        HS = cv2.take([128, 512], F32)
        TH = [cv2.take([128, 512], F32) for _ in range(2)]
        CA = [cv2.take([128, 512], F32) for _ in range(2)]
        print("phase1 arena used", max(cv.off, cv2.off))
        HTH = sb("HTH", [128, 8, 2], BF16)
        halo_t = sb("halo_t", [128, 4], F32)

        pes = contextlib.ExitStack()
        with pes:
            psT = [pes.enter_context(nc.psum_tensor("psT%d" % i, [128, D], BF16)) for i in range(2)]
            psP = [pes.enter_context(nc.psum_tensor("psP%d" % i, [128, 512], F32)) for i in range(4)]
            psM = pes.enter_context(nc.psum_tensor("psM", [128, 512], F32))
            psP_free = [None] * 4
            psP_i = [-1]

            def psp_next():
                psP_i[0] = (psP_i[0] + 1) % 4
                return psP_i[0]

            xslot = [Slot(nc, es, "x%d" % i) for i in range(NXT)]
            wslot = [Slot(nc, es, "w%d" % i) for i in range(3)]
            w_free = [None, None, None]
            w_ready = [None, None, None]
            psT_free = [None, None]
            psM_free = [None]
            sqb_free = [None, None]
            mx_idx = [0]
            state = {"xt_free": [None] * NXT, "xn_free": [None, None], "nt": 0, "xt_gen": [0] * NXT,
                     "last_sq": None, "last_xn": None, "last_tr": None}
            deferred = []

            def flush_deferred():
                while deferred:
                    deferred.pop(0)()

            def load_w(buf, blocks):
                POOL.need(w_free[buf])
                tk = None
                for (c0, ncol, d0) in blocks:
                    src_ = win_d.rearrange("(c p) n -> p c n", p=128)[:, :, c0:c0 + ncol]
                    tk = wslot[buf].dma(POOL, WB[buf][:, :, d0:d0 + ncol], src_)
                w_ready[buf] = tk

            class NormPipe:
                def __init__(self, rows, tok0_of, ht_free_of):
                    self.rows, self.tok0_of, self.ht_free_of, self.n = rows, tok0_of, ht_free_of, len(rows)
                    self.base = state["nt"]
                    state["nt"] += self.n
                    self.step = 0
                    self.tk = [dict() for _ in range(7)]
                    self.ev = {}

                def _emit_step(self, step):
                    tk_ld, tk_sq, tk_v, tk_r, tk_xn, tk_tr, _ = self.tk
                    rows, base = self.rows, self.base
                    i = step
                    if 0 <= i < self.n:
                        k = base + i
                        s = k % NXT
                        assert state["xt_gen"][s] == k // NXT, (state["xt_gen"], k)
                        SP.need(state["xt_free"][s])
                        tk_ld[i] = xslot[s].dma(SP, XT[s][:], x_d[rows[i] * 128:(rows[i] + 1) * 128, :])
                    i = step - 1
                    if 0 <= i < self.n:
                        k = base + i
                        s = k % NXT
                        ACT.need(tk_ld[i], state["last_sq"])
                        tk_sq[i] = ACT.mark(ACT.eng.activation(out=JUNK[:], in_=XT[s][:], func=AF.Square,
                                                               accum_out=ss_all[:, k:k + 1]))
                        state["last_sq"] = tk_sq[i]
                        DVE.need(tk_sq[i])
                        tk_v[i] = DVE.mark(DVE.eng.tensor_scalar(out=v_all[:, k:k + 1], in0=ss_all[:, k:k + 1],
                                                                 scalar1=1.0 / D, scalar2=EPS, op0=ALU.mult, op1=ALU.add))
                        POOL.need(tk_v[i])
                        tk_r[i] = POOL.mark(POOL.eng.tensor_tensor(out=rstd_all[:, k:k + 1], in0=v_all[:, k:k + 1],
                                                                   in1=mhalf[:], op=ALU.pow))
                    i = step - 2
                    if 0 <= i < self.n:
                        k = base + i
                        s = k % NXT
                        s2 = k % 2
                        DVE.need(tk_r[i], tk_ld[i], state["xn_free"][s2])
                        tk_xn[i] = DVE.mark(DVE.eng.tensor_scalar(out=XN[s2][:], in0=XT[s][:], scalar1=rstd_all[:, k:k + 1],
                                                                  scalar2=None, op0=ALU.mult))
                        state["last_xn"] = tk_xn[i]
                        state["xt_free"][s] = [tk_xn[i], tk_sq[i]]
                        state["xt_gen"][s] += 1
                    i = step - 3
                    if 0 <= i < self.n:
                        k = base + i
                        s2 = k % 2
                        PE.need(tk_xn[i], psT_free[s2], tk_idb)
                        ins = None
                        for c in range(8):
                            ins = PE.eng.transpose(out=psT[s2][:, c * 128:(c + 1) * 128], in_=XN[s2][:, c * 128:(c + 1) * 128],
                                                   identity=ident_b[:])
                        tk_tr[i] = PE.mark(ins)
                        state["last_tr"] = tk_tr[i]
                        state["xn_free"][s2] = tk_tr[i]
                    i = step - 4
                    if 0 <= i < self.n:
                        k = base + i
                        s2 = k % 2
                        t0_ = self.tok0_of(i)
                        DVE.need(tk_tr[i], self.ht_free_of(i), tkc)
                        self.ev[i] = DVE.mark(DVE.eng.tensor_tensor(
                            out=HT[:, :, t0_:t0_ + 128], in0=psT[s2][:].rearrange("p (c t) -> p c t", c=8),
                            in1=gmix[:].unsqueeze(2).to_broadcast([128, 8, 128]), op=ALU.mult))
                        psT_free[s2] = self.ev[i]

                def advance(self, upto):
                    upto = min(upto, self.n - 1)
                    while self.step <= upto + 4:
                        self._emit_step(self.step)
                        self.step += 1
                    return self.ev[upto]

            def proj_fm(buf, wcol0, tokc, ready_tk):
                s = psp_next()
                PE.need(psP_free[s], w_ready[buf], ready_tk)
                ins = None
                for c in range(8):
                    ins = PE.eng.matmul(psP[s][:], lhsT=WB[buf][:, c, wcol0:wcol0 + 128],
                                        rhs=HT[:, c, tokc * 512:(tokc + 1) * 512], start=(c == 0), stop=(c == 7))
                tk = PE.mark(ins)
                flush_deferred()
                return s, tk

            def norm_bound(s, tk_mm):
                idx = mx_idx[0]
                mx_idx[0] += 1
                b = idx % 2
                ACT.need(tk_mm, sqb_free[b])
                t_sq = ACT.mark(ACT.eng.activation(out=SQB[b][:], in_=psP[s][:], func=AF.Square))

                def part_b():
                    PE.need(t_sq, psM_free[0])
                    t_m = PE.mark(PE.eng.matmul(psM[:], lhsT=ones_b[:], rhs=SQB[b][:], start=True, stop=True))
                    sqb_free[b] = t_m
                    DVE.need(t_m)
                    t_r = DVE.mark(DVE.eng.tensor_reduce(out=mx_all[:, idx:idx + 1], in_=psM[:], axis=AX.X, op=ALU.max))
                    psM_free[0] = t_r

                deferred.append(part_b)
                return t_sq

            def k_chunk(buf, h, tokc, tokbase, ready_tk):
                s, tk = proj_fm(buf, h * 128, tokc, ready_tk)
                ACT.need(tk)
                t_cp = ACT.mark(ACT.eng.activation(out=KT[:, h, tokbase + tokc * 512: tokbase + (tokc + 1) * 512], in_=psP[s][:],
                                                   func=AF.Copy))
                t_sq = norm_bound(s, tk)
                psP_free[s] = [t_cp, t_sq]

            def q_chunk(buf, h, tokc, ready_tk):
                s, tk = proj_fm(buf, h * 128, tokc, ready_tk)
                ACT.need(tk)
                t_cp = ACT.mark(ACT.eng.activation(out=QT[:, h, tokc * 512:(tokc + 1) * 512], in_=psP[s][:],
                                                   func=AF.Copy, scale=0.125))
                t_sq = norm_bound(s, tk)
                psP_free[s] = [t_cp, t_sq]

            def v_tile(buf, tl, rglob, ready_tk):
                s = psp_next()
                PE.need(psP_free[s], w_ready[buf], ready_tk)
                ins = None
                for c in range(8):
                    ins = PE.eng.matmul(psP[s][:], lhsT=HT[:, c, tl * 128:(tl + 1) * 128], rhs=WB[buf][:, c, :],
                                        start=(c == 0), stop=(c == 7))
                tk = PE.mark(ins)
                flush_deferred()
                DVE.need(tk, tk_pc)
                psP_free[s] = DVE.mark(DVE.eng.tensor_copy(out=VA[:, rglob, :, 0:128],
                                                           in_=psP[s][:].rearrange("p (h d) -> p h d", h=NH)))

            load_w(0, [(512, 512, 0)])
            load_w(1, [(1024, 512, 0)])
            ht_chunk_free = [None] * 4
            np_a = NormPipe(list(range(16, 32)), lambda i: i * 128, lambda i: None)
            np_b = NormPipe(list(range(0, 16)), lambda i: i * 128, lambda i: ht_chunk_free[i // 4])
            ev0 = np_a.advance(0)
            DVE.need(ev0)
            tk_hth = DVE.mark(DVE.eng.tensor_copy(out=HTH[:], in_=HT[:, :, 0:2]))
            LAG = 4
            for s in range(32 + LAG):
                if s < 16:
                    np_a.advance(s)
                elif s < 32:
                    np_b.advance(s - 16)
                if s == 9:
                    late_consts()
                if s == 8:
                    load_w(2, [(0, 512, 0)])
                m = s - LAG
                if 0 <= m < 16:
                    j, r = m // 4, m % 4
                    rdy = np_a.advance(4 * j + 3)
                    k_chunk(0, r, j, SO, rdy)
                    v_tile(1, 4 * j + r, 16 + 4 * j + r, rdy)
                    if r == 3:
                        ht_chunk_free[j] = PE.last()
                elif 16 <= m < 32:
                    j, r = (m - 16) // 4, (m - 16) % 4
                    rdy = np_b.advance(4 * j + 3)
                    q_chunk(2, r, j, rdy)
            flush_deferred()
            rdy_all = np_b.advance(15)
            norm_done = [state["last_sq"], state["last_xn"], state["last_tr"]]
            w_free[2] = PE.last()
            load_w(2, [(1536, 512, 0)])
            for tokc in range(4):
                for h in range(NH):
                    k_chunk(0, h, tokc, 0, rdy_all)
            flush_deferred()
            w_free[0] = PE.last()
            DVE.need(DVE.last())
            nq = 16
            nk = 32
            t_a = DVE.mark(DVE.eng.tensor_reduce(out=mq[:, 0:1], in_=mx_all[:, 16:32], axis=AX.X, op=ALU.max))
            DVE.need(t_a)
            DVE.eng.tensor_reduce(out=mk2[:, 0:1], in_=mx_all[:, 0:16], axis=AX.X, op=ALU.max)
            t_b = DVE.mark(DVE.eng.tensor_reduce(out=mk2[:, 1:2], in_=mx_all[:, 32:48], axis=AX.X, op=ALU.max))
            DVE.need(t_b)
            t_c = DVE.mark(DVE.eng.tensor_tensor(out=mq[:, 1:2], in0=mk2[:, 0:1], in1=mk2[:, 1:2], op=ALU.max))
            DVE.need(t_c)
            t_d = DVE.mark(DVE.eng.tensor_tensor(out=mk2[:, 0:1], in0=mq[:, 0:1], in1=mq[:, 1:2], op=ALU.add))
            DVE.need(t_d)
            tk_negM = DVE.mark(DVE.eng.tensor_scalar(out=negM[:], in0=mk2[:, 0:1], scalar1=-1.0 / 16.0, scalar2=None, op0=ALU.mult))

            DVE.need(tk_negM, LC["t_io"])
            for h in range(NH):
                DVE.eng.tensor_scalar(out=biasA[:, h, :], in0=iotaA[:], scalar1=-SLOPES[h], scalar2=negM[:], op0=ALU.mult, op1=ALU.add)
                DVE.mark(DVE.eng.tensor_scalar(out=biasB[:, h, :], in0=iotaB[:], scalar1=-SLOPES[h], scalar2=negM[:], op0=ALU.mult, op1=ALU.add))


            def conv_blocks(cc):
                return [(2048 + cc * 128, 128, 0), (2560 + cc * 128, 128, 128), (3072 + cc * 128, 128, 256),
                        (3584 + cc * 128, 128, 384)]

            load_w(0, conv_blocks(0))
            for tl in range(16):
                v_tile(1, tl, tl, rdy_all)
            w_free[1] = PE.last()
            load_w(1, conv_blocks(1))
            th_free = [norm_done, norm_done]
            thi = [0]
            for tokc in range(4):
                for h in range(NH):
                    s, tk = proj_fm(2, h * 128, tokc, rdy_all)
                    b = thi[0] % 2
                    thi[0] += 1
                    ACT.need(tk, th_free[b])
                    t_th = ACT.mark(ACT.eng.activation(out=TH[b][:], in_=psP[s][:], func=AF.Tanh, scale=0.5))
                    DVE.need(t_th)
                    t_sg = DVE.mark(DVE.eng.scalar_tensor_tensor(out=SGA[:, h, tokc * 512:(tokc + 1) * 512], in0=TH[b][:],
                                                                 scalar=1.0, in1=psP[s][:], op0=ALU.add, op1=ALU.mult))
                    th_free[b] = t_sg
                    psP_free[s] = t_sg
            w_free[2] = PE.last()
            load_w(2, conv_blocks(2))
            POOL.need(norm_done)
            tk_pad = POOL.mark(POOL.eng.memset(UF[:, 0:1], 0.0))
            uf_free = norm_done
            ca_free = [norm_done, norm_done]
            hs_free = norm_done
            conv_buf = [0, 1, 2, 0]
            for cc in range(4):
                buf = conv_buf[cc]
                PE.need(psM_free[0], w_ready[buf], tk_hth)
                ins = None
                for g_ in range(2):
                    for c in range(8):
                        ins = PE.eng.matmul(psM[:, g_ * 2:g_ * 2 + 2], lhsT=WB[buf][:, c, 128 + g_ * 128:256 + g_ * 128],
                                            rhs=HTH[:, c, :], start=(c == 0), stop=(c == 7))
                tk_h = PE.mark(ins)
                DVE.need(tk_h)
                t_ht = DVE.mark(DVE.eng.tensor_copy(out=halo_t[:], in_=psM[:, 0:4]))
                psM_free[0] = t_ht
                DVE.need(t_ht, uf_free)
                tk_hl = DVE.mark(DVE.eng.tensor_tensor(out=UF[:, 2049:2050], in0=halo_t[:, 0:1], in1=halo_t[:, 2:3], op=ALU.mult))
                tk_u = []
                for tokc in range(4):
                    sC, tkC = proj_fm(buf, 128, tokc, rdy_all)
                    sH, tkH = proj_fm(buf, 256, tokc, rdy_all)
                    ACT.need(tkH, hs_free)
                    t_hs = ACT.mark(ACT.eng.activation(out=HS[:], in_=psP[sH][:], func=AF.Copy))
                    psP_free[sH] = t_hs
                    DVE.need(t_hs, tkC, uf_free)
                    t_u = DVE.mark(DVE.eng.tensor_tensor(out=UF[:, 1 + tokc * 512: 1 + (tokc + 1) * 512], in0=psP[sC][:],
                                                         in1=HS[:], op=ALU.mult))
                    hs_free = t_u
                    psP_free[sC] = t_u
                    tk_u.append(t_u)
                last_readers = []
                for tokc in range(4):
                    sB, tkB = proj_fm(buf, 0, tokc, rdy_all)
                    sG, tkG = proj_fm(buf, 384, tokc, rdy_all)
                    b = thi[0] % 2
                    thi[0] += 1
                    ACT.need(tkG, th_free[b])
                    t_th = ACT.mark(ACT.eng.activation(out=TH[b][:], in_=psP[sG][:], func=AF.Tanh, scale=0.5))
                    DVE.need(t_th)
                    t_sg = DVE.mark(DVE.eng.scalar_tensor_tensor(out=TH[b][:], in0=TH[b][:], scalar=1.0, in1=psP[sG][:],
                                                                 op0=ALU.add, op1=ALU.mult))
                    psP_free[sG] = t_sg
                    DVE.need(t_sg, tkB)
                    t_sgb = DVE.mark(DVE.eng.scalar_tensor_tensor(out=TH[b][:], in0=TH[b][:], scalar=0.5, in1=psP[sB][:],
                                                                  op0=ALU.mult, op1=ALU.mult))
                    psP_free[sB] = t_sgb
                    a = thi[0] % 2
                    c0 = 1 + tokc * 512
                    DVE.need(tk_u[min(tokc + 1, 3)], tk_pad, tk_hl, ca_free[a], tkc)
                    t_a = DVE.mark(DVE.eng.tensor_scalar(out=CA[a][:], in0=UF[:, c0 - 1:c0 + 511],
                                                         scalar1=convw[:, cc * 3:cc * 3 + 1], scalar2=None, op0=ALU.mult))
                    DVE.need(t_a)
                    t_a = DVE.mark(DVE.eng.scalar_tensor_tensor(out=CA[a][:], in0=UF[:, c0:c0 + 512],
                                                                scalar=convw[:, cc * 3 + 1:cc * 3 + 2], in1=CA[a][:],
                                                                op0=ALU.mult, op1=ALU.add))
                    DVE.need(t_a)
                    t_a = DVE.mark(DVE.eng.scalar_tensor_tensor(out=CA[a][:], in0=UF[:, c0 + 1:c0 + 513],
                                                                scalar=convw[:, cc * 3 + 2:cc * 3 + 3], in1=CA[a][:],
                                                                op0=ALU.mult, op1=ALU.add))
                    DVE.need(t_a, t_sgb)
                    t_o = DVE.mark(DVE.eng.tensor_tensor(out=MIXC[:, cc, tokc * 512:(tokc + 1) * 512], in0=CA[a][:],
                                                         in1=TH[b][:], op=ALU.mult))
                    ca_free[a] = t_o
                    th_free[b] = t_o
                    last_readers = [t_o]
                uf_free = last_readers
                w_free[buf] = PE.last()
                if cc == 0:
                    load_w(0, conv_blocks(3))

            P1END = {"psP": list(psP_free), "psM": psM_free[0], "pe": PE.last(), "act": ACT.last(), "dve": DVE.last(),
                     "pool": POOL.last()}

        cv = Carver()
        MIXA = cv.take([128, NH, SO], BF16)
        ET = [cv.take([128, 1024], BF16) for _ in range(3)]
        DG = [cv.take([128, 1024], F32) for _ in range(2)]
        OACC = [cv.take([128, 8, 129], F32) for _ in range(2)]
        OT = [cv.take([128, 4, 128], F32) for _ in range(2)]
        ON = [cv.take([128, 4, 128], F32) for _ in range(2)]
        STG = [cv.take([128, 8, 129], F32) for _ in range(1)]
        WOUT = cv.take([128, 8, D], BF16)
        WG = cv.take([128, 8, D], BF16)
        WP = cv.take([128, 2, D], BF16)
        print("phase2 arena used", cv.off)

        class RawCarver:
            def __init__(self, flat, nbytes):
                self.flat, self.nbytes, self.off = flat, nbytes, 0

            def take(self, shape, dt):
                esz = 2 if dt == BF16 else 4
                n = int(np.prod(shape[1:])) * esz
                self.off = (self.off + 63) // 64 * 64
                assert self.off + n <= self.nbytes, (self.off, n, self.nbytes)
                a = self.flat[:, self.off // 2:(self.off + n) // 2]
                self.off += n
                if dt != BF16:
                    a = a.bitcast(dt)
                if len(shape) == 3:
                    a = a.rearrange("p (a b) -> p a b", a=shape[1])
                return a

        rc1 = RawCarver(KTF[:], NH * S * 2)
        rc2 = RawCarver(VAF[:], 32 * NH * 129 * 2)
        rc3 = RawCarver(QTF[:], NH * SO * 2)
        rc4 = RawCarver(SGAF[:], NH * SO * 2)
        PF32 = rc1.take([128, 16, 256], F32)
        XT3 = [rc1.take([128, D], F32) for _ in range(2)]
        X1 = [rc1.take([128, D], F32) for _ in range(2)]
        TH3 = [rc2.take([128, D], F32) for _ in range(2)]
        X2 = [rc2.take([128, D], F32) for _ in range(2)]
        YO = [rc2.take([128, D], F32) for _ in range(2)]
        PTALL = rc3.take([128, 2, SO], BF16)
        X1N = [rc3.take([128, D], BF16) for _ in range(2)]
        X1NT = [rc3.take([128, 8, 128], BF16) for _ in range(2)]
        GFIN = rc4.take([128, D], F32)
        GPLEB = rc4.take([128, D], F32)
        JUNK3 = rc4.take([128, D], BF16)
        ss3 = sb("ss3", [128, 16], F32)
        v3 = sb("v3", [128, 16], F32)
        r3 = sb("r3", [128, 16], F32)
        ss4 = sb("ss4", [128, 16], F32)
        v4 = sb("v4", [128, 16], F32)
        r4 = sb("r4", [128, 16], F32)

        NTILE = SO // 128
        x3slot = [Slot(nc, es, "x3_%d" % i) for i in range(2)]
        oslot = [Slot(nc, es, "o3_%d" % i) for i in range(2)]
        gslot = Slot(nc, es, "gfin")
        ppslot = [Slot(nc, es, "pld%d" % g) for g in range(4)]
        tk_pld = []
        P3 = {}

        def prefetch_phase3():
            SP.need(PE.last(), DVE.last())
            gslot.dma(SP, GFIN[:], gfin_d)
            P3["gfin"] = gslot.dma(SP, GPLEB[:], gpleb_d)
            for t_ in range(2):
                P3["ldx", t_] = x3slot[t_].dma(SP, XT3[t_][:], x_d[t_ * 128:(t_ + 1) * 128, :])
            for g in range(4):
                tk_pld.append(ppslot[g].dma(SP, PF32[:, 4 * g:4 * g + 4, :],
                                         p_d.rearrange("(t p) c -> p t c", p=128)[:, 4 * g:4 * g + 4, :]))


        wt_slot = Slot(nc, es, "wtail")
        POOL.need(P1END["pe"], P1END["act"], P1END["dve"])
        for half_ in range(2):
            wt_slot.dma(POOL, WOUT[:, :, half_ * 512:(half_ + 1) * 512],
                        wout_d.rearrange("(c p) n -> p c n", p=128)[:, :, half_ * 512:(half_ + 1) * 512])
        for half_ in range(2):
            wt_slot.dma(POOL, WG[:, :, half_ * 512:(half_ + 1) * 512],
                        wg_d.rearrange("(c p) n -> p c n", p=128)[:, :, half_ * 512:(half_ + 1) * 512])
        for half_ in range(2):
            tk_wtail = wt_slot.dma(POOL, WP[:, :, half_ * 512:(half_ + 1) * 512],
                                   wp_d.rearrange("(c p) n -> p c n", p=128)[:, :, half_ * 512:(half_ + 1) * 512])

        pes = contextlib.ExitStack()
        with pes:
            psS = [es.enter_context(nc.psum_tensor("psS%d" % i, [128, 1024], F32)) for i in range(2)]
            psBig = es.enter_context(nc.psum_tensor("psBig", [128, 2048], F32))
            psA = [psBig[:, b_ * 512:(b_ + 1) * 512] for b_ in range(3)]
            psO = psBig[:, 1536:2048]

            def acc_region(r):
                return psA[r // 3][:, (r % 3) * 129:(r % 3) * 129 + 129]

            tiles = []
            for h in range(NH):
                for qc in range(4):
                    unit = h * 4 + qc
                    groups = []
                    groups.append(("B", list(range(4 * qc + 4, 32))))
                    if qc > 0:
                        groups.append(("A", list(range(0, 4 * qc))))
                    groups.append(("C", list(range(4 * qc, 4 * qc + 4))))
                    def dead(g, kt, h=h, qc=qc):
                        dmin = 128 * (4 * qc - kt) - 127 if g == "A" else 128 * (kt - 4 * qc) - 511
                        return g != "C" and SLOPES[h] * dmin >= 110.0
                    groups = [(g, [kt for kt in kts if not dead(g, kt)]) for g, kts in groups]
                    groups = [(g, kts) for g, kts in groups if kts]
                    for gi, (g, kts) in enumerate(groups):
                        for ki, kt in enumerate(kts):
                            tiles.append(dict(h=h, qc=qc, unit=unit, g=g, kt=kt, first=(ki == 0), last=(ki == len(kts) - 1),
                                              gfirst=(gi == 0), unit_last=(gi == len(groups) - 1 and ki == len(kts) - 1)))
            NT = len(tiles)
            psS_free = [None, [P1END["psP"][0], P1END["psP"][1]]]
            et_free = [None, None, None]
            dg_free = [None, None]
            dgi = [0]
            tk_E = {}
            acc_free = [None] * 8
            for r_ in range(8):
                acc_free[r_] = [P1END["psP"][2], P1END["psP"][3], P1END["psM"]][r_ // 3]
            oacc_free = [None, None]
            ot_free = [None, None]
            on_free = [None, None]
            pso_free = [None]
            pending = []
            stg_free = [None, None]
            stg_i = [0]
            unit_evac = {}

            def emit_S(i):
                t = tiles[i]
                h, qc, kt = t["h"], t["qc"], t["kt"]
                b = i % 2
                PE.need(psS_free[b])
                PE.eng.matmul(psS[b][:, 0:512], lhsT=KT[0:64, h, kt * 128:(kt + 1) * 128], rhs=QT[0:64, h, qc * 512:(qc + 1) * 512],
                              start=True, stop=True)
                tk_s = PE.mark(PE.eng.matmul(psS[b][:, 512:1024], lhsT=KT[64:128, h, kt * 128:(kt + 1) * 128],
                                             rhs=QT[64:128, h, qc * 512:(qc + 1) * 512], start=True, stop=True))
                e = i % 3
                if t["g"] == "C":
                    g_ = dgi[0] % 2
                    dgi[0] += 1
                    off = 384 - 128 * (kt - 4 * qc)
                    DVE.need(tk_s, dg_free[g_])
                    ACT.need(et_free[e])
                    for j in range(2):
                        tk_d = DVE.mark(DVE.eng.scalar_tensor_tensor(
                            out=DG[g_][:, j * 512:(j + 1) * 512], in0=T_unit[:, off:off + 512], scalar=-SLOPES[h],
                            in1=psS[b][:, j * 512:(j + 1) * 512], op0=ALU.mult, op1=ALU.add))
                        ACT.need(tk_d)
                        tk_E[i] = ACT.mark(ACT.eng.activation(out=ET[e][:, j * 512:(j + 1) * 512], in_=DG[g_][:, j * 512:(j + 1) * 512],
                                                              func=AF.Exp, bias=negM[:], scale=1.0))
                    psS_free[b] = tk_d
                    dg_free[g_] = tk_E[i]
                else:
                    if t["g"] == "A":
                        bias = biasA[:, h, 4 * qc - kt:4 * qc - kt + 1]
                    else:
                        bias = biasB[:, h, kt - 4 * qc:kt - 4 * qc + 1]
                    ACT.need(tk_s, et_free[e])
                    tk_E[i] = ACT.mark(ACT.eng.activation(out=ET[e][:], in_=psS[b][:], func=AF.Exp, bias=bias, scale=1.0))
                    psS_free[b] = tk_E[i]

            def finalize_part1(unit, h, qc, tks):
                u = unit % 2
                DVE.need(tks, ot_free[u])
                t_ = DVE.mark(DVE.eng.reciprocal(out=rs[u][:], in_=OACC[u][:, :, 128]))
                DVE.need(t_, LC["tk_const"])
                t_c2 = DVE.mark(DVE.eng.tensor_scalar(out=c2[u][:], in0=rs[u][:, 4:8], scalar1=neglam[:], scalar2=None, op0=ALU.mult))
                DVE.need(t_c2)
                for sb_ in range(4):
                    t_ = DVE.mark(DVE.eng.tensor_scalar(out=OT[u][:, sb_, :], in0=OACC[u][:, 4 + sb_, 0:128],
                                                        scalar1=c2[u][:, sb_:sb_ + 1], scalar2=None, op0=ALU.mult))
                DVE.need(t_)
                for sb_ in range(4):
                    t_ = DVE.mark(DVE.eng.scalar_tensor_tensor(out=OT[u][:, sb_, :], in0=OACC[u][:, sb_, 0:128],
                                                               scalar=rs[u][:, sb_:sb_ + 1], in1=OT[u][:, sb_, :],
                                                               op0=ALU.mult, op1=ALU.add))
                oacc_free[u] = t_
                DVE.need(t_)
                DVE.need(on_free[u])
                t_ = DVE.mark(DVE.eng.tensor_tensor(out=ON[u][:], in0=OT[u][:], in1=OT[u][:], op=ALU.mult))
                DVE.need(t_)
                t_ = DVE.mark(DVE.eng.tensor_reduce(out=ssq[u][:], in_=ON[u][:], axis=AX.X, op=ALU.add))
                DVE.need(t_)
                POOL.need(t_)
                t_v = POOL.mark(POOL.eng.tensor_scalar(out=vsq[u][:], in0=ssq[u][:], scalar1=1.0 / 128.0, scalar2=EPS,
                                                       op0=ALU.mult, op1=ALU.add))
                POOL.need(t_v)
                t_p = POOL.mark(POOL.eng.tensor_tensor(out=rsd[u][:], in0=vsq[u][:], in1=mhalf[:].to_broadcast([128, 4]), op=ALU.pow))
                POOL.need(t_p, on_free[u])
                t_on = POOL.mark(POOL.eng.tensor_tensor(out=ON[u][:], in0=OT[u][:],
                                                        in1=rsd[u][:].unsqueeze(2).to_broadcast([128, 4, 128]), op=ALU.mult))
                ot_free[u] = t_on

                def part2():
                    PE.need(t_on, pso_free[0])
                    ins = None
                    for sb_ in range(4):
                        ins = PE.eng.transpose(out=psO[:, sb_ * 128:(sb_ + 1) * 128], in_=ON[u][:, sb_, :], identity=ident_f[:])
                    t_tr = PE.mark(ins)
                    on_free[u] = t_tr
                    DVE.need(t_tr)
                    t_m = DVE.mark(DVE.eng.scalar_tensor_tensor(out=MIXA[:, h, qc * 512:(qc + 1) * 512], in0=psO[:], scalar=sublns[:],
                                                                in1=SGA[:, h, qc * 512:(qc + 1) * 512], op0=ALU.mult, op1=ALU.mult))
                    pso_free[0] = t_m

                pending.append([12, part2])

            def emit_PV(i):
                t = tiles[i]
                h, qc, kt, unit = t["h"], t["qc"], t["kt"], t["unit"]
                e = i % 3
                u = unit % 2
                PE.need(tk_E[i])
                ins = None
                tk_bank = []
                for r in range(8):
                    j, sb_ = r // 4, r % 4
                    if t["first"] and r % 3 == 0:
                        PE.need([acc_free[rr] for rr in range(r, min(r + 3, 8))])
                    ins = PE.eng.matmul(acc_region(r), lhsT=ET[e][:, j * 512 + sb_ * 128: j * 512 + (sb_ + 1) * 128],
                                        rhs=VA[:, kt, h, :], start=(t["first"] and r % 3 == 0), stop=t["last"],
                                        skip_group_check=True)
                    if t["last"] and r in (2, 5):
                        tk_bank.append(PE.mark(ins))
                tk_pv = PE.mark(ins)
                tk_bank.append(tk_pv)
                et_free[e] = tk_pv
                if t["last"]:
                    tks = [None] * 8
                    if t["g"] == "C":
                        DVE.need(unit_evac[unit])
                        for bk in range(3):
                            nreg = 3 if bk < 2 else 2
                            DVE.need(tk_bank[bk])
                            tk_ = DVE.mark(DVE.eng.tensor_tensor(
                                out=OACC[u][:, 3 * bk:3 * bk + nreg, :],
                                in0=psA[bk][:, 0:nreg * 129].rearrange("p (r d) -> p r d", r=nreg),
                                in1=OACC[u][:, 3 * bk:3 * bk + nreg, :], op=ALU.add))
                            for r in range(3 * bk, 3 * bk + nreg):
                                tks[r] = tk_
                    else:
                        fo = 0 if t["g"] == "A" else 4
                        sg_ = 0
                        stg_i[0] += 1
                        DVE.need(stg_free[sg_])
                        cps = []
                        for bk in range(3):
                            nreg = 3 if bk < 2 else 2
                            DVE.need(tk_bank[bk])
                            tk_ = DVE.mark(DVE.eng.tensor_copy(
                                out=STG[sg_][:, 3 * bk:3 * bk + nreg, :],
                                in_=psA[bk][:, 0:nreg * 129].rearrange("p (r d) -> p r d", r=nreg)))
                            cps.append(tk_)
                            for r in range(3 * bk, 3 * bk + nreg):
                                tks[r] = tk_
                        gfirst_ = t["gfirst"]
                        prev_ev = None if gfirst_ else unit_evac[unit]
                        ev_ = []
                        unit_evac[unit] = ev_
                        stg_free[sg_] = ev_

                        def scaled_acc(gfirst_=gfirst_, prev_ev=prev_ev, ev_=ev_, cps=cps, u=u, h=h, fo=fo, sg_=sg_):
                            if gfirst_:
                                DVE.need(cps, oacc_free[u])
                            else:
                                DVE.need(cps, prev_ev)
                            for r in range(8):
                                j, sb_ = r // 4, r % 4
                                if gfirst_:
                                    ev_.append(DVE.mark(DVE.eng.tensor_scalar(
                                        out=OACC[u][:, r, :], in0=STG[sg_][:, r, :], scalar1=fAB[:, h, fo + sb_:fo + sb_ + 1],
                                        scalar2=None, op0=ALU.mult)))
                                else:
                                    ev_.append(DVE.mark(DVE.eng.scalar_tensor_tensor(
                                        out=OACC[u][:, r, :], in0=STG[sg_][:, r, :], scalar=fAB[:, h, fo + sb_:fo + sb_ + 1],
                                        in1=OACC[u][:, r, :], op0=ALU.mult, op1=ALU.add)))

                        pending.append([3, scaled_acc])
                    for r in range(8):
                        acc_free[r] = tks[r]
                    if t["g"] == "C":
                        unit_evac[unit] = tks
                    if t["unit_last"]:
                        finalize_part1(unit, h, qc, unit_evac[unit])

            first_h3 = min(k for k, t_ in enumerate(tiles) if t_["h"] == NH - 1)
            for i in range(NT + 2):
                if i < NT:
                    emit_S(i)
                if i == first_h3 + 16:
                    prefetch_phase3()
                if i >= 2:
                    emit_PV(i - 2)
                for pnd in list(pending):
                    pnd[0] -= 1
                    if pnd[0] <= 0:
                        pending.remove(pnd)
                        pnd[1]()
            EY = {}
            tk_lastS = PE.last()
            PE.need(psS_free[0], tk_wtail)
            ins = None
            for half_ in range(2):
                for c in range(8):
                    lhs = MIXA[:, c, 0:128] if c < 4 else MIXC[:, c - 4, 0:128]
                    ins = PE.eng.matmul(psS[0][:, half_ * 512:(half_ + 1) * 512], lhsT=lhs, rhs=WOUT[:, c, half_ * 512:(half_ + 1) * 512],
                                        start=(c == 0), stop=(c == 7))
            EY["y0"] = PE.mark(ins)
            EY["ptall"] = []
            tk_prev = psS_free[1]
            ACT.need(tk_lastS)
            for g in range(4):
                PE.need(tk_pld[g], tk_prev)
                for c2_ in range(2):
                    for tt in range(4):
                        ins = PE.eng.transpose(out=psS[1][:, (c2_ * 4 + tt) * 128:(c2_ * 4 + tt + 1) * 128],
                                               in_=PF32[:, 4 * g + tt, c2_ * 128:(c2_ + 1) * 128], identity=ident_f[:])
                t_tr = PE.mark(ins)
                ACT.need(t_tr)
                tk_prev = ACT.mark(ACT.eng.activation(out=PTALL[:, :, 4 * g * 128:(4 * g + 4) * 128],
                                                      in_=psS[1][:].rearrange("p (c t) -> p c t", c=2), func=AF.Copy))
                EY["ptall"].append(tk_prev)
            for pnd in list(pending):
                pnd[1]()
            pending.clear()
            barrier()

        pes = contextlib.ExitStack()
        with pes:
            psY, psG = psS[0], psS[1]
            psPP = psBig[:, 0:1024]
            psT3 = psBig[:, 1024:1536].bitcast(BF16)
            T = {}
            fr = {"xt3": [None, None], "x1": [None, None], "x1n": [None, None], "x1nt": [None, None],
                  "th3": [None, None], "x2": [None, None], "yo": [None, None],
                  "psY": None, "psG": None, "psPP": None, "psT3": None}
            stores = []

            def st_load(t):
                s = t % 2
                SP.need(fr["xt3"][s])
                T["ldx", t] = x3slot[s].dma(SP, XT3[s][:], x_d[t * 128:(t + 1) * 128, :])

            def st_y(t):
                s = t % 2
                if t == 0:
                    T["y", t] = EY["y0"]
                else:
                    PE.need(fr["psY"], tk_wtail)
                    ins = None
                    for half_ in range(2):
                        for c in range(8):
                            lhs = MIXA[:, c, t * 128:(t + 1) * 128] if c < 4 else MIXC[:, c - 4, t * 128:(t + 1) * 128]
                            ins = PE.eng.matmul(psY[:, half_ * 512:(half_ + 1) * 512], lhsT=lhs, rhs=WOUT[:, c, half_ * 512:(half_ + 1) * 512],
                                                start=(c == 0), stop=(c == 7))
                    T["y", t] = PE.mark(ins)
                DVE.need(T["y", t], T["ldx", t], fr["x1"][s])
                T["x1", t] = DVE.mark(DVE.eng.tensor_tensor(out=X1[s][:], in0=psY[:], in1=XT3[s][:], op=ALU.add))
                fr["psY"] = T["x1", t]
                fr["xt3"][s] = T["x1", t]

            def st_sq3(t):
                s = t % 2
                ACT.need(T["x1", t], fr.get("junk"))
                T["sq3", t] = fr["junk"] = ACT.mark(ACT.eng.activation(out=JUNK3[:], in_=X1[s][:], func=AF.Square, accum_out=ss3[:, t:t + 1]))
                DVE.need(T["sq3", t])
                T["v3", t] = DVE.mark(DVE.eng.tensor_scalar(out=v3[:, t:t + 1], in0=ss3[:, t:t + 1], scalar1=1.0 / D, scalar2=EPS,
                                                            op0=ALU.mult, op1=ALU.add))
                POOL.need(T["v3", t])
                T["r3", t] = POOL.mark(POOL.eng.tensor_tensor(out=r3[:, t:t + 1], in0=v3[:, t:t + 1], in1=mhalf[:], op=ALU.pow))

            def st_x1n(t):
                s = t % 2
                DVE.need(T["r3", t], fr["x1n"][s], P3["gfin"])
                T["x1n", t] = DVE.mark(DVE.eng.scalar_tensor_tensor(out=X1N[s][:], in0=X1[s][:], scalar=r3[:, t:t + 1], in1=GPLEB[:],
                                                                    op0=ALU.mult, op1=ALU.mult))

            def st_c1(t):
                s = t % 2
                PE.need(T["x1n", t], fr["psT3"])
                ins = None
                for c in range(8):
                    ins = PE.eng.transpose(out=psT3[:, c * 128:(c + 1) * 128], in_=X1N[s][:, c * 128:(c + 1) * 128], identity=ident_b[:])
                T["tr3", t] = PE.mark(ins)
                fr["x1n"][s] = T["tr3", t]
                ACT.need(T["tr3", t], fr["x1nt"][s])
                T["x1nt_a", t] = ACT.mark(ACT.eng.activation(out=X1NT[s][:, 0:4, :].rearrange("p c t -> p (c t)"),
                                                             in_=psT3[:, 0:512], func=AF.Copy))
                T["x1nt", t] = ACT.mark(ACT.eng.activation(out=X1NT[s][:, 4:8, :].rearrange("p c t -> p (c t)"),
                                                           in_=psT3[:, 512:1024], func=AF.Copy))
                fr["psT3"] = T["x1nt", t]
                PE.need(T["ptall"], fr["psPP"])
                for half_ in range(2):
                    for c2_ in range(2):
                        ins = PE.eng.matmul(psPP[:, half_ * 512:(half_ + 1) * 512], lhsT=PTALL[:, c2_, t * 128:(t + 1) * 128],
                                            rhs=WP[:, c2_, half_ * 512:(half_ + 1) * 512], start=(c2_ == 0), stop=(c2_ == 1))
                T["pp", t] = PE.mark(ins)
                PE.need(T["x1nt_a", t], fr["psG"])
                for cg in range(2):
                    if cg == 1:
                        PE.need(T["x1nt", t])
                    for half_ in range(2):
                        for c in range(4 * cg, 4 * cg + 4):
                            ins = PE.eng.matmul(psG[:, half_ * 512:(half_ + 1) * 512], lhsT=X1NT[s][:, c, :],
                                                rhs=WG[:, c, half_ * 512:(half_ + 1) * 512], start=(c == 0), stop=(c == 7))
                T["g", t] = PE.mark(ins)
                fr["x1nt"][s] = T["g", t]

            def st_c2a(t):
                s = t % 2
                ACT.need(T["g", t], fr["th3"][s])
                T["th", t] = ACT.mark(ACT.eng.activation(out=TH3[s][:], in_=psG[:], func=AF.Tanh, scale=0.5))
                fr["psG"] = T["th", t]
                DVE.need(T["th", t], T["pp", t])
                T["gp", t] = DVE.mark(DVE.eng.scalar_tensor_tensor(out=TH3[s][:], in0=TH3[s][:], scalar=1.0, in1=psPP[:],
                                                                   op0=ALU.add, op1=ALU.mult))
                fr["psPP"] = T["gp", t]
                DVE.need(T["gp", t], fr["x2"][s])
                T["x2", t] = DVE.mark(DVE.eng.scalar_tensor_tensor(out=X2[s][:], in0=TH3[s][:], scalar=0.5, in1=X1[s][:],
                                                                   op0=ALU.mult, op1=ALU.add))
                fr["th3"][s] = T["x2", t]
                fr["x1"][s] = T["x2", t]

            def st_c2b(t):
                s = t % 2
                ACT.need(T["x2", t], fr.get("junk"))
                T["sq4", t] = fr["junk"] = ACT.mark(ACT.eng.activation(out=JUNK3[:], in_=X2[s][:], func=AF.Square, accum_out=ss4[:, t:t + 1]))
                DVE.need(T["sq4", t])
                T["v4", t] = DVE.mark(DVE.eng.tensor_scalar(out=v4[:, t:t + 1], in0=ss4[:, t:t + 1], scalar1=1.0 / D, scalar2=EPS,
                                                            op0=ALU.mult, op1=ALU.add))
                POOL.need(T["v4", t])
                T["r4", t] = POOL.mark(POOL.eng.tensor_tensor(out=r4[:, t:t + 1], in0=v4[:, t:t + 1], in1=mhalf[:], op=ALU.pow))

            def st_d(t):
                s = t % 2
                DVE.need(T["r4", t], fr["yo"][s], P3["gfin"])
                T["yo", t] = DVE.mark(DVE.eng.scalar_tensor_tensor(out=YO[s][:], in0=X2[s][:], scalar=r4[:, t:t + 1], in1=GFIN[:],
                                                                   op0=ALU.mult, op1=ALU.mult))
                fr["x2"][s] = T["yo", t]
                SP.need(T["yo", t])
                tk = oslot[s].dma(SP, y_d[t * 128:(t + 1) * 128, :], YO[s][:])
                fr["yo"][s] = tk
                stores.append(tk)

            T["ldx", 0], T["ldx", 1] = P3["ldx", 0], P3["ldx", 1]
            T["ptall"] = EY["ptall"]
            fr["psG"] = EY["ptall"][-1]
            for it in range(NTILE + 4):
                if 0 <= it - 1 < NTILE:
                    st_x1n(it - 1)
                if 0 <= it - 2 < NTILE:
                    st_c2a(it - 2)
                if it < NTILE:
                    if 2 <= it + 1 < NTILE:
                        st_load(it + 1)
                    st_y(it)
                if 0 <= it - 1 < NTILE:
                    st_c1(it - 1)
                if 0 <= it - 2 < NTILE:
                    st_c2b(it - 2)
                if it < NTILE:
                    st_sq3(it)
                if 0 <= it - 3 < NTILE:
                    st_d(it - 3)
            SP.need(*stores)
            DVE.need(*stores)
            barrier()

        ost = Slot(nc, es, "out")
        tk_out = []
        if debug:
            dslot = Slot(nc, es, "dbgs")
            barrier()
            stage = Carver(40960).take([128, 4096], F32)
            tkd = None

            def dump(name, src_ap, n):
                nonlocal tkd
                DVE.need(tkd)
                t = DVE.mark(DVE.eng.tensor_copy(out=stage[:, 0:n], in_=src_ap))
                SP.need(t)
                tkd = dslot.dma(SP, dbg[name], stage[:, 0:n])

            if "qt" in dbg:
                dump("qt", QT[:, 0, :], SO)
            if "kt" in dbg:
                dump("kt", KT[:, 1, :], S)
            if "va" in dbg:
                dump("va", VA[:, 17, :, :].rearrange("p h d -> p (h d)"), 516)
            if "sga" in dbg:
                dump("sga", SGA[:, 2, :], SO)
            if "mixc" in dbg:
                dump("mixc", MIXC[:, 3, :], SO)
            if "mixc0" in dbg:
                dump("mixc0", MIXC[:, 0, :], SO)
            if "negm" in dbg:
                dump("negm", negM[:], 1)
            if "ht" in dbg:
                dump("ht", HT[:, 5, :], SO)
            for hh in range(NH):
                if "mixa%d" % hh in dbg:
                    dump("mixa%d" % hh, MIXA[:, hh, :], SO)
            SP.need(tkd)
            DVE.need(tkd)
    return nc


def _core_inputs(inputs, c):
    b, half = c // 2, c % 2
    x = np.asarray(inputs["x"][b], dtype=np.float32)
    p = np.asarray(inputs["p"][0, b], dtype=np.float32)
    cw = np.asarray(inputs["conv_w"][0], dtype=np.float32)
    if half == 1:
        x = x[::-1]
        p = p[::-1]
        cw = cw[::-1]
    p = p[:SO]
    lamv = np.concatenate([np.asarray(inputs[k][0], dtype=np.float32) for k in
                           ("lambda_q1", "lambda_k1", "lambda_q2", "lambda_k2")])
    return {
        "x": np.ascontiguousarray(x),
        "p": np.ascontiguousarray(p),
        "w_in": np.ascontiguousarray(inputs["w_in"][0], dtype=np.float32),
        "w_out": np.ascontiguousarray(inputs["w_out"][0], dtype=np.float32),
        "w_g": np.ascontiguousarray(inputs["w_ple_gate"][0], dtype=np.float32),
        "w_p": np.ascontiguousarray(inputs["w_ple_proj"][0], dtype=np.float32),
        "gmix": np.ascontiguousarray(np.asarray(inputs["mix_norm_g"][0], dtype=np.float32).reshape(8, 128).T),
        "gple": np.ascontiguousarray(np.asarray(inputs["ple_norm_g"][0], dtype=np.float32).reshape(8, 128).T),
        "gfin": np.ascontiguousarray(np.broadcast_to(np.asarray(inputs["final_norm_g"], dtype=np.float32)[None, :], (128, D))),
        "gpleb": np.ascontiguousarray(np.broadcast_to(np.asarray(inputs["ple_norm_g"][0], dtype=np.float32)[None, :], (128, D))),
        "subln": np.ascontiguousarray(np.asarray(inputs["subln_g"][0], dtype=np.float32).reshape(128, 1)),
        "convw": np.ascontiguousarray(cw.reshape(3, 4, 128).transpose(2, 1, 0).reshape(128, 12)),
        "lamv": np.ascontiguousarray(np.broadcast_to(lamv[None, :], (128, 256))),
    }


def kernel(**inputs):
    nc = build_program()
    in_maps = [_core_inputs(inputs, c) for c in range(NCORES)]
    res = run_bass_kernel_spmd(nc, in_maps, core_ids=list(range(NCORES)))
    out = np.empty((4, S, D), dtype=np.float32)
    for c in range(NCORES):
        b, half = c // 2, c % 2
        yc = res.results[c]["y"]
        if half == 0:
            out[b, :SO] = yc
        else:
            out[b, SO:] = yc[::-1]
    return out
```

```python
import contextlib
import numpy as np
import concourse.bass as bass
import concourse.mybir as mybir
from concourse.bass_utils import run_bass_kernel_spmd

F32 = mybir.dt.float32
BF16 = mybir.dt.bfloat16
AF = mybir.ActivationFunctionType
ALU = mybir.AluOpType
AX = mybir.AxisListType

NCORES = 8
S = 4096
SO = 2048
D = 1024
NH = 4
EPS = 1e-6
SLOPES = [2.0 ** (-8.0 * (h + 1) / NH) for h in range(NH)]
LAMBDA_INIT = 0.8 - 0.6 * 1.0


class Eng:
    def __init__(self, nc, es, name, eng):
        self.nc, self.name, self.eng = nc, name, eng
        self.sem = es.enter_context(nc.semaphore("sem_" + name))
        self.cnt = 0
        self.waited = {}

    def mark(self, instr):
        self.cnt += 1
        instr.then_inc(self.sem, 1)
        return (self.sem, self.cnt, self.name)

    def last(self):
        return (self.sem, self.cnt, self.name) if self.cnt else None

    def need(self, *tks):
        for tk in tks:
            if tk is None:
                continue
            if isinstance(tk, list):
                self.need(*tk)
                continue
            sem, val, name = tk
            if self.waited.get(name, 0) >= val:
                continue
            self.eng.wait_ge(sem, val)
            self.waited[name] = val


class Slot:
    def __init__(self, nc, es, name):
        self.sem = es.enter_context(nc.semaphore("dq_" + name))
        self.cnt = 0
        self.name = "dq_" + name

    def dma(self, q, out, in_, **kw):
        ins = q.eng.dma_start(out=out, in_=in_, **kw)
        self.cnt += 16
        ins.then_inc(self.sem, 16)
        return (self.sem, self.cnt, self.name)


def build_program(debug=None):
    nc = bass.Bass("TRN2", target_bir_lowering=False)

    def din(name, shape):
        return nc.dram_tensor(name, shape, F32, kind="ExternalInput").ap()

    x_d = din("x", [S, D])
    p_d = din("p", [SO, 256])
    win_d = din("w_in", [D, 4096])
    wout_d = din("w_out", [D, D])
    wg_d = din("w_g", [D, D])
    wp_d = din("w_p", [256, D])
    gmix_d = din("gmix", [128, 8])
    gple_d = din("gple", [128, 8])
    gfin_d = din("gfin", [128, D])
    gpleb_d = din("gpleb", [128, D])
    subln_d = din("subln", [128, 1])
    convw_d = din("convw", [128, 12])
    lam_d = din("lamv", [128, 256])
    y_d = nc.dram_tensor("y", [SO, D], F32, kind="ExternalOutput").ap()
    dbg = {}
    if debug:
        for name, shape in debug.items():
            dbg[name] = nc.dram_tensor("dbg_" + name, shape, F32, kind="ExternalOutput").ap()

    es = contextlib.ExitStack()
    with es:
        PE = Eng(nc, es, "pe", nc.tensor)
        ACT = Eng(nc, es, "act", nc.scalar)
        DVE = Eng(nc, es, "dve", nc.vector)
        POOL = Eng(nc, es, "pool", nc.gpsimd)
        SP = Eng(nc, es, "sp", nc.sync)
        ENGS = [PE, ACT, DVE, POOL]

        def sb(name, shape, dt):
            return es.enter_context(nc.sbuf_tensor(name, shape, dt))

        def barrier(engs=None, extra=()):
            tks = [e.last() for e in ENGS] + list(extra)
            for e in (engs or (ENGS + [SP])):
                e.need(*tks)

        QTF = sb("QT", [128, NH * SO], BF16)
        KTF = sb("KT", [128, NH * S], BF16)
        VAF = sb("VA", [128, 32 * NH * 129], BF16)
        QT = QTF[:].rearrange("p (h s) -> p h s", h=NH)
        KT = KTF[:].rearrange("p (h s) -> p h s", h=NH)
        VA = VAF[:].rearrange("p (a b c) -> p a b c", a=32, b=NH)
        SGAF = sb("SGA", [128, NH * SO], BF16)
        SGA = SGAF[:].rearrange("p (h s) -> p h s", h=NH)
        MIXC = sb("MIXC", [128, 4, SO], BF16)
        ident_f = sb("ident_f", [128, 128], F32)
        ident_b = sb("ident_b", [128, 128], BF16)
        ones_b = sb("ones_b", [128, 128], BF16)
        gmix = sb("gmix_s", [128, 8], F32)
        gple = sb("gple_s", [128, 8], F32)
        subln = sb("subln_s", [128, 1], F32)
        sublns = sb("sublns", [128, 1], F32)
        convw = sb("convw_s", [128, 12], F32)
        lamv = sb("lamv_s", [128, 256], F32)
        lamt = sb("lamt", [128, 8], F32)
        neglam = sb("neglam", [128, 1], F32)
        mhalf = sb("mhalf", [128, 1], F32)
        ss_all = sb("ss_all", [128, 48], F32)
        v_all = sb("v_all", [128, 48], F32)
        rstd_all = sb("rstd_all", [128, 48], F32)
        mx_all = sb("mx_all", [128, 48], F32)
        negM = sb("negM", [128, 1], F32)
        uh = sb("uh", [128, 8], F32)
        uhalo = sb("uhalo", [128, 4], F32)

        T_unit = sb("T_unit", [128, 896], F32)
        iotaA = sb("iotaA", [128, 13], F32)
        iotaB = sb("iotaB", [128, 32], F32)
        iotaF = sb("iotaF", [128, 8], F32)
        biasA = sb("biasA", [128, NH, 13], F32)
        biasB = sb("biasB", [128, NH, 32], F32)
        fAB = sb("fAB", [128, NH, 8], F32)
        rs = [sb("rs%d" % i, [128, 8], F32) for i in range(2)]
        c2 = [sb("c2_%d" % i, [128, 4], F32) for i in range(2)]
        ssq = [sb("ssq%d" % i, [128, 4], F32) for i in range(2)]
        vsq = [sb("vsq%d" % i, [128, 4], F32) for i in range(2)]
        rsd = [sb("rsd%d" % i, [128, 4], F32) for i in range(2)]
        mq = sb("mq", [128, 2], F32)
        mk2 = sb("mk2", [128, 2], F32)
        arena_bytes = (nc.sbuf_bytes_remaining - 1024) // 64 * 64
        ARENA = sb("ARENA", [128, arena_bytes // 2], BF16)
        print("arena bytes", arena_bytes)

        class Carver:
            def __init__(self, off=0):
                self.off = off

            def take(self, shape, dt):
                esz = 2 if dt == BF16 else 4
                n = int(np.prod(shape[1:]))
                nbytes = n * esz
                self.off = (self.off + 63) // 64 * 64
                assert self.off + nbytes <= arena_bytes, (self.off, nbytes, arena_bytes)
                a = ARENA[:, self.off // 2:(self.off + nbytes) // 2]
                self.off += nbytes
                if dt != BF16:
                    a = a.bitcast(dt)
                if len(shape) == 3:
                    a = a.rearrange("p (a b) -> p a b", a=shape[1])
                elif len(shape) == 4:
                    a = a.rearrange("p (a b c) -> p a b c", a=shape[1], b=shape[2])
                return a

        cslot = Slot(nc, es, "const")
        tkc = None
        for dst, src in [(gmix, gmix_d), (gple, gple_d), (subln, subln_d), (convw, convw_d), (lamv, lam_d)]:
            tkc = cslot.dma(SP, dst[:], src)
        POOL.need(POOL.mark(POOL.eng.memset(ident_f[:], 0.0)))
        POOL.eng.affine_select(out=ident_f[:], in_=ident_f[:], pattern=[[1, 128]], base=0,
                               channel_multiplier=-1, compare_op=ALU.not_equal, fill=1.0)
        POOL.eng.memset(ones_b[:], 1.0)
        POOL.eng.memset(mhalf[:], -0.5)
        POOL.eng.memset(mx_all[:], 0.0)
        tk_pc = POOL.mark(POOL.eng.memset(VA[:, :, :, 128:129], 1.0))
        DVE.need(tk_pc)
        tk_idb = DVE.mark(DVE.eng.tensor_copy(out=ident_b[:], in_=ident_f[:]))
        lprod = sb("lprod", [128, 128], F32)
        LC = {}

        def late_consts():
            DVE.need(tkc)
            DVE.eng.tensor_tensor(out=lprod[:, 0:64], in0=lamv[:, 0:64], in1=lamv[:, 64:128], op=ALU.mult)
            t0 = DVE.mark(DVE.eng.tensor_tensor(out=lprod[:, 64:128], in0=lamv[:, 128:192], in1=lamv[:, 192:256], op=ALU.mult))
            DVE.need(t0)
            t1 = DVE.mark(DVE.eng.tensor_reduce(out=lamt[:, 0:2], in_=lprod[:].rearrange("p (a b) -> p a b", a=2),
                                                axis=AX.X, op=ALU.add))
            ACT.need(t1)
            t2 = ACT.mark(ACT.eng.activation(out=lamt[:, 2:4], in_=lamt[:, 0:2], func=AF.Exp))
            DVE.need(t2)
            t3 = DVE.mark(DVE.eng.tensor_tensor(out=lamt[:, 4:5], in0=lamt[:, 3:4], in1=lamt[:, 2:3], op=ALU.subtract))
            DVE.need(t3)
            DVE.mark(DVE.eng.tensor_scalar(out=neglam[:], in0=lamt[:, 4:5], scalar1=-LAMBDA_INIT, scalar2=None, op0=ALU.add))
            tk_const = DVE.mark(DVE.eng.tensor_scalar(out=sublns[:], in0=subln[:], scalar1=(1.0 - LAMBDA_INIT) * 0.5,
                                                      scalar2=None, op0=ALU.mult))

            POOL.eng.iota(T_unit[:], pattern=[[1, 896]], base=-384, channel_multiplier=-1, allow_small_or_imprecise_dtypes=True)
            POOL.eng.iota(iotaA[:], pattern=[[128, 13]], base=0, channel_multiplier=-1, allow_small_or_imprecise_dtypes=True)
            POOL.eng.iota(iotaB[:], pattern=[[128, 32]], base=-511, channel_multiplier=1, allow_small_or_imprecise_dtypes=True)
            POOL.eng.iota(iotaF[:, 0:4], pattern=[[128, 4]], base=0, channel_multiplier=1, allow_small_or_imprecise_dtypes=True)
            t_io = POOL.mark(POOL.eng.iota(iotaF[:, 4:8], pattern=[[-128, 4]], base=511, channel_multiplier=-1,
                                           allow_small_or_imprecise_dtypes=True))
            ACT.need(t_io)
            ACT.mark(ACT.eng.activation(out=T_unit[:], in_=T_unit[:], func=AF.Abs))
            for h in range(NH):
                ACT.mark(ACT.eng.activation(out=fAB[:, h, :], in_=iotaF[:], func=AF.Exp, scale=-SLOPES[h]))

            LC["tk_const"] = tk_const
            LC["t_io"] = t_io

        cv = Carver()
        HT = cv.take([128, 8, SO], BF16)
        WB = [cv.take([128, 8, 512], BF16) for _ in range(3)]
        SQB = [cv.take([128, 512], BF16) for _ in range(2)]
        alias0 = cv.off
        NXT = 5
        XT = [cv.take([128, D], F32) for _ in range(NXT)]
        XN = [cv.take([128, D], BF16) for _ in range(2)]
        JUNK = cv.take([128, D], BF16)
        norm_end = cv.off
        cv2 = Carver(alias0)
        UF = cv2.take([128, 2050], F32)
        HS = cv2.take([128, 512], F32)
        TH = [cv2.take([128, 512], F32) for _ in range(2)]
        CA = [cv2.take([128, 512], F32) for _ in range(2)]
        print("phase1 arena used", max(cv.off, cv2.off))
        HTH = sb("HTH", [128, 8, 2], BF16)
        halo_t = sb("halo_t", [128, 4], F32)

        pes = contextlib.ExitStack()
        with pes:
            psT = [pes.enter_context(nc.psum_tensor("psT%d" % i, [128, D], BF16)) for i in range(2)]
            psP = [pes.enter_context(nc.psum_tensor("psP%d" % i, [128, 512], F32)) for i in range(4)]
            psM = pes.enter_context(nc.psum_tensor("psM", [128, 512], F32))
            psP_free = [None] * 4
            psP_i = [-1]

            def psp_next():
                psP_i[0] = (psP_i[0] + 1) % 4
                return psP_i[0]

            xslot = [Slot(nc, es, "x%d" % i) for i in range(NXT)]
            wslot = [Slot(nc, es, "w%d" % i) for i in range(3)]
            w_free = [None, None, None]
            w_ready = [None, None, None]
            psT_free = [None, None]
            psM_free = [None]
            sqb_free = [None, None]
            mx_idx = [0]
            state = {"xt_free": [None] * NXT, "xn_free": [None, None], "nt": 0, "xt_gen": [0] * NXT,
                     "last_sq": None, "last_xn": None, "last_tr": None}
            deferred = []

            def flush_deferred():
                while deferred:
                    deferred.pop(0)()

            def load_w(buf, blocks):
                POOL.need(w_free[buf])
                tk = None
                for (c0, ncol, d0) in blocks:
                    src_ = win_d.rearrange("(c p) n -> p c n", p=128)[:, :, c0:c0 + ncol]
                    tk = wslot[buf].dma(POOL, WB[buf][:, :, d0:d0 + ncol], src_)
                w_ready[buf] = tk

            class NormPipe:
                def __init__(self, rows, tok0_of, ht_free_of):
                    self.rows, self.tok0_of, self.ht_free_of, self.n = rows, tok0_of, ht_free_of, len(rows)
                    self.base = state["nt"]
                    state["nt"] += self.n
                    self.step = 0
                    self.tk = [dict() for _ in range(7)]
                    self.ev = {}

                def _emit_step(self, step):
                    tk_ld, tk_sq, tk_v, tk_r, tk_xn, tk_tr, _ = self.tk
                    rows, base = self.rows, self.base
                    i = step
                    if 0 <= i < self.n:
                        k = base + i
                        s = k % NXT
                        assert state["xt_gen"][s] == k // NXT, (state["xt_gen"], k)
                        SP.need(state["xt_free"][s])
                        tk_ld[i] = xslot[s].dma(SP, XT[s][:], x_d[rows[i] * 128:(rows[i] + 1) * 128, :])
                    i = step - 1
                    if 0 <= i < self.n:
                        k = base + i
                        s = k % NXT
                        ACT.need(tk_ld[i], state["last_sq"])
                        tk_sq[i] = ACT.mark(ACT.eng.activation(out=JUNK[:], in_=XT[s][:], func=AF.Square,
                                                               accum_out=ss_all[:, k:k + 1]))
                        state["last_sq"] = tk_sq[i]
                        DVE.need(tk_sq[i])
                        tk_v[i] = DVE.mark(DVE.eng.tensor_scalar(out=v_all[:, k:k + 1], in0=ss_all[:, k:k + 1],
                                                                 scalar1=1.0 / D, scalar2=EPS, op0=ALU.mult, op1=ALU.add))
                        POOL.need(tk_v[i])
                        tk_r[i] = POOL.mark(POOL.eng.tensor_tensor(out=rstd_all[:, k:k + 1], in0=v_all[:, k:k + 1],
                                                                   in1=mhalf[:], op=ALU.pow))
                    i = step - 2
                    if 0 <= i < self.n:
                        k = base + i
                        s = k % NXT
                        s2 = k % 2
                        DVE.need(tk_r[i], tk_ld[i], state["xn_free"][s2])
                        tk_xn[i] = DVE.mark(DVE.eng.tensor_scalar(out=XN[s2][:], in0=XT[s][:], scalar1=rstd_all[:, k:k + 1],
                                                                  scalar2=None, op0=ALU.mult))
                        state["last_xn"] = tk_xn[i]
                        state["xt_free"][s] = [tk_xn[i], tk_sq[i]]
                        state["xt_gen"][s] += 1
                    i = step - 3
                    if 0 <= i < self.n:
                        k = base + i
                        s2 = k % 2
                        PE.need(tk_xn[i], psT_free[s2], tk_idb)
                        ins = None
                        for c in range(8):
                            ins = PE.eng.transpose(out=psT[s2][:, c * 128:(c + 1) * 128], in_=XN[s2][:, c * 128:(c + 1) * 128],
                                                   identity=ident_b[:])
                        tk_tr[i] = PE.mark(ins)
                        state["last_tr"] = tk_tr[i]
                        state["xn_free"][s2] = tk_tr[i]
                    i = step - 4
                    if 0 <= i < self.n:
                        k = base + i
                        s2 = k % 2
                        t0_ = self.tok0_of(i)
                        DVE.need(tk_tr[i], self.ht_free_of(i), tkc)
                        self.ev[i] = DVE.mark(DVE.eng.tensor_tensor(
                            out=HT[:, :, t0_:t0_ + 128], in0=psT[s2][:].rearrange("p (c t) -> p c t", c=8),
                            in1=gmix[:].unsqueeze(2).to_broadcast([128, 8, 128]), op=ALU.mult))
                        psT_free[s2] = self.ev[i]

                def advance(self, upto):
                    upto = min(upto, self.n - 1)
                    while self.step <= upto + 4:
                        self._emit_step(self.step)
                        self.step += 1
                    return self.ev[upto]

            def proj_fm(buf, wcol0, tokc, ready_tk):
                s = psp_next()
                PE.need(psP_free[s], w_ready[buf], ready_tk)
                ins = None
                for c in range(8):
                    ins = PE.eng.matmul(psP[s][:], lhsT=WB[buf][:, c, wcol0:wcol0 + 128],
                                        rhs=HT[:, c, tokc * 512:(tokc + 1) * 512], start=(c == 0), stop=(c == 7))
                tk = PE.mark(ins)
                flush_deferred()
                return s, tk

            def norm_bound(s, tk_mm):
                idx = mx_idx[0]
                mx_idx[0] += 1
                b = idx % 2
                ACT.need(tk_mm, sqb_free[b])
                t_sq = ACT.mark(ACT.eng.activation(out=SQB[b][:], in_=psP[s][:], func=AF.Square))

                def part_b():
                    PE.need(t_sq, psM_free[0])
                    t_m = PE.mark(PE.eng.matmul(psM[:], lhsT=ones_b[:], rhs=SQB[b][:], start=True, stop=True))
                    sqb_free[b] = t_m
                    DVE.need(t_m)
                    t_r = DVE.mark(DVE.eng.tensor_reduce(out=mx_all[:, idx:idx + 1], in_=psM[:], axis=AX.X, op=ALU.max))
                    psM_free[0] = t_r

                deferred.append(part_b)
                return t_sq

            def k_chunk(buf, h, tokc, tokbase, ready_tk):
                s, tk = proj_fm(buf, h * 128, tokc, ready_tk)
                ACT.need(tk)
                t_cp = ACT.mark(ACT.eng.activation(out=KT[:, h, tokbase + tokc * 512: tokbase + (tokc + 1) * 512], in_=psP[s][:],
                                                   func=AF.Copy))
                t_sq = norm_bound(s, tk)
                psP_free[s] = [t_cp, t_sq]

            def q_chunk(buf, h, tokc, ready_tk):
                s, tk = proj_fm(buf, h * 128, tokc, ready_tk)
                ACT.need(tk)
                t_cp = ACT.mark(ACT.eng.activation(out=QT[:, h, tokc * 512:(tokc + 1) * 512], in_=psP[s][:],
                                                   func=AF.Copy, scale=0.125))
                t_sq = norm_bound(s, tk)
                psP_free[s] = [t_cp, t_sq]

            def v_tile(buf, tl, rglob, ready_tk):
                s = psp_next()
                PE.need(psP_free[s], w_ready[buf], ready_tk)
                ins = None
                for c in range(8):
                    ins = PE.eng.matmul(psP[s][:], lhsT=HT[:, c, tl * 128:(tl + 1) * 128], rhs=WB[buf][:, c, :],
                                        start=(c == 0), stop=(c == 7))
                tk = PE.mark(ins)
                flush_deferred()
                ACT.need(tk, tk_pc)
                psP_free[s] = ACT.mark(ACT.eng.activation(out=VA[:, rglob, :, 0:128],
                                                          in_=psP[s][:].rearrange("p (h d) -> p h d", h=NH), func=AF.Copy))

            load_w(0, [(512, 512, 0)])
            load_w(1, [(1024, 512, 0)])
            ht_chunk_free = [None] * 4
            np_a = NormPipe(list(range(16, 32)), lambda i: i * 128, lambda i: None)
            np_b = NormPipe(list(range(0, 16)), lambda i: i * 128, lambda i: ht_chunk_free[i // 4])
            ev0 = np_a.advance(0)
            DVE.need(ev0)
            tk_hth = DVE.mark(DVE.eng.tensor_copy(out=HTH[:], in_=HT[:, :, 0:2]))
            LAG = 4
            for s in range(32 + LAG):
                if s < 16:
                    np_a.advance(s)
                elif s < 32:
                    np_b.advance(s - 16)
                if s == 9:
                    late_consts()
                if s == 8:
                    load_w(2, [(0, 512, 0)])
                m = s - LAG
                if 0 <= m < 16:
                    j, r = m // 4, m % 4
                    rdy = np_a.advance(4 * j + 3)
                    k_chunk(0, r, j, SO, rdy)
                    v_tile(1, 4 * j + r, 16 + 4 * j + r, rdy)
                    if r == 3:
                        ht_chunk_free[j] = PE.last()
                elif 16 <= m < 32:
                    j, r = (m - 16) // 4, (m - 16) % 4
                    rdy = np_b.advance(4 * j + 3)
                    q_chunk(2, r, j, rdy)
            flush_deferred()
            rdy_all = np_b.advance(15)
            norm_done = [state["last_sq"], state["last_xn"], state["last_tr"]]
            w_free[2] = PE.last()
            load_w(2, [(1536, 512, 0)])
            for tokc in range(4):
                for h in range(NH):
                    k_chunk(0, h, tokc, 0, rdy_all)
            flush_deferred()
            w_free[0] = PE.last()
            DVE.need(DVE.last())
            nq = 16
            nk = 32
            t_a = DVE.mark(DVE.eng.tensor_reduce(out=mq[:, 0:1], in_=mx_all[:, 16:32], axis=AX.X, op=ALU.max))
            DVE.need(t_a)
            DVE.eng.tensor_reduce(out=mk2[:, 0:1], in_=mx_all[:, 0:16], axis=AX.X, op=ALU.max)
            t_b = DVE.mark(DVE.eng.tensor_reduce(out=mk2[:, 1:2], in_=mx_all[:, 32:48], axis=AX.X, op=ALU.max))
            DVE.need(t_b)
            t_c = DVE.mark(DVE.eng.tensor_tensor(out=mq[:, 1:2], in0=mk2[:, 0:1], in1=mk2[:, 1:2], op=ALU.max))
            DVE.need(t_c)
            t_d = DVE.mark(DVE.eng.tensor_tensor(out=mk2[:, 0:1], in0=mq[:, 0:1], in1=mq[:, 1:2], op=ALU.add))
            DVE.need(t_d)
            tk_negM = DVE.mark(DVE.eng.tensor_scalar(out=negM[:], in0=mk2[:, 0:1], scalar1=-1.0 / 16.0, scalar2=None, op0=ALU.mult))

            DVE.need(tk_negM, LC["t_io"])
            for h in range(NH):
                DVE.eng.tensor_scalar(out=biasA[:, h, :], in0=iotaA[:], scalar1=-SLOPES[h], scalar2=negM[:], op0=ALU.mult, op1=ALU.add)
                DVE.mark(DVE.eng.tensor_scalar(out=biasB[:, h, :], in0=iotaB[:], scalar1=-SLOPES[h], scalar2=negM[:], op0=ALU.mult, op1=ALU.add))


            def conv_blocks(cc):
                return [(2048 + cc * 128, 128, 0), (2560 + cc * 128, 128, 128), (3072 + cc * 128, 128, 256),
                        (3584 + cc * 128, 128, 384)]

            load_w(0, conv_blocks(0))
            for tl in range(16):
                v_tile(1, tl, tl, rdy_all)
            w_free[1] = PE.last()
            load_w(1, conv_blocks(1))
            th_free = [norm_done, norm_done]
            thi = [0]
            for tokc in range(4):
                for h in range(NH):
                    s, tk = proj_fm(2, h * 128, tokc, rdy_all)
                    b = thi[0] % 2
                    thi[0] += 1
                    ACT.need(tk, th_free[b])
                    t_th = ACT.mark(ACT.eng.activation(out=TH[b][:], in_=psP[s][:], func=AF.Tanh, scale=0.5))
                    DVE.need(t_th)
                    t_sg = DVE.mark(DVE.eng.scalar_tensor_tensor(out=SGA[:, h, tokc * 512:(tokc + 1) * 512], in0=TH[b][:],
                                                                 scalar=1.0, in1=psP[s][:], op0=ALU.add, op1=ALU.mult))
                    th_free[b] = t_sg
                    psP_free[s] = t_sg
            w_free[2] = PE.last()
            load_w(2, conv_blocks(2))
            POOL.need(norm_done)
            tk_pad = POOL.mark(POOL.eng.memset(UF[:, 0:1], 0.0))
            uf_free = norm_done
            ca_free = [norm_done, norm_done]
            hs_free = norm_done
            conv_buf = [0, 1, 2, 0]
            for cc in range(4):
                buf = conv_buf[cc]
                PE.need(psM_free[0], w_ready[buf], tk_hth)
                ins = None
                for g_ in range(2):
                    for c in range(8):
                        ins = PE.eng.matmul(psM[:, g_ * 2:g_ * 2 + 2], lhsT=WB[buf][:, c, 128 + g_ * 128:256 + g_ * 128],
                                            rhs=HTH[:, c, :], start=(c == 0), stop=(c == 7))
                tk_h = PE.mark(ins)
                DVE.need(tk_h)
                t_ht = DVE.mark(DVE.eng.tensor_copy(out=halo_t[:], in_=psM[:, 0:4]))
                psM_free[0] = t_ht
                DVE.need(t_ht, uf_free)
                tk_hl = DVE.mark(DVE.eng.tensor_tensor(out=UF[:, 2049:2050], in0=halo_t[:, 0:1], in1=halo_t[:, 2:3], op=ALU.mult))
                tk_u = []
                for tokc in range(4):
                    sC, tkC = proj_fm(buf, 128, tokc, rdy_all)
                    sH, tkH = proj_fm(buf, 256, tokc, rdy_all)
                    ACT.need(tkH, hs_free)
                    t_hs = ACT.mark(ACT.eng.activation(out=HS[:], in_=psP[sH][:], func=AF.Copy))
                    psP_free[sH] = t_hs
                    DVE.need(t_hs, tkC, uf_free)
                    t_u = DVE.mark(DVE.eng.tensor_tensor(out=UF[:, 1 + tokc * 512: 1 + (tokc + 1) * 512], in0=psP[sC][:],
                                                         in1=HS[:], op=ALU.mult))
                    hs_free = t_u
                    psP_free[sC] = t_u
                    tk_u.append(t_u)
                last_readers = []
                for tokc in range(4):
                    sB, tkB = proj_fm(buf, 0, tokc, rdy_all)
                    sG, tkG = proj_fm(buf, 384, tokc, rdy_all)
                    b = thi[0] % 2
                    thi[0] += 1
                    ACT.need(tkG, th_free[b])
                    t_th = ACT.mark(ACT.eng.activation(out=TH[b][:], in_=psP[sG][:], func=AF.Tanh, scale=0.5))
                    DVE.need(t_th)
                    t_sg = DVE.mark(DVE.eng.scalar_tensor_tensor(out=TH[b][:], in0=TH[b][:], scalar=1.0, in1=psP[sG][:],
                                                                 op0=ALU.add, op1=ALU.mult))
                    psP_free[sG] = t_sg
                    DVE.need(t_sg, tkB)
                    t_sgb = DVE.mark(DVE.eng.scalar_tensor_tensor(out=TH[b][:], in0=TH[b][:], scalar=0.5, in1=psP[sB][:],
                                                                  op0=ALU.mult, op1=ALU.mult))
                    psP_free[sB] = t_sgb
                    a = thi[0] % 2
                    c0 = 1 + tokc * 512
                    DVE.need(tk_u[min(tokc + 1, 3)], tk_pad, tk_hl, ca_free[a], tkc)
                    t_a = DVE.mark(DVE.eng.tensor_scalar(out=CA[a][:], in0=UF[:, c0 - 1:c0 + 511],
                                                         scalar1=convw[:, cc * 3:cc * 3 + 1], scalar2=None, op0=ALU.mult))
                    DVE.need(t_a)
                    t_a = DVE.mark(DVE.eng.scalar_tensor_tensor(out=CA[a][:], in0=UF[:, c0:c0 + 512],
                                                                scalar=convw[:, cc * 3 + 1:cc * 3 + 2], in1=CA[a][:],
                                                                op0=ALU.mult, op1=ALU.add))
                    DVE.need(t_a)
                    t_a = DVE.mark(DVE.eng.scalar_tensor_tensor(out=CA[a][:], in0=UF[:, c0 + 1:c0 + 513],
                                                                scalar=convw[:, cc * 3 + 2:cc * 3 + 3], in1=CA[a][:],
                                                                op0=ALU.mult, op1=ALU.add))
                    DVE.need(t_a, t_sgb)
                    t_o = DVE.mark(DVE.eng.tensor_tensor(out=MIXC[:, cc, tokc * 512:(tokc + 1) * 512], in0=CA[a][:],
                                                         in1=TH[b][:], op=ALU.mult))
                    ca_free[a] = t_o
                    th_free[b] = t_o
                    last_readers = [t_o]
                uf_free = last_readers
                w_free[buf] = PE.last()
                if cc == 0:
                    load_w(0, conv_blocks(3))

            P1END = {"psP": list(psP_free), "psM": psM_free[0], "pe": PE.last(), "act": ACT.last(), "dve": DVE.last(),
                     "pool": POOL.last()}

        cv = Carver()
        MIXA = cv.take([128, NH, SO], BF16)
        ET = [cv.take([128, 1024], BF16) for _ in range(3)]
        DG = [cv.take([128, 1024], F32) for _ in range(2)]
        OACC = [cv.take([128, 8, 129], F32) for _ in range(2)]
        OT = [cv.take([128, 4, 128], F32) for _ in range(2)]
        ON = [cv.take([128, 4, 128], F32) for _ in range(2)]
        STG = [cv.take([128, 8, 129], F32) for _ in range(1)]
        WOUT = cv.take([128, 8, D], BF16)
        WG = cv.take([128, 8, D], BF16)
        WP = cv.take([128, 2, D], BF16)
        print("phase2 arena used", cv.off)

        class RawCarver:
            def __init__(self, flat, nbytes):
                self.flat, self.nbytes, self.off = flat, nbytes, 0

            def take(self, shape, dt):
                esz = 2 if dt == BF16 else 4
                n = int(np.prod(shape[1:])) * esz
                self.off = (self.off + 63) // 64 * 64
                assert self.off + n <= self.nbytes, (self.off, n, self.nbytes)
                a = self.flat[:, self.off // 2:(self.off + n) // 2]
                self.off += n
                if dt != BF16:
                    a = a.bitcast(dt)
                if len(shape) == 3:
                    a = a.rearrange("p (a b) -> p a b", a=shape[1])
                return a

        rc1 = RawCarver(KTF[:], NH * S * 2)
        rc2 = RawCarver(VAF[:], 32 * NH * 129 * 2)
        rc3 = RawCarver(QTF[:], NH * SO * 2)
        rc4 = RawCarver(SGAF[:], NH * SO * 2)
        PF32 = rc1.take([128, 16, 256], F32)
        XT3 = [rc1.take([128, D], F32) for _ in range(2)]
        X1 = [rc1.take([128, D], F32) for _ in range(2)]
        TH3 = [rc2.take([128, D], F32) for _ in range(2)]
        X2 = [rc2.take([128, D], F32) for _ in range(2)]
        YO = [rc2.take([128, D], F32) for _ in range(2)]
        PTALL = rc3.take([128, 2, SO], BF16)
        X1N = [rc3.take([128, D], BF16) for _ in range(2)]
        X1NT = [rc3.take([128, 8, 128], BF16) for _ in range(2)]
        GFIN = rc4.take([128, D], F32)
        GPLEB = rc4.take([128, D], F32)
        JUNK3 = rc4.take([128, D], BF16)
        ss3 = sb("ss3", [128, 16], F32)
        v3 = sb("v3", [128, 16], F32)
        r3 = sb("r3", [128, 16], F32)
        ss4 = sb("ss4", [128, 16], F32)
        v4 = sb("v4", [128, 16], F32)
        r4 = sb("r4", [128, 16], F32)

        NTILE = SO // 128
        x3slot = [Slot(nc, es, "x3_%d" % i) for i in range(2)]
        oslot = [Slot(nc, es, "o3_%d" % i) for i in range(2)]
        gslot = Slot(nc, es, "gfin")
        ppslot = [Slot(nc, es, "pld%d" % g) for g in range(4)]
        tk_pld = []
        P3 = {}

        def prefetch_phase3():
            SP.need(PE.last(), DVE.last())
            gslot.dma(SP, GFIN[:], gfin_d)
            P3["gfin"] = gslot.dma(SP, GPLEB[:], gpleb_d)
            for t_ in range(2):
                P3["ldx", t_] = x3slot[t_].dma(SP, XT3[t_][:], x_d[t_ * 128:(t_ + 1) * 128, :])
            for g in range(4):
                tk_pld.append(ppslot[g].dma(SP, PF32[:, 4 * g:4 * g + 4, :],
                                         p_d.rearrange("(t p) c -> p t c", p=128)[:, 4 * g:4 * g + 4, :]))


        wt_slot = Slot(nc, es, "wtail")
        POOL.need(P1END["pe"], P1END["act"], P1END["dve"])
        for half_ in range(2):
            wt_slot.dma(POOL, WOUT[:, :, half_ * 512:(half_ + 1) * 512],
                        wout_d.rearrange("(c p) n -> p c n", p=128)[:, :, half_ * 512:(half_ + 1) * 512])
        for half_ in range(2):
            wt_slot.dma(POOL, WG[:, :, half_ * 512:(half_ + 1) * 512],
                        wg_d.rearrange("(c p) n -> p c n", p=128)[:, :, half_ * 512:(half_ + 1) * 512])
        for half_ in range(2):
            tk_wtail = wt_slot.dma(POOL, WP[:, :, half_ * 512:(half_ + 1) * 512],
                                   wp_d.rearrange("(c p) n -> p c n", p=128)[:, :, half_ * 512:(half_ + 1) * 512])

        pes = contextlib.ExitStack()
        with pes:
            psS = [es.enter_context(nc.psum_tensor("psS%d" % i, [128, 1024], F32)) for i in range(2)]
            psBig = es.enter_context(nc.psum_tensor("psBig", [128, 2048], F32))
            psA = [psBig[:, b_ * 512:(b_ + 1) * 512] for b_ in range(3)]
            psO = psBig[:, 1536:2048]

            def acc_region(r):
                return psA[r // 3][:, (r % 3) * 129:(r % 3) * 129 + 129]

            tiles = []
            for h in range(NH):
                for qc in range(4):
                    unit = h * 4 + qc
                    groups = []
                    groups.append(("B", list(range(4 * qc + 4, 32))))
                    if qc > 0:
                        groups.append(("A", list(range(0, 4 * qc))))
                    groups.append(("C", list(range(4 * qc, 4 * qc + 4))))
                    def dead(g, kt, h=h, qc=qc):
                        dmin = 128 * (4 * qc - kt) - 127 if g == "A" else 128 * (kt - 4 * qc) - 511
                        return g != "C" and SLOPES[h] * dmin >= 110.0
                    groups = [(g, [kt for kt in kts if not dead(g, kt)]) for g, kts in groups]
                    groups = [(g, kts) for g, kts in groups if kts]
                    for gi, (g, kts) in enumerate(groups):
                        for ki, kt in enumerate(kts):
                            tiles.append(dict(h=h, qc=qc, unit=unit, g=g, kt=kt, first=(ki == 0), last=(ki == len(kts) - 1),
                                              gfirst=(gi == 0), unit_last=(gi == len(groups) - 1 and ki == len(kts) - 1)))
            NT = len(tiles)
            psS_free = [None, [P1END["psP"][0], P1END["psP"][1]]]
            et_free = [None, None, None]
            dg_free = [None, None]
            dgi = [0]
            tk_E = {}
            acc_free = [None] * 8
            for r_ in range(8):
                acc_free[r_] = [P1END["psP"][2], P1END["psP"][3], P1END["psM"]][r_ // 3]
            oacc_free = [None, None]
            ot_free = [None, None]
            on_free = [None, None]
            pso_free = [None]
            pending = []
            stg_free = [None, None]
            stg_i = [0]
            unit_evac = {}

            def emit_S(i):
                t = tiles[i]
                h, qc, kt = t["h"], t["qc"], t["kt"]
                b = i % 2
                PE.need(psS_free[b])
                PE.eng.matmul(psS[b][:, 0:512], lhsT=KT[0:64, h, kt * 128:(kt + 1) * 128], rhs=QT[0:64, h, qc * 512:(qc + 1) * 512],
                              start=True, stop=True)
                tk_s = PE.mark(PE.eng.matmul(psS[b][:, 512:1024], lhsT=KT[64:128, h, kt * 128:(kt + 1) * 128],
                                             rhs=QT[64:128, h, qc * 512:(qc + 1) * 512], start=True, stop=True))
                e = i % 3
                if t["g"] == "C":
                    g_ = dgi[0] % 2
                    dgi[0] += 1
                    off = 384 - 128 * (kt - 4 * qc)
                    DVE.need(tk_s, dg_free[g_])
                    tk_d = DVE.mark(DVE.eng.scalar_tensor_tensor(
                        out=DG[g_][:].rearrange("p (j q) -> p j q", j=2),
                        in0=T_unit[:, off:off + 512].unsqueeze(1).to_broadcast([128, 2, 512]), scalar=-SLOPES[h],
                        in1=psS[b][:].rearrange("p (j q) -> p j q", j=2), op0=ALU.mult, op1=ALU.add))
                    psS_free[b] = tk_d
                    ACT.need(tk_d, et_free[e])
                    tk_E[i] = ACT.mark(ACT.eng.activation(out=ET[e][:], in_=DG[g_][:], func=AF.Exp, bias=negM[:], scale=1.0))
                    dg_free[g_] = tk_E[i]
                else:
                    if t["g"] == "A":
                        bias = biasA[:, h, 4 * qc - kt:4 * qc - kt + 1]
                    else:
                        bias = biasB[:, h, kt - 4 * qc:kt - 4 * qc + 1]
                    ACT.need(tk_s, et_free[e])
                    tk_E[i] = ACT.mark(ACT.eng.activation(out=ET[e][:], in_=psS[b][:], func=AF.Exp, bias=bias, scale=1.0))
                    psS_free[b] = tk_E[i]

            def finalize_part1(unit, h, qc, tks):
                u = unit % 2
                DVE.need(tks, ot_free[u])
                t_ = DVE.mark(DVE.eng.reciprocal(out=rs[u][:], in_=OACC[u][:, :, 128]))
                DVE.need(t_, LC["tk_const"])
                t_c2 = DVE.mark(DVE.eng.tensor_scalar(out=c2[u][:], in0=rs[u][:, 4:8], scalar1=neglam[:], scalar2=None, op0=ALU.mult))
                DVE.need(t_c2)
                for sb_ in range(4):
                    t_ = DVE.mark(DVE.eng.tensor_scalar(out=OT[u][:, sb_, :], in0=OACC[u][:, 4 + sb_, 0:128],
                                                        scalar1=c2[u][:, sb_:sb_ + 1], scalar2=None, op0=ALU.mult))
                DVE.need(t_)
                for sb_ in range(4):
                    t_ = DVE.mark(DVE.eng.scalar_tensor_tensor(out=OT[u][:, sb_, :], in0=OACC[u][:, sb_, 0:128],
                                                               scalar=rs[u][:, sb_:sb_ + 1], in1=OT[u][:, sb_, :],
                                                               op0=ALU.mult, op1=ALU.add))
                oacc_free[u] = t_
                DVE.need(t_)
                DVE.need(on_free[u])
                t_ = DVE.mark(DVE.eng.tensor_tensor(out=ON[u][:], in0=OT[u][:], in1=OT[u][:], op=ALU.mult))
                DVE.need(t_)
                t_ = DVE.mark(DVE.eng.tensor_reduce(out=ssq[u][:], in_=ON[u][:], axis=AX.X, op=ALU.add))
                DVE.need(t_)
                POOL.need(t_)
                t_v = POOL.mark(POOL.eng.tensor_scalar(out=vsq[u][:], in0=ssq[u][:], scalar1=1.0 / 128.0, scalar2=EPS,
                                                       op0=ALU.mult, op1=ALU.add))
                POOL.need(t_v)
                t_p = POOL.mark(POOL.eng.tensor_tensor(out=rsd[u][:], in0=vsq[u][:], in1=mhalf[:].to_broadcast([128, 4]), op=ALU.pow))
                POOL.need(t_p, on_free[u])
                t_on = POOL.mark(POOL.eng.tensor_tensor(out=ON[u][:], in0=OT[u][:],
                                                        in1=rsd[u][:].unsqueeze(2).to_broadcast([128, 4, 128]), op=ALU.mult))
                ot_free[u] = t_on

                def part2():
                    PE.need(t_on, pso_free[0])
                    ins = None
                    for sb_ in range(4):
                        ins = PE.eng.transpose(out=psO[:, sb_ * 128:(sb_ + 1) * 128], in_=ON[u][:, sb_, :], identity=ident_f[:])
                    t_tr = PE.mark(ins)
                    on_free[u] = t_tr
                    DVE.need(t_tr)
                    t_m = DVE.mark(DVE.eng.scalar_tensor_tensor(out=MIXA[:, h, qc * 512:(qc + 1) * 512], in0=psO[:], scalar=sublns[:],
                                                                in1=SGA[:, h, qc * 512:(qc + 1) * 512], op0=ALU.mult, op1=ALU.mult))
                    pso_free[0] = t_m

                pending.append([12, part2])

            def emit_PV(i):
                t = tiles[i]
                h, qc, kt, unit = t["h"], t["qc"], t["kt"], t["unit"]
                e = i % 3
                u = unit % 2
                PE.need(tk_E[i])
                ins = None
                tk_bank = []
                for r in range(8):
                    j, sb_ = r // 4, r % 4
                    if t["first"] and r % 3 == 0:
                        PE.need([acc_free[rr] for rr in range(r, min(r + 3, 8))])
                    ins = PE.eng.matmul(acc_region(r), lhsT=ET[e][:, j * 512 + sb_ * 128: j * 512 + (sb_ + 1) * 128],
                                        rhs=VA[:, kt, h, :], start=(t["first"] and r % 3 == 0), stop=t["last"],
                                        skip_group_check=True)
                    if t["last"] and r in (2, 5):
                        tk_bank.append(PE.mark(ins))
                tk_pv = PE.mark(ins)
                tk_bank.append(tk_pv)
                et_free[e] = tk_pv
                if t["last"]:
                    tks = [None] * 8
                    if t["g"] == "C":
                        DVE.need(unit_evac[unit])
                        for bk in range(3):
                            nreg = 3 if bk < 2 else 2
                            DVE.need(tk_bank[bk])
                            tk_ = DVE.mark(DVE.eng.tensor_tensor(
                                out=OACC[u][:, 3 * bk:3 * bk + nreg, :],
                                in0=psA[bk][:, 0:nreg * 129].rearrange("p (r d) -> p r d", r=nreg),
                                in1=OACC[u][:, 3 * bk:3 * bk + nreg, :], op=ALU.add))
                            for r in range(3 * bk, 3 * bk + nreg):
                                tks[r] = tk_
                    else:
                        fo = 0 if t["g"] == "A" else 4
                        sg_ = 0
                        stg_i[0] += 1
                        DVE.need(stg_free[sg_])
                        cps = []
                        for bk in range(3):
                            nreg = 3 if bk < 2 else 2
                            DVE.need(tk_bank[bk])
                            tk_ = DVE.mark(DVE.eng.tensor_copy(
                                out=STG[sg_][:, 3 * bk:3 * bk + nreg, :],
                                in_=psA[bk][:, 0:nreg * 129].rearrange("p (r d) -> p r d", r=nreg)))
                            cps.append(tk_)
                            for r in range(3 * bk, 3 * bk + nreg):
                                tks[r] = tk_
                        gfirst_ = t["gfirst"]
                        prev_ev = None if gfirst_ else unit_evac[unit]
                        ev_ = []
                        unit_evac[unit] = ev_
                        stg_free[sg_] = ev_

                        def scaled_acc(gfirst_=gfirst_, prev_ev=prev_ev, ev_=ev_, cps=cps, u=u, h=h, fo=fo, sg_=sg_):
                            if gfirst_:
                                DVE.need(cps, oacc_free[u])
                            else:
                                DVE.need(cps, prev_ev)
                            for r in range(8):
                                j, sb_ = r // 4, r % 4
                                if gfirst_:
                                    ev_.append(DVE.mark(DVE.eng.tensor_scalar(
                                        out=OACC[u][:, r, :], in0=STG[sg_][:, r, :], scalar1=fAB[:, h, fo + sb_:fo + sb_ + 1],
                                        scalar2=None, op0=ALU.mult)))
                                else:
                                    ev_.append(DVE.mark(DVE.eng.scalar_tensor_tensor(
                                        out=OACC[u][:, r, :], in0=STG[sg_][:, r, :], scalar=fAB[:, h, fo + sb_:fo + sb_ + 1],
                                        in1=OACC[u][:, r, :], op0=ALU.mult, op1=ALU.add)))

                        pending.append([3, scaled_acc])
                    for r in range(8):
                        acc_free[r] = tks[r]
                    if t["g"] == "C":
                        unit_evac[unit] = tks
                    if t["unit_last"]:
                        finalize_part1(unit, h, qc, unit_evac[unit])

            first_h3 = min(k for k, t_ in enumerate(tiles) if t_["h"] == NH - 1)
            for i in range(NT + 2):
                if i < NT:
                    emit_S(i)
                if i == first_h3 + 16:
                    prefetch_phase3()
                if i >= 2:
                    emit_PV(i - 2)
                for pnd in list(pending):
                    pnd[0] -= 1
                    if pnd[0] <= 0:
                        pending.remove(pnd)
                        pnd[1]()
            EY = {}
            tk_lastS = PE.last()
            PE.need(psS_free[0], tk_wtail)
            ins = None
            for half_ in range(2):
                for c in range(8):
                    lhs = MIXA[:, c, 0:128] if c < 4 else MIXC[:, c - 4, 0:128]
                    ins = PE.eng.matmul(psS[0][:, half_ * 512:(half_ + 1) * 512], lhsT=lhs, rhs=WOUT[:, c, half_ * 512:(half_ + 1) * 512],
                                        start=(c == 0), stop=(c == 7))
            EY["y0"] = PE.mark(ins)
            EY["ptall"] = []
            tk_prev = psS_free[1]
            ACT.need(tk_lastS)
            for g in range(4):
                PE.need(tk_pld[g], tk_prev)
                for c2_ in range(2):
                    for tt in range(4):
                        ins = PE.eng.transpose(out=psS[1][:, (c2_ * 4 + tt) * 128:(c2_ * 4 + tt + 1) * 128],
                                               in_=PF32[:, 4 * g + tt, c2_ * 128:(c2_ + 1) * 128], identity=ident_f[:])
                t_tr = PE.mark(ins)
                ACT.need(t_tr)
                tk_prev = ACT.mark(ACT.eng.activation(out=PTALL[:, :, 4 * g * 128:(4 * g + 4) * 128],
                                                      in_=psS[1][:].rearrange("p (c t) -> p c t", c=2), func=AF.Copy))
                EY["ptall"].append(tk_prev)
            for pnd in list(pending):
                pnd[1]()
            pending.clear()
            barrier()

        pes = contextlib.ExitStack()
        with pes:
            psY, psG = psS[0], psS[1]
            psPP = psBig[:, 0:1024]
            psT3 = psBig[:, 1024:1536].bitcast(BF16)
            T = {}
            fr = {"xt3": [None, None], "x1": [None, None], "x1n": [None, None], "x1nt": [None, None],
                  "th3": [None, None], "x2": [None, None], "yo": [None, None],
                  "psY": None, "psG": None, "psPP": None, "psT3": None}
            stores = []

            def st_load(t):
                s = t % 2
                SP.need(fr["xt3"][s])
                T["ldx", t] = x3slot[s].dma(SP, XT3[s][:], x_d[t * 128:(t + 1) * 128, :])

            def st_y(t):
                s = t % 2
                if t == 0:
                    T["y", t] = EY["y0"]
                else:
                    PE.need(fr["psY"], tk_wtail)
                    ins = None
                    for half_ in range(2):
                        for c in range(8):
                            lhs = MIXA[:, c, t * 128:(t + 1) * 128] if c < 4 else MIXC[:, c - 4, t * 128:(t + 1) * 128]
                            ins = PE.eng.matmul(psY[:, half_ * 512:(half_ + 1) * 512], lhsT=lhs, rhs=WOUT[:, c, half_ * 512:(half_ + 1) * 512],
                                                start=(c == 0), stop=(c == 7))
                    T["y", t] = PE.mark(ins)
                DVE.need(T["y", t], T["ldx", t], fr["x1"][s])
                T["x1", t] = DVE.mark(DVE.eng.tensor_tensor(out=X1[s][:], in0=psY[:], in1=XT3[s][:], op=ALU.add))
                fr["psY"] = T["x1", t]
                fr["xt3"][s] = T["x1", t]

            def st_sq3(t):
                s = t % 2
                ACT.need(T["x1", t], fr.get("junk"))
                T["sq3", t] = fr["junk"] = ACT.mark(ACT.eng.activation(out=JUNK3[:], in_=X1[s][:], func=AF.Square, accum_out=ss3[:, t:t + 1]))
                DVE.need(T["sq3", t])
                T["v3", t] = DVE.mark(DVE.eng.tensor_scalar(out=v3[:, t:t + 1], in0=ss3[:, t:t + 1], scalar1=1.0 / D, scalar2=EPS,
                                                            op0=ALU.mult, op1=ALU.add))
                POOL.need(T["v3", t])
                T["r3", t] = POOL.mark(POOL.eng.tensor_tensor(out=r3[:, t:t + 1], in0=v3[:, t:t + 1], in1=mhalf[:], op=ALU.pow))

            def st_x1n(t):
                s = t % 2
                DVE.need(T["r3", t], fr["x1n"][s], P3["gfin"])
                T["x1n", t] = DVE.mark(DVE.eng.scalar_tensor_tensor(out=X1N[s][:], in0=X1[s][:], scalar=r3[:, t:t + 1], in1=GPLEB[:],
                                                                    op0=ALU.mult, op1=ALU.mult))

            def st_c1(t):
                s = t % 2
                PE.need(T["x1n", t], fr["psT3"])
                ins = None
                for c in range(8):
                    ins = PE.eng.transpose(out=psT3[:, c * 128:(c + 1) * 128], in_=X1N[s][:, c * 128:(c + 1) * 128], identity=ident_b[:])
                T["tr3", t] = PE.mark(ins)
                fr["x1n"][s] = T["tr3", t]
                ACT.need(T["tr3", t], fr["x1nt"][s])
                T["x1nt_a", t] = ACT.mark(ACT.eng.activation(out=X1NT[s][:, 0:4, :].rearrange("p c t -> p (c t)"),
                                                             in_=psT3[:, 0:512], func=AF.Copy))
                T["x1nt", t] = ACT.mark(ACT.eng.activation(out=X1NT[s][:, 4:8, :].rearrange("p c t -> p (c t)"),
                                                           in_=psT3[:, 512:1024], func=AF.Copy))
                fr["psT3"] = T["x1nt", t]
                PE.need(T["ptall"], fr["psPP"])
                for half_ in range(2):
                    for c2_ in range(2):
                        ins = PE.eng.matmul(psPP[:, half_ * 512:(half_ + 1) * 512], lhsT=PTALL[:, c2_, t * 128:(t + 1) * 128],
                                            rhs=WP[:, c2_, half_ * 512:(half_ + 1) * 512], start=(c2_ == 0), stop=(c2_ == 1))
                T["pp", t] = PE.mark(ins)
                PE.need(T["x1nt_a", t], fr["psG"])
                for cg in range(2):
                    if cg == 1:
                        PE.need(T["x1nt", t])
                    for half_ in range(2):
                        for c in range(4 * cg, 4 * cg + 4):
                            ins = PE.eng.matmul(psG[:, half_ * 512:(half_ + 1) * 512], lhsT=X1NT[s][:, c, :],
                                                rhs=WG[:, c, half_ * 512:(half_ + 1) * 512], start=(c == 0), stop=(c == 7))
                T["g", t] = PE.mark(ins)
                fr["x1nt"][s] = T["g", t]

            def st_c2a(t):
                s = t % 2
                ACT.need(T["g", t], fr["th3"][s])
                T["th", t] = ACT.mark(ACT.eng.activation(out=TH3[s][:], in_=psG[:], func=AF.Tanh, scale=0.5))
                fr["psG"] = T["th", t]
                DVE.need(T["th", t], T["pp", t])
                T["gp", t] = DVE.mark(DVE.eng.scalar_tensor_tensor(out=TH3[s][:], in0=TH3[s][:], scalar=1.0, in1=psPP[:],
                                                                   op0=ALU.add, op1=ALU.mult))
                fr["psPP"] = T["gp", t]
                DVE.need(T["gp", t], fr["x2"][s])
                T["x2", t] = DVE.mark(DVE.eng.scalar_tensor_tensor(out=X2[s][:], in0=TH3[s][:], scalar=0.5, in1=X1[s][:],
                                                                   op0=ALU.mult, op1=ALU.add))
                fr["th3"][s] = T["x2", t]
                fr["x1"][s] = T["x2", t]

            def st_c2b(t):
                s = t % 2
                ACT.need(T["x2", t], fr.get("junk"))
                T["sq4", t] = fr["junk"] = ACT.mark(ACT.eng.activation(out=JUNK3[:], in_=X2[s][:], func=AF.Square, accum_out=ss4[:, t:t + 1]))
                DVE.need(T["sq4", t])
                T["v4", t] = DVE.mark(DVE.eng.tensor_scalar(out=v4[:, t:t + 1], in0=ss4[:, t:t + 1], scalar1=1.0 / D, scalar2=EPS,
                                                            op0=ALU.mult, op1=ALU.add))
                POOL.need(T["v4", t])
                T["r4", t] = POOL.mark(POOL.eng.tensor_tensor(out=r4[:, t:t + 1], in0=v4[:, t:t + 1], in1=mhalf[:], op=ALU.pow))

            def st_d(t):
                s = t % 2
                DVE.need(T["r4", t], fr["yo"][s], P3["gfin"])
                T["yo", t] = DVE.mark(DVE.eng.scalar_tensor_tensor(out=YO[s][:], in0=X2[s][:], scalar=r4[:, t:t + 1], in1=GFIN[:],
                                                                   op0=ALU.mult, op1=ALU.mult))
                fr["x2"][s] = T["yo", t]
                SP.need(T["yo", t])
                tk = oslot[s].dma(SP, y_d[t * 128:(t + 1) * 128, :], YO[s][:])
                fr["yo"][s] = tk
                stores.append(tk)

            T["ldx", 0], T["ldx", 1] = P3["ldx", 0], P3["ldx", 1]
            T["ptall"] = EY["ptall"]
            fr["psG"] = EY["ptall"][-1]
            for it in range(NTILE + 4):
                if 0 <= it - 1 < NTILE:
                    st_x1n(it - 1)
                if 0 <= it - 2 < NTILE:
                    st_c2a(it - 2)
                if it < NTILE:
                    if 2 <= it + 1 < NTILE:
                        st_load(it + 1)
                    st_y(it)
                if 0 <= it - 1 < NTILE:
                    st_c1(it - 1)
                if 0 <= it - 2 < NTILE:
                    st_c2b(it - 2)
                if it < NTILE:
                    st_sq3(it)
                if 0 <= it - 3 < NTILE:
                    st_d(it - 3)
            SP.need(*stores)
            DVE.need(*stores)
            barrier()

        ost = Slot(nc, es, "out")
        tk_out = []
        if debug:
            dslot = Slot(nc, es, "dbgs")
            barrier()
            stage = Carver(40960).take([128, 4096], F32)
            tkd = None

            def dump(name, src_ap, n):
                nonlocal tkd
                DVE.need(tkd)
                t = DVE.mark(DVE.eng.tensor_copy(out=stage[:, 0:n], in_=src_ap))
                SP.need(t)
                tkd = dslot.dma(SP, dbg[name], stage[:, 0:n])

            if "qt" in dbg:
                dump("qt", QT[:, 0, :], SO)
            if "kt" in dbg:
                dump("kt", KT[:, 1, :], S)
            if "va" in dbg:
                dump("va", VA[:, 17, :, :].rearrange("p h d -> p (h d)"), 516)
            if "sga" in dbg:
                dump("sga", SGA[:, 2, :], SO)
            if "mixc" in dbg:
                dump("mixc", MIXC[:, 3, :], SO)
            if "mixc0" in dbg:
                dump("mixc0", MIXC[:, 0, :], SO)
            if "negm" in dbg:
                dump("negm", negM[:], 1)
            if "ht" in dbg:
                dump("ht", HT[:, 5, :], SO)
            for hh in range(NH):
                if "mixa%d" % hh in dbg:
                    dump("mixa%d" % hh, MIXA[:, hh, :], SO)
            SP.need(tkd)
            DVE.need(tkd)
    return nc


def _core_inputs(inputs, c):
    b, half = c // 2, c % 2
    x = np.asarray(inputs["x"][b], dtype=np.float32)
    p = np.asarray(inputs["p"][0, b], dtype=np.float32)
    cw = np.asarray(inputs["conv_w"][0], dtype=np.float32)
    if half == 1:
        x = x[::-1]
        p = p[::-1]
        cw = cw[::-1]
    p = p[:SO]
    lamv = np.concatenate([np.asarray(inputs[k][0], dtype=np.float32) for k in
                           ("lambda_q1", "lambda_k1", "lambda_q2", "lambda_k2")])
    return {
        "x": np.ascontiguousarray(x),
        "p": np.ascontiguousarray(p),
        "w_in": np.ascontiguousarray(inputs["w_in"][0], dtype=np.float32),
        "w_out": np.ascontiguousarray(inputs["w_out"][0], dtype=np.float32),
        "w_g": np.ascontiguousarray(inputs["w_ple_gate"][0], dtype=np.float32),
        "w_p": np.ascontiguousarray(inputs["w_ple_proj"][0], dtype=np.float32),
        "gmix": np.ascontiguousarray(np.asarray(inputs["mix_norm_g"][0], dtype=np.float32).reshape(8, 128).T),
        "gple": np.ascontiguousarray(np.asarray(inputs["ple_norm_g"][0], dtype=np.float32).reshape(8, 128).T),
        "gfin": np.ascontiguousarray(np.broadcast_to(np.asarray(inputs["final_norm_g"], dtype=np.float32)[None, :], (128, D))),
        "gpleb": np.ascontiguousarray(np.broadcast_to(np.asarray(inputs["ple_norm_g"][0], dtype=np.float32)[None, :], (128, D))),
        "subln": np.ascontiguousarray(np.asarray(inputs["subln_g"][0], dtype=np.float32).reshape(128, 1)),
        "convw": np.ascontiguousarray(cw.reshape(3, 4, 128).transpose(2, 1, 0).reshape(128, 12)),
        "lamv": np.ascontiguousarray(np.broadcast_to(lamv[None, :], (128, 256))),
    }


def kernel(**inputs):
    nc = build_program()
    in_maps = [_core_inputs(inputs, c) for c in range(NCORES)]
    res = run_bass_kernel_spmd(nc, in_maps, core_ids=list(range(NCORES)))
    out = np.empty((4, S, D), dtype=np.float32)
    for c in range(NCORES):
        b, half = c // 2, c % 2
        yc = res.results[c]["y"]
        if half == 0:
            out[b, :SO] = yc
        else:
            out[b, SO:] = yc[::-1]
    return out
```

```python
import contextlib
import numpy as np
import concourse.bass as bass
import concourse.mybir as mybir
from concourse.bass_utils import run_bass_kernel_spmd

F32 = mybir.dt.float32
BF16 = mybir.dt.bfloat16
AF = mybir.ActivationFunctionType
ALU = mybir.AluOpType
AX = mybir.AxisListType

NCORES = 8
S = 4096
SO = 2048
D = 1024
NH = 4
EPS = 1e-6
SLOPES = [2.0 ** (-8.0 * (h + 1) / NH) for h in range(NH)]
LAMBDA_INIT = 0.8 - 0.6 * 1.0


class Eng:
    def __init__(self, nc, es, name, eng):
        self.nc, self.name, self.eng = nc, name, eng
        self.sem = es.enter_context(nc.semaphore("sem_" + name))
        self.cnt = 0
        self.waited = {}

    def mark(self, instr):
        self.cnt += 1
        instr.then_inc(self.sem, 1)
        return (self.sem, self.cnt, self.name)

    def last(self):
        return (self.sem, self.cnt, self.name) if self.cnt else None

    def need(self, *tks):
        for tk in tks:
            if tk is None:
                continue
            if isinstance(tk, list):
                self.need(*tk)
                continue
            sem, val, name = tk
            if self.waited.get(name, 0) >= val:
                continue
            self.eng.wait_ge(sem, val)
            self.waited[name] = val


class Slot:
    def __init__(self, nc, es, name):
        self.sem = es.enter_context(nc.semaphore("dq_" + name))
        self.cnt = 0
        self.name = "dq_" + name

    def dma(self, q, out, in_, **kw):
        ins = q.eng.dma_start(out=out, in_=in_, **kw)
        self.cnt += 16
        ins.then_inc(self.sem, 16)
        return (self.sem, self.cnt, self.name)


def build_program(debug=None):
    nc = bass.Bass("TRN2", target_bir_lowering=False)

    def din(name, shape):
        return nc.dram_tensor(name, shape, F32, kind="ExternalInput").ap()

    x_d = din("x", [S, D])
    p_d = din("p", [SO, 256])
    win_d = din("w_in", [D, 4096])
    wout_d = din("w_out", [D, D])
    wg_d = din("w_g", [D, D])
    wp_d = din("w_p", [256, D])
    gmix_d = din("gmix", [128, 8])
    gple_d = din("gple", [128, 8])
    gfin_d = din("gfin", [128, D])
    gpleb_d = din("gpleb", [128, D])
    subln_d = din("subln", [128, 1])
    convw_d = din("convw", [128, 12])
    lam_d = din("lamv", [128, 256])
    y_d = nc.dram_tensor("y", [SO, D], F32, kind="ExternalOutput").ap()
    dbg = {}
    if debug:
        for name, shape in debug.items():
            dbg[name] = nc.dram_tensor("dbg_" + name, shape, F32, kind="ExternalOutput").ap()

    es = contextlib.ExitStack()
    with es:
        PE = Eng(nc, es, "pe", nc.tensor)
        ACT = Eng(nc, es, "act", nc.scalar)
        DVE = Eng(nc, es, "dve", nc.vector)
        POOL = Eng(nc, es, "pool", nc.gpsimd)
        SP = Eng(nc, es, "sp", nc.sync)
        ENGS = [PE, ACT, DVE, POOL]

        def sb(name, shape, dt):
            return es.enter_context(nc.sbuf_tensor(name, shape, dt))

        def barrier(engs=None, extra=()):
            tks = [e.last() for e in ENGS] + list(extra)
            for e in (engs or (ENGS + [SP])):
                e.need(*tks)

        QTF = sb("QT", [128, NH * SO], BF16)
        KTF = sb("KT", [128, NH * S], BF16)
        VAF = sb("VA", [128, 32 * NH * 129], BF16)
        QT = QTF[:].rearrange("p (h s) -> p h s", h=NH)
        KT = KTF[:].rearrange("p (h s) -> p h s", h=NH)
        VA = VAF[:].rearrange("p (a b c) -> p a b c", a=32, b=NH)
        SGAF = sb("SGA", [128, NH * SO], BF16)
        SGA = SGAF[:].rearrange("p (h s) -> p h s", h=NH)
        MIXC = sb("MIXC", [128, 4, SO], BF16)
        ident_f = sb("ident_f", [128, 128], F32)
        ident_b = sb("ident_b", [128, 128], BF16)
        ones_b = sb("ones_b", [128, 128], BF16)
        gmix = sb("gmix_s", [128, 8], F32)
        gple = sb("gple_s", [128, 8], F32)
        subln = sb("subln_s", [128, 1], F32)
        sublns = sb("sublns", [128, 1], F32)
        convw = sb("convw_s", [128, 12], F32)
        lamv = sb("lamv_s", [128, 256], F32)
        lamt = sb("lamt", [128, 8], F32)
        neglam = sb("neglam", [128, 1], F32)
        mhalf = sb("mhalf", [128, 1], F32)
        ss_all = sb("ss_all", [128, 48], F32)
        v_all = sb("v_all", [128, 48], F32)
        rstd_all = sb("rstd_all", [128, 48], F32)
        mx_all = sb("mx_all", [128, 48], F32)
        negM = sb("negM", [128, 1], F32)
        uh = sb("uh", [128, 8], F32)
        uhalo = sb("uhalo", [128, 4], F32)

        T_unit = sb("T_unit", [128, 896], F32)
        iotaA = sb("iotaA", [128, 13], F32)
        iotaB = sb("iotaB", [128, 32], F32)
        iotaF = sb("iotaF", [128, 8], F32)
        biasA = sb("biasA", [128, NH, 13], F32)
        biasB = sb("biasB", [128, NH, 32], F32)
        fAB = sb("fAB", [128, NH, 8], F32)
        rs = [sb("rs%d" % i, [128, 8], F32) for i in range(2)]
        c2 = [sb("c2_%d" % i, [128, 4], F32) for i in range(2)]
        ssq = [sb("ssq%d" % i, [128, 4], F32) for i in range(2)]
        vsq = [sb("vsq%d" % i, [128, 4], F32) for i in range(2)]
        rsd = [sb("rsd%d" % i, [128, 4], F32) for i in range(2)]
        mq = sb("mq", [128, 2], F32)
        mk2 = sb("mk2", [128, 2], F32)
        arena_bytes = (nc.sbuf_bytes_remaining - 1024) // 64 * 64
        ARENA = sb("ARENA", [128, arena_bytes // 2], BF16)
        print("arena bytes", arena_bytes)

        class Carver:
            def __init__(self, off=0):
                self.off = off

            def take(self, shape, dt):
                esz = 2 if dt == BF16 else 4
                n = int(np.prod(shape[1:]))
                nbytes = n * esz
                self.off = (self.off + 63) // 64 * 64
                assert self.off + nbytes <= arena_bytes, (self.off, nbytes, arena_bytes)
                a = ARENA[:, self.off // 2:(self.off + nbytes) // 2]
                self.off += nbytes
                if dt != BF16:
                    a = a.bitcast(dt)
                if len(shape) == 3:
                    a = a.rearrange("p (a b) -> p a b", a=shape[1])
                elif len(shape) == 4:
                    a = a.rearrange("p (a b c) -> p a b c", a=shape[1], b=shape[2])
                return a

        cslot = Slot(nc, es, "const")
        tkc = None
        for dst, src in [(gmix, gmix_d), (gple, gple_d), (subln, subln_d), (convw, convw_d), (lamv, lam_d)]:
            tkc = cslot.dma(SP, dst[:], src)
        POOL.need(POOL.mark(POOL.eng.memset(ident_f[:], 0.0)))
        POOL.eng.affine_select(out=ident_f[:], in_=ident_f[:], pattern=[[1, 128]], base=0,
                               channel_multiplier=-1, compare_op=ALU.not_equal, fill=1.0)
        POOL.eng.memset(ones_b[:], 1.0)
        POOL.eng.memset(mhalf[:], -0.5)
        POOL.eng.memset(mx_all[:], 0.0)
        tk_pc = POOL.mark(POOL.eng.memset(VA[:, :, :, 128:129], 1.0))
        DVE.need(tk_pc)
        tk_idb = DVE.mark(DVE.eng.tensor_copy(out=ident_b[:], in_=ident_f[:]))
        lprod = sb("lprod", [128, 128], F32)
        LC = {}

        def late_consts():
            DVE.need(tkc)
            DVE.eng.tensor_tensor(out=lprod[:, 0:64], in0=lamv[:, 0:64], in1=lamv[:, 64:128], op=ALU.mult)
            t0 = DVE.mark(DVE.eng.tensor_tensor(out=lprod[:, 64:128], in0=lamv[:, 128:192], in1=lamv[:, 192:256], op=ALU.mult))
            DVE.need(t0)
            t1 = DVE.mark(DVE.eng.tensor_reduce(out=lamt[:, 0:2], in_=lprod[:].rearrange("p (a b) -> p a b", a=2),
                                                axis=AX.X, op=ALU.add))
            ACT.need(t1)
            t2 = ACT.mark(ACT.eng.activation(out=lamt[:, 2:4], in_=lamt[:, 0:2], func=AF.Exp))
            DVE.need(t2)
            t3 = DVE.mark(DVE.eng.tensor_tensor(out=lamt[:, 4:5], in0=lamt[:, 3:4], in1=lamt[:, 2:3], op=ALU.subtract))
            DVE.need(t3)
            DVE.mark(DVE.eng.tensor_scalar(out=neglam[:], in0=lamt[:, 4:5], scalar1=-LAMBDA_INIT, scalar2=None, op0=ALU.add))
            tk_const = DVE.mark(DVE.eng.tensor_scalar(out=sublns[:], in0=subln[:], scalar1=(1.0 - LAMBDA_INIT) * 0.5,
                                                      scalar2=None, op0=ALU.mult))

            POOL.eng.iota(T_unit[:], pattern=[[1, 896]], base=-384, channel_multiplier=-1, allow_small_or_imprecise_dtypes=True)
            POOL.eng.iota(iotaA[:], pattern=[[128, 13]], base=0, channel_multiplier=-1, allow_small_or_imprecise_dtypes=True)
            POOL.eng.iota(iotaB[:], pattern=[[128, 32]], base=-511, channel_multiplier=1, allow_small_or_imprecise_dtypes=True)
            POOL.eng.iota(iotaF[:, 0:4], pattern=[[128, 4]], base=0, channel_multiplier=1, allow_small_or_imprecise_dtypes=True)
            t_io = POOL.mark(POOL.eng.iota(iotaF[:, 4:8], pattern=[[-128, 4]], base=511, channel_multiplier=-1,
                                           allow_small_or_imprecise_dtypes=True))
            ACT.need(t_io)
            ACT.mark(ACT.eng.activation(out=T_unit[:], in_=T_unit[:], func=AF.Abs))
            for h in range(NH):
                ACT.mark(ACT.eng.activation(out=fAB[:, h, :], in_=iotaF[:], func=AF.Exp, scale=-SLOPES[h]))

            LC["tk_const"] = tk_const
            LC["t_io"] = t_io

        cv = Carver()
        HT = cv.take([128, 8, SO], BF16)
        WB = [cv.take([128, 8, 512], BF16) for _ in range(3)]
        SQB = [cv.take([128, 512], BF16) for _ in range(2)]
        alias0 = cv.off
        NXT = 5
        XT = [cv.take([128, D], F32) for _ in range(NXT)]
        XN = [cv.take([128, D], BF16) for _ in range(2)]
        JUNK = cv.take([128, D], BF16)
        norm_end = cv.off
        cv2 = Carver(alias0)
        UF = cv2.take([128, 2050], F32)
        HS = cv2.take([128, 512], F32)
        TH = [cv2.take([128, 512], F32) for _ in range(2)]
        CA = [cv2.take([128, 512], F32) for _ in range(2)]
        print("phase1 arena used", max(cv.off, cv2.off))
        HTH = sb("HTH", [128, 8, 2], BF16)
        halo_t = sb("halo_t", [128, 4], F32)

        pes = contextlib.ExitStack()
        with pes:
            psT = [pes.enter_context(nc.psum_tensor("psT%d" % i, [128, D], BF16)) for i in range(2)]
            NPSP = 5
            psP = [pes.enter_context(nc.psum_tensor("psP%d" % i, [128, 512], F32)) for i in range(NPSP)]
            psM = pes.enter_context(nc.psum_tensor("psM", [128, 512], F32))
            psP_free = [None] * NPSP
            psP_i = [-1]

            def psp_next():
                psP_i[0] = (psP_i[0] + 1) % NPSP
                return psP_i[0]

            xslot = [Slot(nc, es, "x%d" % i) for i in range(NXT)]
            wslot = [Slot(nc, es, "w%d" % i) for i in range(3)]
            w_free = [None, None, None]
            w_ready = [None, None, None]
            psT_free = [None, None]
            psM_free = [None]
            sqb_free = [None, None]
            mx_idx = [0]
            state = {"xt_free": [None] * NXT, "xn_free": [None, None], "nt": 0, "xt_gen": [0] * NXT,
                     "last_sq": None, "last_xn": None, "last_tr": None}
            deferred = []

            def flush_deferred():
                while deferred:
                    deferred.pop(0)()

            def load_w(buf, blocks):
                POOL.need(w_free[buf])
                tk = None
                for (c0, ncol, d0) in blocks:
                    src_ = win_d.rearrange("(c p) n -> p c n", p=128)[:, :, c0:c0 + ncol]
                    tk = wslot[buf].dma(POOL, WB[buf][:, :, d0:d0 + ncol], src_)
                w_ready[buf] = tk

            class NormPipe:
                def __init__(self, rows, tok0_of, ht_free_of):
                    self.rows, self.tok0_of, self.ht_free_of, self.n = rows, tok0_of, ht_free_of, len(rows)
                    self.base = state["nt"]
                    state["nt"] += self.n
                    self.step = 0
                    self.tk = [dict() for _ in range(7)]
                    self.ev = {}

                def _emit_step(self, step):
                    tk_ld, tk_sq, tk_v, tk_r, tk_xn, tk_tr, _ = self.tk
                    rows, base = self.rows, self.base
                    i = step
                    if 0 <= i < self.n:
                        k = base + i
                        s = k % NXT
                        assert state["xt_gen"][s] == k // NXT, (state["xt_gen"], k)
                        SP.need(state["xt_free"][s])
                        tk_ld[i] = xslot[s].dma(SP, XT[s][:], x_d[rows[i] * 128:(rows[i] + 1) * 128, :])
                    i = step - 1
                    if 0 <= i < self.n:
                        k = base + i
                        s = k % NXT
                        ACT.need(tk_ld[i], state["last_sq"])
                        tk_sq[i] = ACT.mark(ACT.eng.activation(out=JUNK[:], in_=XT[s][:], func=AF.Square,
                                                               accum_out=ss_all[:, k:k + 1]))
                        state["last_sq"] = tk_sq[i]
                        DVE.need(tk_sq[i])
                        tk_v[i] = DVE.mark(DVE.eng.tensor_scalar(out=v_all[:, k:k + 1], in0=ss_all[:, k:k + 1],
                                                                 scalar1=1.0 / D, scalar2=EPS, op0=ALU.mult, op1=ALU.add))
                        POOL.need(tk_v[i])
                        tk_r[i] = POOL.mark(POOL.eng.tensor_tensor(out=rstd_all[:, k:k + 1], in0=v_all[:, k:k + 1],
                                                                   in1=mhalf[:], op=ALU.pow))
                    i = step - 2
                    if 0 <= i < self.n:
                        k = base + i
                        s = k % NXT
                        s2 = k % 2
                        DVE.need(tk_r[i], tk_ld[i], state["xn_free"][s2])
                        tk_xn[i] = DVE.mark(DVE.eng.tensor_scalar(out=XN[s2][:], in0=XT[s][:], scalar1=rstd_all[:, k:k + 1],
                                                                  scalar2=None, op0=ALU.mult))
                        state["last_xn"] = tk_xn[i]
                        state["xt_free"][s] = [tk_xn[i], tk_sq[i]]
                        state["xt_gen"][s] += 1
                    i = step - 3
                    if 0 <= i < self.n:
                        k = base + i
                        s2 = k % 2
                        PE.need(tk_xn[i], psT_free[s2], tk_idb)
                        ins = None
                        for c in range(8):
                            ins = PE.eng.transpose(out=psT[s2][:, c * 128:(c + 1) * 128], in_=XN[s2][:, c * 128:(c + 1) * 128],
                                                   identity=ident_b[:])
                        tk_tr[i] = PE.mark(ins)
                        state["last_tr"] = tk_tr[i]
                        state["xn_free"][s2] = tk_tr[i]
                    i = step - 4
                    if 0 <= i < self.n:
                        k = base + i
                        s2 = k % 2
                        t0_ = self.tok0_of(i)
                        DVE.need(tk_tr[i], self.ht_free_of(i), tkc)
                        self.ev[i] = DVE.mark(DVE.eng.tensor_tensor(
                            out=HT[:, :, t0_:t0_ + 128], in0=psT[s2][:].rearrange("p (c t) -> p c t", c=8),
                            in1=gmix[:].unsqueeze(2).to_broadcast([128, 8, 128]), op=ALU.mult))
                        psT_free[s2] = self.ev[i]

                def advance(self, upto):
                    upto = min(upto, self.n - 1)
                    while self.step <= upto + 4:
                        self._emit_step(self.step)
                        self.step += 1
                    return self.ev[upto]

            def proj_fm(buf, wcol0, tokc, ready_tk):
                s = psp_next()
                PE.need(psP_free[s], w_ready[buf], ready_tk)
                ins = None
                for c in range(8):
                    ins = PE.eng.matmul(psP[s][:], lhsT=WB[buf][:, c, wcol0:wcol0 + 128],
                                        rhs=HT[:, c, tokc * 512:(tokc + 1) * 512], start=(c == 0), stop=(c == 7))
                tk = PE.mark(ins)
                flush_deferred()
                return s, tk

            def norm_bound(s, tk_mm):
                idx = mx_idx[0]
                mx_idx[0] += 1
                b = idx % 2
                ACT.need(tk_mm, sqb_free[b])
                t_sq = ACT.mark(ACT.eng.activation(out=SQB[b][:], in_=psP[s][:], func=AF.Square))

                def part_b():
                    PE.need(t_sq, psM_free[0])
                    t_m = PE.mark(PE.eng.matmul(psM[:], lhsT=ones_b[:], rhs=SQB[b][:], start=True, stop=True))
                    sqb_free[b] = t_m
                    DVE.need(t_m)
                    t_r = DVE.mark(DVE.eng.tensor_reduce(out=mx_all[:, idx:idx + 1], in_=psM[:], axis=AX.X, op=ALU.max))
                    psM_free[0] = t_r

                deferred.append(part_b)
                return t_sq

            def k_chunk(buf, h, tokc, tokbase, ready_tk):
                s, tk = proj_fm(buf, h * 128, tokc, ready_tk)
                ACT.need(tk)
                t_cp = ACT.mark(ACT.eng.activation(out=KT[:, h, tokbase + tokc * 512: tokbase + (tokc + 1) * 512], in_=psP[s][:],
                                                   func=AF.Copy))
                t_sq = norm_bound(s, tk)
                psP_free[s] = [t_cp, t_sq]

            def q_chunk(buf, h, tokc, ready_tk):
                s, tk = proj_fm(buf, h * 128, tokc, ready_tk)
                ACT.need(tk)
                t_cp = ACT.mark(ACT.eng.activation(out=QT[:, h, tokc * 512:(tokc + 1) * 512], in_=psP[s][:],
                                                   func=AF.Copy, scale=0.125))
                t_sq = norm_bound(s, tk)
                psP_free[s] = [t_cp, t_sq]

            def v_tile(buf, tl, rglob, ready_tk):
                s = psp_next()
                PE.need(psP_free[s], w_ready[buf], ready_tk)
                ins = None
                for c in range(8):
                    ins = PE.eng.matmul(psP[s][:], lhsT=HT[:, c, tl * 128:(tl + 1) * 128], rhs=WB[buf][:, c, :],
                                        start=(c == 0), stop=(c == 7))
                tk = PE.mark(ins)
                flush_deferred()
                DVE.need(tk, tk_pc)
                psP_free[s] = DVE.mark(DVE.eng.tensor_copy(out=VA[:, rglob, :, 0:128],
                                                           in_=psP[s][:].rearrange("p (h d) -> p h d", h=NH)))

            load_w(0, [(512, 512, 0)])
            load_w(1, [(1024, 512, 0)])
            ht_chunk_free = [None] * 4
            np_a = NormPipe(list(range(16, 32)), lambda i: i * 128, lambda i: None)
            np_b = NormPipe(list(range(0, 16)), lambda i: i * 128, lambda i: ht_chunk_free[i // 4])
            ev0 = np_a.advance(0)
            DVE.need(ev0)
            tk_hth = DVE.mark(DVE.eng.tensor_copy(out=HTH[:], in_=HT[:, :, 0:2]))
            LAG = 4
            for s in range(32 + LAG):
                if s < 16:
                    np_a.advance(s)
                elif s < 32:
                    np_b.advance(s - 16)
                if s == 9:
                    late_consts()
                if s == 8:
                    load_w(2, [(0, 512, 0)])
                m = s - LAG
                if 0 <= m < 16:
                    j, r = m // 4, m % 4
                    rdy = np_a.advance(4 * j + 3)
                    k_chunk(0, r, j, SO, rdy)
                    v_tile(1, 4 * j + r, 16 + 4 * j + r, rdy)
                    if r == 3:
                        ht_chunk_free[j] = PE.last()
                elif 16 <= m < 32:
                    j, r = (m - 16) // 4, (m - 16) % 4
                    rdy = np_b.advance(4 * j + 3)
                    q_chunk(2, r, j, rdy)
            flush_deferred()
            rdy_all = np_b.advance(15)
            norm_done = [state["last_sq"], state["last_xn"], state["last_tr"]]
            w_free[2] = PE.last()
            load_w(2, [(1536, 512, 0)])
            for tokc in range(4):
                for h in range(NH):
                    k_chunk(0, h, tokc, 0, rdy_all)
            flush_deferred()
            w_free[0] = PE.last()
            DVE.need(DVE.last())
            nq = 16
            nk = 32
            t_a = DVE.mark(DVE.eng.tensor_reduce(out=mq[:, 0:1], in_=mx_all[:, 16:32], axis=AX.X, op=ALU.max))
            DVE.need(t_a)
            DVE.eng.tensor_reduce(out=mk2[:, 0:1], in_=mx_all[:, 0:16], axis=AX.X, op=ALU.max)
            t_b = DVE.mark(DVE.eng.tensor_reduce(out=mk2[:, 1:2], in_=mx_all[:, 32:48], axis=AX.X, op=ALU.max))
            DVE.need(t_b)
            t_c = DVE.mark(DVE.eng.tensor_tensor(out=mq[:, 1:2], in0=mk2[:, 0:1], in1=mk2[:, 1:2], op=ALU.max))
            DVE.need(t_c)
            t_d = DVE.mark(DVE.eng.tensor_tensor(out=mk2[:, 0:1], in0=mq[:, 0:1], in1=mq[:, 1:2], op=ALU.add))
            DVE.need(t_d)
            tk_negM = DVE.mark(DVE.eng.tensor_scalar(out=negM[:], in0=mk2[:, 0:1], scalar1=-1.0 / 16.0, scalar2=None, op0=ALU.mult))

            DVE.need(tk_negM, LC["t_io"])
            for h in range(NH):
                DVE.eng.tensor_scalar(out=biasA[:, h, :], in0=iotaA[:], scalar1=-SLOPES[h], scalar2=negM[:], op0=ALU.mult, op1=ALU.add)
                DVE.mark(DVE.eng.tensor_scalar(out=biasB[:, h, :], in0=iotaB[:], scalar1=-SLOPES[h], scalar2=negM[:], op0=ALU.mult, op1=ALU.add))


            def conv_blocks(cc):
                return [(2048 + cc * 128, 128, 0), (2560 + cc * 128, 128, 128), (3072 + cc * 128, 128, 256),
                        (3584 + cc * 128, 128, 384)]

            load_w(0, conv_blocks(0))
            for tl in range(16):
                v_tile(1, tl, tl, rdy_all)
            w_free[1] = PE.last()
            load_w(1, conv_blocks(1))
            th_free = [norm_done, norm_done]
            thi = [0]
            for tokc in range(4):
                for h in range(NH):
                    s, tk = proj_fm(2, h * 128, tokc, rdy_all)
                    b = thi[0] % 2
                    thi[0] += 1
                    ACT.need(tk, th_free[b])
                    t_th = ACT.mark(ACT.eng.activation(out=TH[b][:], in_=psP[s][:], func=AF.Tanh, scale=0.5))
                    DVE.need(t_th)
                    t_sg = DVE.mark(DVE.eng.scalar_tensor_tensor(out=SGA[:, h, tokc * 512:(tokc + 1) * 512], in0=TH[b][:],
                                                                 scalar=1.0, in1=psP[s][:], op0=ALU.add, op1=ALU.mult))
                    th_free[b] = t_sg
                    psP_free[s] = t_sg
            w_free[2] = PE.last()
            load_w(2, conv_blocks(2))
            POOL.need(norm_done)
            tk_pad = POOL.mark(POOL.eng.memset(UF[:, 0:1], 0.0))
            uf_free = norm_done
            ca_free = [norm_done, norm_done]
            hs_free = norm_done
            conv_buf = [0, 1, 2, 0]
            for cc in range(4):
                buf = conv_buf[cc]
                PE.need(psM_free[0], w_ready[buf], tk_hth)
                ins = None
                for g_ in range(2):
                    for c in range(8):
                        ins = PE.eng.matmul(psM[:, g_ * 2:g_ * 2 + 2], lhsT=WB[buf][:, c, 128 + g_ * 128:256 + g_ * 128],
                                            rhs=HTH[:, c, :], start=(c == 0), stop=(c == 7))
                tk_h = PE.mark(ins)
                DVE.need(tk_h)
                t_ht = DVE.mark(DVE.eng.tensor_copy(out=halo_t[:], in_=psM[:, 0:4]))
                psM_free[0] = t_ht
                DVE.need(t_ht, uf_free)
                tk_hl = DVE.mark(DVE.eng.tensor_tensor(out=UF[:, 2049:2050], in0=halo_t[:, 0:1], in1=halo_t[:, 2:3], op=ALU.mult))
                tk_u = []
                for tokc in range(4):
                    sC, tkC = proj_fm(buf, 128, tokc, rdy_all)
                    sH, tkH = proj_fm(buf, 256, tokc, rdy_all)
                    ACT.need(tkH, hs_free)
                    t_hs = ACT.mark(ACT.eng.activation(out=HS[:], in_=psP[sH][:], func=AF.Copy))
                    psP_free[sH] = t_hs
                    DVE.need(t_hs, tkC, uf_free)
                    t_u = DVE.mark(DVE.eng.tensor_tensor(out=UF[:, 1 + tokc * 512: 1 + (tokc + 1) * 512], in0=psP[sC][:],
                                                         in1=HS[:], op=ALU.mult))
                    hs_free = t_u
                    psP_free[sC] = t_u
                    tk_u.append(t_u)
                last_readers = []
                for tokc in range(4):
                    sB, tkB = proj_fm(buf, 0, tokc, rdy_all)
                    sG, tkG = proj_fm(buf, 384, tokc, rdy_all)
                    b = thi[0] % 2
                    thi[0] += 1
                    ACT.need(tkG, th_free[b])
                    t_th = ACT.mark(ACT.eng.activation(out=TH[b][:], in_=psP[sG][:], func=AF.Tanh, scale=0.5))
                    DVE.need(t_th)
                    t_sg = DVE.mark(DVE.eng.scalar_tensor_tensor(out=TH[b][:], in0=TH[b][:], scalar=1.0, in1=psP[sG][:],
                                                                 op0=ALU.add, op1=ALU.mult))
                    psP_free[sG] = t_sg
                    DVE.need(t_sg, tkB)
                    t_sgb = DVE.mark(DVE.eng.scalar_tensor_tensor(out=TH[b][:], in0=TH[b][:], scalar=0.5, in1=psP[sB][:],
                                                                  op0=ALU.mult, op1=ALU.mult))
                    psP_free[sB] = t_sgb
                    a = thi[0] % 2
                    c0 = 1 + tokc * 512
                    DVE.need(tk_u[min(tokc + 1, 3)], tk_pad, tk_hl, ca_free[a], tkc)
                    t_a = DVE.mark(DVE.eng.tensor_scalar(out=CA[a][:], in0=UF[:, c0 - 1:c0 + 511],
                                                         scalar1=convw[:, cc * 3:cc * 3 + 1], scalar2=None, op0=ALU.mult))
                    DVE.need(t_a)
                    t_a = DVE.mark(DVE.eng.scalar_tensor_tensor(out=CA[a][:], in0=UF[:, c0:c0 + 512],
                                                                scalar=convw[:, cc * 3 + 1:cc * 3 + 2], in1=CA[a][:],
                                                                op0=ALU.mult, op1=ALU.add))
                    DVE.need(t_a)
                    t_a = DVE.mark(DVE.eng.scalar_tensor_tensor(out=CA[a][:], in0=UF[:, c0 + 1:c0 + 513],
                                                                scalar=convw[:, cc * 3 + 2:cc * 3 + 3], in1=CA[a][:],
                                                                op0=ALU.mult, op1=ALU.add))
                    DVE.need(t_a, t_sgb)
                    t_o = DVE.mark(DVE.eng.tensor_tensor(out=MIXC[:, cc, tokc * 512:(tokc + 1) * 512], in0=CA[a][:],
                                                         in1=TH[b][:], op=ALU.mult))
                    ca_free[a] = t_o
                    th_free[b] = t_o
                    last_readers = [t_o]
                uf_free = last_readers
                w_free[buf] = PE.last()
                if cc == 0:
                    load_w(0, conv_blocks(3))

            P1END = {"psP": list(psP_free), "psM": psM_free[0], "pe": PE.last(), "act": ACT.last(), "dve": DVE.last(),
                     "pool": POOL.last()}

        cv = Carver()
        MIXA = cv.take([128, NH, SO], BF16)
        ET = [cv.take([128, 1024], BF16) for _ in range(3)]
        DG = [cv.take([128, 1024], F32) for _ in range(2)]
        OACC = [cv.take([128, 8, 129], F32) for _ in range(2)]
        OT = [cv.take([128, 4, 128], F32) for _ in range(2)]
        ON = [cv.take([128, 4, 128], F32) for _ in range(2)]
        STG = [cv.take([128, 8, 129], F32) for _ in range(1)]
        WOUT = cv.take([128, 8, D], BF16)
        WG = cv.take([128, 8, D], BF16)
        WP = cv.take([128, 2, D], BF16)
        print("phase2 arena used", cv.off)

        class RawCarver:
            def __init__(self, flat, nbytes):
                self.flat, self.nbytes, self.off = flat, nbytes, 0

            def take(self, shape, dt):
                esz = 2 if dt == BF16 else 4
                n = int(np.prod(shape[1:])) * esz
                self.off = (self.off + 63) // 64 * 64
                assert self.off + n <= self.nbytes, (self.off, n, self.nbytes)
                a = self.flat[:, self.off // 2:(self.off + n) // 2]
                self.off += n
                if dt != BF16:
                    a = a.bitcast(dt)
                if len(shape) == 3:
                    a = a.rearrange("p (a b) -> p a b", a=shape[1])
                return a

        rc1 = RawCarver(KTF[:], NH * S * 2)
        rc2 = RawCarver(VAF[:], 32 * NH * 129 * 2)
        rc3 = RawCarver(QTF[:], NH * SO * 2)
        rc4 = RawCarver(SGAF[:], NH * SO * 2)
        PF32 = rc1.take([128, 16, 256], F32)
        XT3 = [rc1.take([128, D], F32) for _ in range(2)]
        X1 = [rc1.take([128, D], F32) for _ in range(2)]
        TH3 = [rc2.take([128, D], F32) for _ in range(2)]
        X2 = [rc2.take([128, D], F32) for _ in range(2)]
        YO = [rc2.take([128, D], F32) for _ in range(2)]
        PTALL = rc3.take([128, 2, SO], BF16)
        X1N = [rc3.take([128, D], BF16) for _ in range(2)]
        X1NT = [rc3.take([128, 8, 128], BF16) for _ in range(2)]
        GFIN = rc4.take([128, D], F32)
        GPLEB = rc4.take([128, D], F32)
        JUNK3 = rc4.take([128, D], BF16)
        ss3 = sb("ss3", [128, 16], F32)
        v3 = sb("v3", [128, 16], F32)
        r3 = sb("r3", [128, 16], F32)
        ss4 = sb("ss4", [128, 16], F32)
        v4 = sb("v4", [128, 16], F32)
        r4 = sb("r4", [128, 16], F32)

        NTILE = SO // 128
        x3slot = [Slot(nc, es, "x3_%d" % i) for i in range(2)]
        oslot = [Slot(nc, es, "o3_%d" % i) for i in range(2)]
        gslot = Slot(nc, es, "gfin")
        ppslot = [Slot(nc, es, "pld%d" % g) for g in range(4)]
        tk_pld = []
        P3 = {}

        def prefetch_phase3():
            SP.need(PE.last(), DVE.last())
            gslot.dma(SP, GFIN[:], gfin_d)
            P3["gfin"] = gslot.dma(SP, GPLEB[:], gpleb_d)
            for t_ in range(2):
                P3["ldx", t_] = x3slot[t_].dma(SP, XT3[t_][:], x_d[t_ * 128:(t_ + 1) * 128, :])
            for g in range(4):
                tk_pld.append(ppslot[g].dma(SP, PF32[:, 4 * g:4 * g + 4, :],
                                         p_d.rearrange("(t p) c -> p t c", p=128)[:, 4 * g:4 * g + 4, :]))


        wt_slot = Slot(nc, es, "wtail")
        POOL.need(P1END["pe"], P1END["act"], P1END["dve"])
        for half_ in range(2):
            wt_slot.dma(POOL, WOUT[:, :, half_ * 512:(half_ + 1) * 512],
                        wout_d.rearrange("(c p) n -> p c n", p=128)[:, :, half_ * 512:(half_ + 1) * 512])
        for half_ in range(2):
            wt_slot.dma(POOL, WG[:, :, half_ * 512:(half_ + 1) * 512],
                        wg_d.rearrange("(c p) n -> p c n", p=128)[:, :, half_ * 512:(half_ + 1) * 512])
        for half_ in range(2):
            tk_wtail = wt_slot.dma(POOL, WP[:, :, half_ * 512:(half_ + 1) * 512],
                                   wp_d.rearrange("(c p) n -> p c n", p=128)[:, :, half_ * 512:(half_ + 1) * 512])

        pes = contextlib.ExitStack()
        with pes:
            psS = [es.enter_context(nc.psum_tensor("psS%d" % i, [128, 1024], F32)) for i in range(2)]
            psBig = es.enter_context(nc.psum_tensor("psBig", [128, 2048], F32))
            psA = [psBig[:, b_ * 512:(b_ + 1) * 512] for b_ in range(3)]
            psO = psBig[:, 1536:2048]

            def acc_region(r):
                return psA[r // 3][:, (r % 3) * 129:(r % 3) * 129 + 129]

            tiles = []
            for h in range(NH):
                for qc in range(4):
                    unit = h * 4 + qc
                    groups = []
                    groups.append(("B", list(range(4 * qc + 4, 32))))
                    if qc > 0:
                        groups.append(("A", list(range(0, 4 * qc))))
                    groups.append(("C", list(range(4 * qc, 4 * qc + 4))))
                    def dead(g, kt, h=h, qc=qc):
                        dmin = 128 * (4 * qc - kt) - 127 if g == "A" else 128 * (kt - 4 * qc) - 511
                        return g != "C" and SLOPES[h] * dmin >= 110.0
                    groups = [(g, [kt for kt in kts if not dead(g, kt)]) for g, kts in groups]
                    groups = [(g, kts) for g, kts in groups if kts]
                    for gi, (g, kts) in enumerate(groups):
                        for ki, kt in enumerate(kts):
                            tiles.append(dict(h=h, qc=qc, unit=unit, g=g, kt=kt, first=(ki == 0), last=(ki == len(kts) - 1),
                                              gfirst=(gi == 0), unit_last=(gi == len(groups) - 1 and ki == len(kts) - 1)))
            NT = len(tiles)
            psS_free = [None, [P1END["psP"][0], P1END["psP"][1]]]
            et_free = [None, None, None]
            dg_free = [None, None]
            dgi = [0]
            tk_E = {}
            acc_free = [None] * 8
            for r_ in range(8):
                acc_free[r_] = [P1END["psP"][2], P1END["psP"][3], P1END["psP"][4]][r_ // 3]
            oacc_free = [None, None]
            ot_free = [None, None]
            on_free = [None, None]
            pso_free = [P1END["psM"]]
            pending = []
            stg_free = [None, None]
            stg_i = [0]
            unit_evac = {}

            def emit_S(i):
                t = tiles[i]
                h, qc, kt = t["h"], t["qc"], t["kt"]
                b = i % 2
                PE.need(psS_free[b])
                PE.eng.matmul(psS[b][:, 0:512], lhsT=KT[0:64, h, kt * 128:(kt + 1) * 128], rhs=QT[0:64, h, qc * 512:(qc + 1) * 512],
                              start=True, stop=True)
                tk_s = PE.mark(PE.eng.matmul(psS[b][:, 512:1024], lhsT=KT[64:128, h, kt * 128:(kt + 1) * 128],
                                             rhs=QT[64:128, h, qc * 512:(qc + 1) * 512], start=True, stop=True))
                e = i % 3
                if t["g"] == "C":
                    g_ = dgi[0] % 2
                    dgi[0] += 1
                    off = 384 - 128 * (kt - 4 * qc)
                    DVE.need(tk_s, dg_free[g_])
                    tk_d = DVE.mark(DVE.eng.scalar_tensor_tensor(
                        out=DG[g_][:].rearrange("p (j q) -> p j q", j=2),
                        in0=T_unit[:, off:off + 512].unsqueeze(1).to_broadcast([128, 2, 512]), scalar=-SLOPES[h],
                        in1=psS[b][:].rearrange("p (j q) -> p j q", j=2), op0=ALU.mult, op1=ALU.add))
                    psS_free[b] = tk_d
                    ACT.need(tk_d, et_free[e])
                    tk_E[i] = ACT.mark(ACT.eng.activation(out=ET[e][:], in_=DG[g_][:], func=AF.Exp, bias=negM[:], scale=1.0))
                    dg_free[g_] = tk_E[i]
                else:
                    if t["g"] == "A":
                        bias = biasA[:, h, 4 * qc - kt:4 * qc - kt + 1]
                    else:
                        bias = biasB[:, h, kt - 4 * qc:kt - 4 * qc + 1]
                    ACT.need(tk_s, et_free[e])
                    tk_E[i] = ACT.mark(ACT.eng.activation(out=ET[e][:], in_=psS[b][:], func=AF.Exp, bias=bias, scale=1.0))
                    psS_free[b] = tk_E[i]

            def finalize_part1(unit, h, qc, tks):
                u = unit % 2
                DVE.need(tks, ot_free[u])
                t_ = DVE.mark(DVE.eng.reciprocal(out=rs[u][:], in_=OACC[u][:, :, 128]))
                DVE.need(t_, LC["tk_const"])
                t_c2 = DVE.mark(DVE.eng.tensor_scalar(out=c2[u][:], in0=rs[u][:, 4:8], scalar1=neglam[:], scalar2=None, op0=ALU.mult))
                DVE.need(t_c2)
                for sb_ in range(4):
                    t_ = DVE.mark(DVE.eng.tensor_scalar(out=OT[u][:, sb_, :], in0=OACC[u][:, 4 + sb_, 0:128],
                                                        scalar1=c2[u][:, sb_:sb_ + 1], scalar2=None, op0=ALU.mult))
                DVE.need(t_)
                for sb_ in range(4):
                    t_ = DVE.mark(DVE.eng.scalar_tensor_tensor(out=OT[u][:, sb_, :], in0=OACC[u][:, sb_, 0:128],
                                                               scalar=rs[u][:, sb_:sb_ + 1], in1=OT[u][:, sb_, :],
                                                               op0=ALU.mult, op1=ALU.add))
                oacc_free[u] = t_
                DVE.need(t_)
                DVE.need(on_free[u])
                t_ = DVE.mark(DVE.eng.tensor_tensor(out=ON[u][:], in0=OT[u][:], in1=OT[u][:], op=ALU.mult))
                DVE.need(t_)
                t_ = DVE.mark(DVE.eng.tensor_reduce(out=ssq[u][:], in_=ON[u][:], axis=AX.X, op=ALU.add))
                DVE.need(t_)
                POOL.need(t_)
                t_v = POOL.mark(POOL.eng.tensor_scalar(out=vsq[u][:], in0=ssq[u][:], scalar1=1.0 / 128.0, scalar2=EPS,
                                                       op0=ALU.mult, op1=ALU.add))
                POOL.need(t_v)
                t_p = POOL.mark(POOL.eng.tensor_tensor(out=rsd[u][:], in0=vsq[u][:], in1=mhalf[:].to_broadcast([128, 4]), op=ALU.pow))
                POOL.need(t_p, on_free[u])
                t_on = POOL.mark(POOL.eng.tensor_tensor(out=ON[u][:], in0=OT[u][:],
                                                        in1=rsd[u][:].unsqueeze(2).to_broadcast([128, 4, 128]), op=ALU.mult))
                ot_free[u] = t_on

                def part2():
                    PE.need(t_on, pso_free[0])
                    ins = None
                    for sb_ in range(4):
                        ins = PE.eng.transpose(out=psO[:, sb_ * 128:(sb_ + 1) * 128], in_=ON[u][:, sb_, :], identity=ident_f[:])
                    t_tr = PE.mark(ins)
                    on_free[u] = t_tr
                    DVE.need(t_tr)
                    t_m = DVE.mark(DVE.eng.scalar_tensor_tensor(out=MIXA[:, h, qc * 512:(qc + 1) * 512], in0=psO[:], scalar=sublns[:],
                                                                in1=SGA[:, h, qc * 512:(qc + 1) * 512], op0=ALU.mult, op1=ALU.mult))
                    pso_free[0] = t_m

                pending.append([12, part2])

            def emit_PV(i):
                t = tiles[i]
                h, qc, kt, unit = t["h"], t["qc"], t["kt"], t["unit"]
                e = i % 3
                u = unit % 2
                PE.need(tk_E[i])
                ins = None
                tk_bank = []
                for r in range(8):
                    j, sb_ = r // 4, r % 4
                    if t["first"] and r % 3 == 0:
                        PE.need([acc_free[rr] for rr in range(r, min(r + 3, 8))])
                    ins = PE.eng.matmul(acc_region(r), lhsT=ET[e][:, j * 512 + sb_ * 128: j * 512 + (sb_ + 1) * 128],
                                        rhs=VA[:, kt, h, :], start=(t["first"] and r % 3 == 0), stop=t["last"],
                                        skip_group_check=True)
                    if t["last"] and r in (2, 5):
                        tk_bank.append(PE.mark(ins))
                tk_pv = PE.mark(ins)
                tk_bank.append(tk_pv)
                et_free[e] = tk_pv
                if t["last"]:
                    tks = [None] * 8
                    if t["g"] == "C":
                        DVE.need(unit_evac[unit])
                        for bk in range(3):
                            nreg = 3 if bk < 2 else 2
                            DVE.need(tk_bank[bk])
                            tk_ = DVE.mark(DVE.eng.tensor_tensor(
                                out=OACC[u][:, 3 * bk:3 * bk + nreg, :],
                                in0=psA[bk][:, 0:nreg * 129].rearrange("p (r d) -> p r d", r=nreg),
                                in1=OACC[u][:, 3 * bk:3 * bk + nreg, :], op=ALU.add))
                            for r in range(3 * bk, 3 * bk + nreg):
                                tks[r] = tk_
                    else:
                        fo = 0 if t["g"] == "A" else 4
                        sg_ = 0
                        stg_i[0] += 1
                        DVE.need(stg_free[sg_])
                        cps = []
                        for bk in range(3):
                            nreg = 3 if bk < 2 else 2
                            DVE.need(tk_bank[bk])
                            tk_ = DVE.mark(DVE.eng.tensor_copy(
                                out=STG[sg_][:, 3 * bk:3 * bk + nreg, :],
                                in_=psA[bk][:, 0:nreg * 129].rearrange("p (r d) -> p r d", r=nreg)))
                            cps.append(tk_)
                            for r in range(3 * bk, 3 * bk + nreg):
                                tks[r] = tk_
                        gfirst_ = t["gfirst"]
                        prev_ev = None if gfirst_ else unit_evac[unit]
                        ev_ = []
                        unit_evac[unit] = ev_
                        stg_free[sg_] = ev_

                        def scaled_acc(gfirst_=gfirst_, prev_ev=prev_ev, ev_=ev_, cps=cps, u=u, h=h, fo=fo, sg_=sg_):
                            if gfirst_:
                                DVE.need(cps, oacc_free[u])
                            else:
                                DVE.need(cps, prev_ev)
                            for r in range(8):
                                j, sb_ = r // 4, r % 4
                                if gfirst_:
                                    ev_.append(DVE.mark(DVE.eng.tensor_scalar(
                                        out=OACC[u][:, r, :], in0=STG[sg_][:, r, :], scalar1=fAB[:, h, fo + sb_:fo + sb_ + 1],
                                        scalar2=None, op0=ALU.mult)))
                                else:
                                    ev_.append(DVE.mark(DVE.eng.scalar_tensor_tensor(
                                        out=OACC[u][:, r, :], in0=STG[sg_][:, r, :], scalar=fAB[:, h, fo + sb_:fo + sb_ + 1],
                                        in1=OACC[u][:, r, :], op0=ALU.mult, op1=ALU.add)))

                        pending.append([3, scaled_acc])
                    for r in range(8):
                        acc_free[r] = tks[r]
                    if t["g"] == "C":
                        unit_evac[unit] = tks
                    if t["unit_last"]:
                        finalize_part1(unit, h, qc, unit_evac[unit])

            first_h3 = min(k for k, t_ in enumerate(tiles) if t_["h"] == NH - 1)
            for i in range(NT + 2):
                if i < NT:
                    emit_S(i)
                if i == first_h3 + 16:
                    prefetch_phase3()
                if i >= 2:
                    emit_PV(i - 2)
                for pnd in list(pending):
                    pnd[0] -= 1
                    if pnd[0] <= 0:
                        pending.remove(pnd)
                        pnd[1]()
            EY = {}
            tk_lastS = PE.last()
            PE.need(psS_free[0], tk_wtail)
            ins = None
            for half_ in range(2):
                for c in range(8):
                    lhs = MIXA[:, c, 0:128] if c < 4 else MIXC[:, c - 4, 0:128]
                    ins = PE.eng.matmul(psS[0][:, half_ * 512:(half_ + 1) * 512], lhsT=lhs, rhs=WOUT[:, c, half_ * 512:(half_ + 1) * 512],
                                        start=(c == 0), stop=(c == 7))
            EY["y0"] = PE.mark(ins)
            EY["ptall"] = []
            tk_prev = psS_free[1]
            ACT.need(tk_lastS)
            for g in range(4):
                PE.need(tk_pld[g], tk_prev)
                for c2_ in range(2):
                    for tt in range(4):
                        ins = PE.eng.transpose(out=psS[1][:, (c2_ * 4 + tt) * 128:(c2_ * 4 + tt + 1) * 128],
                                               in_=PF32[:, 4 * g + tt, c2_ * 128:(c2_ + 1) * 128], identity=ident_f[:])
                t_tr = PE.mark(ins)
                ACT.need(t_tr)
                tk_prev = ACT.mark(ACT.eng.activation(out=PTALL[:, :, 4 * g * 128:(4 * g + 4) * 128],
                                                      in_=psS[1][:].rearrange("p (c t) -> p c t", c=2), func=AF.Copy))
                EY["ptall"].append(tk_prev)
            for pnd in list(pending):
                pnd[1]()
            pending.clear()
            barrier()

        pes = contextlib.ExitStack()
        with pes:
            psY, psG = psS[0], psS[1]
            psPP = psBig[:, 0:1024]
            psT3 = psBig[:, 1024:1536].bitcast(BF16)
            T = {}
            fr = {"xt3": [None, None], "x1": [None, None], "x1n": [None, None], "x1nt": [None, None],
                  "th3": [None, None], "x2": [None, None], "yo": [None, None],
                  "psY": None, "psG": None, "psPP": None, "psT3": None}
            stores = []

            def st_load(t):
                s = t % 2
                SP.need(fr["xt3"][s])
                T["ldx", t] = x3slot[s].dma(SP, XT3[s][:], x_d[t * 128:(t + 1) * 128, :])

            def st_y(t):
                s = t % 2
                if t == 0:
                    T["y", t] = EY["y0"]
                else:
                    PE.need(fr["psY"], tk_wtail)
                    ins = None
                    for half_ in range(2):
                        for c in range(8):
                            lhs = MIXA[:, c, t * 128:(t + 1) * 128] if c < 4 else MIXC[:, c - 4, t * 128:(t + 1) * 128]
                            ins = PE.eng.matmul(psY[:, half_ * 512:(half_ + 1) * 512], lhsT=lhs, rhs=WOUT[:, c, half_ * 512:(half_ + 1) * 512],
                                                start=(c == 0), stop=(c == 7))
                    T["y", t] = PE.mark(ins)
                DVE.need(T["y", t], T["ldx", t], fr["x1"][s])
                T["x1", t] = DVE.mark(DVE.eng.tensor_tensor(out=X1[s][:], in0=psY[:], in1=XT3[s][:], op=ALU.add))
                fr["psY"] = T["x1", t]
                fr["xt3"][s] = T["x1", t]

            def st_sq3(t):
                s = t % 2
                ACT.need(T["x1", t], fr.get("junk"))
                T["sq3", t] = fr["junk"] = ACT.mark(ACT.eng.activation(out=JUNK3[:], in_=X1[s][:], func=AF.Square, accum_out=ss3[:, t:t + 1]))
                DVE.need(T["sq3", t])
                T["v3", t] = DVE.mark(DVE.eng.tensor_scalar(out=v3[:, t:t + 1], in0=ss3[:, t:t + 1], scalar1=1.0 / D, scalar2=EPS,
                                                            op0=ALU.mult, op1=ALU.add))
                POOL.need(T["v3", t])
                T["r3", t] = POOL.mark(POOL.eng.tensor_tensor(out=r3[:, t:t + 1], in0=v3[:, t:t + 1], in1=mhalf[:], op=ALU.pow))

            def st_x1n(t):
                s = t % 2
                DVE.need(T["r3", t], fr["x1n"][s], P3["gfin"])
                T["x1n", t] = DVE.mark(DVE.eng.scalar_tensor_tensor(out=X1N[s][:], in0=X1[s][:], scalar=r3[:, t:t + 1], in1=GPLEB[:],
                                                                    op0=ALU.mult, op1=ALU.mult))

            def st_c1(t):
                s = t % 2
                PE.need(T["x1n", t], fr["psT3"])
                ins = None
                for c in range(8):
                    ins = PE.eng.transpose(out=psT3[:, c * 128:(c + 1) * 128], in_=X1N[s][:, c * 128:(c + 1) * 128], identity=ident_b[:])
                T["tr3", t] = PE.mark(ins)
                fr["x1n"][s] = T["tr3", t]
                ACT.need(T["tr3", t], fr["x1nt"][s])
                T["x1nt_a", t] = ACT.mark(ACT.eng.activation(out=X1NT[s][:, 0:4, :].rearrange("p c t -> p (c t)"),
                                                             in_=psT3[:, 0:512], func=AF.Copy))
                T["x1nt", t] = ACT.mark(ACT.eng.activation(out=X1NT[s][:, 4:8, :].rearrange("p c t -> p (c t)"),
                                                           in_=psT3[:, 512:1024], func=AF.Copy))
                fr["psT3"] = T["x1nt", t]
                PE.need(T["ptall"], fr["psPP"])
                for half_ in range(2):
                    for c2_ in range(2):
                        ins = PE.eng.matmul(psPP[:, half_ * 512:(half_ + 1) * 512], lhsT=PTALL[:, c2_, t * 128:(t + 1) * 128],
                                            rhs=WP[:, c2_, half_ * 512:(half_ + 1) * 512], start=(c2_ == 0), stop=(c2_ == 1))
                T["pp", t] = PE.mark(ins)
                PE.need(T["x1nt_a", t], fr["psG"])
                for cg in range(2):
                    if cg == 1:
                        PE.need(T["x1nt", t])
                    for half_ in range(2):
                        for c in range(4 * cg, 4 * cg + 4):
                            ins = PE.eng.matmul(psG[:, half_ * 512:(half_ + 1) * 512], lhsT=X1NT[s][:, c, :],
                                                rhs=WG[:, c, half_ * 512:(half_ + 1) * 512], start=(c == 0), stop=(c == 7))
                T["g", t] = PE.mark(ins)
                fr["x1nt"][s] = T["g", t]

            def st_c2a(t):
                s = t % 2
                ACT.need(T["g", t], fr["th3"][s])
                T["th", t] = ACT.mark(ACT.eng.activation(out=TH3[s][:], in_=psG[:], func=AF.Tanh, scale=0.5))
                fr["psG"] = T["th", t]
                DVE.need(T["th", t], T["pp", t])
                T["gp", t] = DVE.mark(DVE.eng.scalar_tensor_tensor(out=TH3[s][:], in0=TH3[s][:], scalar=1.0, in1=psPP[:],
                                                                   op0=ALU.add, op1=ALU.mult))
                fr["psPP"] = T["gp", t]
                DVE.need(T["gp", t], fr["x2"][s])
                T["x2", t] = DVE.mark(DVE.eng.scalar_tensor_tensor(out=X2[s][:], in0=TH3[s][:], scalar=0.5, in1=X1[s][:],
                                                                   op0=ALU.mult, op1=ALU.add))
                fr["th3"][s] = T["x2", t]
                fr["x1"][s] = T["x2", t]

            def st_c2b(t):
                s = t % 2
                ACT.need(T["x2", t], fr.get("junk"))
                T["sq4", t] = fr["junk"] = ACT.mark(ACT.eng.activation(out=JUNK3[:], in_=X2[s][:], func=AF.Square, accum_out=ss4[:, t:t + 1]))
                DVE.need(T["sq4", t])
                T["v4", t] = DVE.mark(DVE.eng.tensor_scalar(out=v4[:, t:t + 1], in0=ss4[:, t:t + 1], scalar1=1.0 / D, scalar2=EPS,
                                                            op0=ALU.mult, op1=ALU.add))
                POOL.need(T["v4", t])
                T["r4", t] = POOL.mark(POOL.eng.tensor_tensor(out=r4[:, t:t + 1], in0=v4[:, t:t + 1], in1=mhalf[:], op=ALU.pow))

            def st_d(t):
                s = t % 2
                DVE.need(T["r4", t], fr["yo"][s], P3["gfin"])
                T["yo", t] = DVE.mark(DVE.eng.scalar_tensor_tensor(out=YO[s][:], in0=X2[s][:], scalar=r4[:, t:t + 1], in1=GFIN[:],
                                                                   op0=ALU.mult, op1=ALU.mult))
                fr["x2"][s] = T["yo", t]
                SP.need(T["yo", t])
                tk = oslot[s].dma(SP, y_d[t * 128:(t + 1) * 128, :], YO[s][:])
                fr["yo"][s] = tk
                stores.append(tk)

            T["ldx", 0], T["ldx", 1] = P3["ldx", 0], P3["ldx", 1]
            T["ptall"] = EY["ptall"]
            fr["psG"] = EY["ptall"][-1]
            for it in range(NTILE + 4):
                if 0 <= it - 1 < NTILE:
                    st_x1n(it - 1)
                if 0 <= it - 2 < NTILE:
                    st_c2a(it - 2)
                if it < NTILE:
                    if 2 <= it + 1 < NTILE:
                        st_load(it + 1)
                    st_y(it)
                if 0 <= it - 1 < NTILE:
                    st_c1(it - 1)
                if 0 <= it - 2 < NTILE:
                    st_c2b(it - 2)
                if it < NTILE:
                    st_sq3(it)
                if 0 <= it - 3 < NTILE:
                    st_d(it - 3)
            SP.need(*stores)
            DVE.need(*stores)
            barrier()

        ost = Slot(nc, es, "out")
        tk_out = []
        if debug:
            dslot = Slot(nc, es, "dbgs")
            barrier()
            stage = Carver(40960).take([128, 4096], F32)
            tkd = None

            def dump(name, src_ap, n):
                nonlocal tkd
                DVE.need(tkd)
                t = DVE.mark(DVE.eng.tensor_copy(out=stage[:, 0:n], in_=src_ap))
                SP.need(t)
                tkd = dslot.dma(SP, dbg[name], stage[:, 0:n])

            if "qt" in dbg:
                dump("qt", QT[:, 0, :], SO)
            if "kt" in dbg:
                dump("kt", KT[:, 1, :], S)
            if "va" in dbg:
                dump("va", VA[:, 17, :, :].rearrange("p h d -> p (h d)"), 516)
            if "sga" in dbg:
                dump("sga", SGA[:, 2, :], SO)
            if "mixc" in dbg:
                dump("mixc", MIXC[:, 3, :], SO)
            if "mixc0" in dbg:
                dump("mixc0", MIXC[:, 0, :], SO)
            if "negm" in dbg:
                dump("negm", negM[:], 1)
            if "ht" in dbg:
                dump("ht", HT[:, 5, :], SO)
            for hh in range(NH):
                if "mixa%d" % hh in dbg:
                    dump("mixa%d" % hh, MIXA[:, hh, :], SO)
            SP.need(tkd)
            DVE.need(tkd)
    return nc


def _core_inputs(inputs, c):
    b, half = c // 2, c % 2
    x = np.asarray(inputs["x"][b], dtype=np.float32)
    p = np.asarray(inputs["p"][0, b], dtype=np.float32)
    cw = np.asarray(inputs["conv_w"][0], dtype=np.float32)
    if half == 1:
        x = x[::-1]
        p = p[::-1]
        cw = cw[::-1]
    p = p[:SO]
    lamv = np.concatenate([np.asarray(inputs[k][0], dtype=np.float32) for k in
                           ("lambda_q1", "lambda_k1", "lambda_q2", "lambda_k2")])
    return {
        "x": np.ascontiguousarray(x),
        "p": np.ascontiguousarray(p),
        "w_in": np.ascontiguousarray(inputs["w_in"][0], dtype=np.float32),
        "w_out": np.ascontiguousarray(inputs["w_out"][0], dtype=np.float32),
        "w_g": np.ascontiguousarray(inputs["w_ple_gate"][0], dtype=np.float32),
        "w_p": np.ascontiguousarray(inputs["w_ple_proj"][0], dtype=np.float32),
        "gmix": np.ascontiguousarray(np.asarray(inputs["mix_norm_g"][0], dtype=np.float32).reshape(8, 128).T),
        "gple": np.ascontiguousarray(np.asarray(inputs["ple_norm_g"][0], dtype=np.float32).reshape(8, 128).T),
        "gfin": np.ascontiguousarray(np.broadcast_to(np.asarray(inputs["final_norm_g"], dtype=np.float32)[None, :], (128, D))),
        "gpleb": np.ascontiguousarray(np.broadcast_to(np.asarray(inputs["ple_norm_g"][0], dtype=np.float32)[None, :], (128, D))),
        "subln": np.ascontiguousarray(np.asarray(inputs["subln_g"][0], dtype=np.float32).reshape(128, 1)),
        "convw": np.ascontiguousarray(cw.reshape(3, 4, 128).transpose(2, 1, 0).reshape(128, 12)),
        "lamv": np.ascontiguousarray(np.broadcast_to(lamv[None, :], (128, 256))),
    }


def kernel(**inputs):
    nc = build_program()
    in_maps = [_core_inputs(inputs, c) for c in range(NCORES)]
    res = run_bass_kernel_spmd(nc, in_maps, core_ids=list(range(NCORES)))
    out = np.empty((4, S, D), dtype=np.float32)
    for c in range(NCORES):
        b, half = c // 2, c % 2
        yc = res.results[c]["y"]
        if half == 0:
            out[b, :SO] = yc
        else:
            out[b, SO:] = yc[::-1]
    return out
```
